# Optimizing a Trainium2 kernel written in Bass

```python
import jax, jax.numpy as jnp
from jax import lax
import numpy as np

D_MODEL = 1024
BATCH = 4
SEQ = 8192
DEPTH = 4
DEC_BATCH = 16
DEC_SEQ = 64
PAST_LEN = 2048

CHUNK = 64
N_MIXERS = 2
N_POOL = (DEPTH + 1) // 2
N_HGRN = DEPTH // 2
POOL_WINDOWS = (2, 4, 8, 16)
POOL_GROUPS = len(POOL_WINDOWS)
POOL_GW = D_MODEL // POOL_GROUPS
POOL_CACHE = min(max(POOL_WINDOWS) - 1, PAST_LEN)
HG_HEAD_DIM = 128
HG_HEADS = D_MODEL // HG_HEAD_DIM
D_FF = 2816
CONV_W = 3
EPS = 1e-6

kernel_name = 'pool_hgrn2_convffn_stream_step'

F32 = jnp.float32


def rmsnorm(x, g):
    xf = x.astype(F32)
    y = xf * lax.rsqrt(jnp.mean(xf * xf, axis=-1, keepdims=True) + EPS)
    return (y * g.astype(F32)).astype(x.dtype)


def pool_mix(h, hist, pos0, w, scale):
    L = h.shape[1]
    P = hist.shape[1]
    ext = jnp.concatenate([hist.astype(h.dtype), h], axis=1)
    c = jnp.pad(jnp.cumsum(ext.astype(F32), axis=1), ((0, 0), (1, 0), (0, 0)))
    end = jnp.arange(L) + P + 1
    pos = jnp.arange(L) + pos0
    outs = []
    for g, win in enumerate(POOL_WINDOWS):
        sl = slice(g * POOL_GW, (g + 1) * POOL_GW)
        cg = c[..., sl]
        lo = jnp.maximum(end - win, 0)
        wsum = cg[:, P + 1:] - jnp.take(cg, lo, axis=1)
        div = jnp.minimum(win, pos + 1).astype(F32)
        d = wsum / div[None, :, None] - h[..., sl].astype(F32)
        outs.append(jnp.einsum('bld,de->ble', d.astype(h.dtype), w[g]))
    y = jnp.concatenate(outs, axis=-1) * scale
    return y, ext[:, -POOL_CACHE:]


def hgrn_chunk(S, inp):
    q, k, v, g = inp
    C = q.shape[2]
    b = jnp.cumsum(g, axis=2)
    causal = jnp.tril(jnp.ones((C, C), dtype=bool))
    diff = b[:, :, :, None, :] - b[:, :, None, :, :]
    decay = jnp.exp(jnp.where(causal[None, None, :, :, None], diff, -jnp.inf))
    A = jnp.einsum('bhtk,bhsk,bhtsk->bhts', q, k, decay)
    o = jnp.einsum('bhtk,bhkv->bhtv', q * jnp.exp(b), S) + jnp.einsum('bhts,bhsv->bhtv', A, v)
    bC = b[:, :, -1:, :]
    S_new = jnp.exp(bC[:, :, 0, :])[..., None] * S + jnp.einsum('bhsk,bhsv->bhkv', k * jnp.exp(bC - b), v)
    return S_new, o


def hgrn_mix(h, S0, w_in, lb, gn, w_out):
    B, L, _ = h.shape
    proj = jnp.einsum('bld,de->ble', h, w_in).astype(F32)
    zq, zf, zi, zg = jnp.split(proj, 4, axis=-1)
    lbf = lb.astype(F32)
    q = jax.nn.silu(zq)
    log_f = jnp.logaddexp(jnp.log(lbf), jnp.log1p(-lbf) + jax.nn.log_sigmoid(zf))
    k = (1.0 - lbf) * jax.nn.sigmoid(-zf)
    C = CHUNK if L % CHUNK == 0 else L
    nc = L // C

    def to_chunks(a):
        return a.reshape(B, nc, C, HG_HEADS, HG_HEAD_DIM).transpose(1, 0, 3, 2, 4)

    S_fin, o = lax.scan(hgrn_chunk, S0.astype(F32), (to_chunks(q), to_chunks(k), to_chunks(zi), to_chunks(log_f)))
    o = o.transpose(1, 0, 3, 2, 4).reshape(B, L, HG_HEADS, HG_HEAD_DIM)
    o = o * lax.rsqrt(jnp.mean(o * o, axis=-1, keepdims=True) + EPS) * gn.astype(F32).reshape(HG_HEADS, HG_HEAD_DIM)
    o = o.reshape(B, L, D_MODEL) * jax.nn.silu(zg)
    y = jnp.einsum('bld,de->ble', o.astype(h.dtype), w_out)
    return y, S_fin


def conv_ffn(h, hist, w_up, cw, cb, w_down):
    L = h.shape[1]
    up = jnp.einsum('bld,df->blf', h, w_up)
    gate_pre, val = jnp.split(up, 2, axis=-1)
    ext = jnp.concatenate([hist.astype(up.dtype), gate_pre], axis=1)
    conv = cb
    for j in range(CONV_W):
        conv = conv + cw[j] * ext[:, j:j + L]
    hid = jax.nn.silu(conv.astype(F32)) * val.astype(F32)
    y = jnp.einsum('blf,fd->bld', hid.astype(h.dtype), w_down)
    return y, ext[:, -(CONV_W - 1):]


def trunk(x, pool_hist, hgrn_S, conv_hist, pos0, lb_all, norm_mix_g, pool_w, pool_scale, hgrn_w_in,
          hgrn_norm_g, hgrn_w_out, norm_ffn_g, ffn_w_up, ffn_conv_w, ffn_conv_b, ffn_w_down, norm_out_g):
    new_pool, new_hgrn, new_conv = [], [], []
    for i in range(DEPTH):
        hn = rmsnorm(x, norm_mix_g[i])
        j = i // N_MIXERS
        if i % N_MIXERS == 0:
            y, st = pool_mix(hn, pool_hist[j], pos0, pool_w[j], pool_scale[j])
            new_pool.append(st)
        else:
            y, st = hgrn_mix(hn, hgrn_S[j], hgrn_w_in[j], lb_all[i], hgrn_norm_g[j], hgrn_w_out[j])
            new_hgrn.append(st)
        x = x + y.astype(x.dtype)
        hn = rmsnorm(x, norm_ffn_g[i])
        y, cs = conv_ffn(hn, conv_hist[i], ffn_w_up[i], ffn_conv_w[i], ffn_conv_b[i], ffn_w_down[i])
        new_conv.append(cs)
        x = x + y.astype(x.dtype)
    x = rmsnorm(x, norm_out_g)
    return x, jnp.stack(new_pool), jnp.stack(new_hgrn), jnp.stack(new_conv)


def setup_inputs(seed: int = 0) -> dict:
    key = jax.random.key(seed)
    ks = jax.random.split(key, 18)
    n = jax.random.normal
    D, F = D_MODEL, D_FF
    return {
        'x_prompt': n(ks[0], (BATCH, SEQ, D), F32),
        'x_sample': n(ks[1], (DEC_BATCH, DEC_SEQ, D), F32),
        'state_pool': n(ks[2], (N_POOL, DEC_BATCH, POOL_CACHE, D), F32),
        'state_hgrn': 0.5 * n(ks[3], (N_HGRN, DEC_BATCH, HG_HEADS, HG_HEAD_DIM, HG_HEAD_DIM), F32),
        'state_ffn_conv': n(ks[4], (DEPTH, DEC_BATCH, CONV_W - 1, F), F32),
        'norm_mix_g': 1.0 + 0.02 * n(ks[5], (DEPTH, D), F32),
        'pool_w': n(ks[6], (N_POOL, POOL_GROUPS, POOL_GW, POOL_GW), F32) * POOL_GW ** -0.5,
        'pool_scale': 1.0 + 0.1 * n(ks[7], (N_POOL, D), F32),
        'hgrn_w_in': n(ks[8], (N_HGRN, D, 4 * D), F32) * D ** -0.5,
        'hgrn_lb_logits': 0.1 * n(ks[9], (DEPTH, D), F32),
        'hgrn_norm_g': 1.0 + 0.02 * n(ks[10], (N_HGRN, D), F32),
        'hgrn_w_out': n(ks[11], (N_HGRN, D, D), F32) * D ** -0.5,
        'norm_ffn_g': 1.0 + 0.02 * n(ks[12], (DEPTH, D), F32),
        'ffn_w_up': n(ks[13], (DEPTH, D, 2 * F), F32) * D ** -0.5,
        'ffn_conv_w': n(ks[14], (DEPTH, CONV_W, F), F32) * CONV_W ** -0.5,
        'ffn_conv_b': 0.01 * n(ks[15], (DEPTH, F), F32),
        'ffn_w_down': n(ks[16], (DEPTH, F, D), F32) * F ** -0.5,
        'norm_out_g': 1.0 + 0.02 * n(ks[17], (D,), F32),
    }


def reference(x_prompt, x_sample, state_pool, state_hgrn, state_ffn_conv, norm_mix_g, pool_w, pool_scale,
              hgrn_w_in, hgrn_lb_logits, hgrn_norm_g, hgrn_w_out, norm_ffn_g, ffn_w_up, ffn_conv_w,
              ffn_conv_b, ffn_w_down, norm_out_g):
    sm = jax.nn.softmax(hgrn_lb_logits.astype(F32), axis=0)
    lb_all = jnp.cumsum(sm, axis=0) - sm[0]
    weights = (norm_mix_g, pool_w, pool_scale, hgrn_w_in, hgrn_norm_g, hgrn_w_out, norm_ffn_g,
               ffn_w_up, ffn_conv_w, ffn_conv_b, ffn_w_down, norm_out_g)

    Bp = x_prompt.shape[0]
    p_pool = [jnp.zeros((Bp, 0, D_MODEL), x_prompt.dtype) for _ in range(N_POOL)]
    p_hgrn = [jnp.zeros((Bp, HG_HEADS, HG_HEAD_DIM, HG_HEAD_DIM), F32) for _ in range(N_HGRN)]
    p_conv = [jnp.zeros((Bp, CONV_W - 1, D_FF), x_prompt.dtype) for _ in range(DEPTH)]
    y_prompt, pool_p, hgrn_p, conv_p = trunk(x_prompt, p_pool, p_hgrn, p_conv, 0, lb_all, *weights)

    s_pool = [state_pool[j] for j in range(N_POOL)]
    s_hgrn = [state_hgrn[j] for j in range(N_HGRN)]
    s_conv = [state_ffn_conv[i] for i in range(DEPTH)]
    y_sample, pool_s, hgrn_s, conv_s = trunk(x_sample, s_pool, s_hgrn, s_conv, PAST_LEN, lb_all, *weights)

    return (y_prompt, y_sample, pool_p, pool_s, hgrn_p, hgrn_s, conv_p, conv_s)
```

```python
from contextlib import ExitStack
import numpy as np
import concourse.bass as bass
import concourse.mybir as mybir
from concourse.bass_utils import run_bass_kernel_spmd

F32 = mybir.dt.float32
BF16 = mybir.dt.bfloat16
AF = mybir.ActivationFunctionType
ALU = mybir.AluOpType

D = 1024
NDC = 8
DFF = 2816
NFC = 22
NH = 8
EPS = 1e-6
TT = 512
TSLOT = 528
NTMP = 12
SAME_ENG_SYNC = True


class Op:
    __slots__ = ("eng", "fn", "deps", "dma_key", "ndma", "idx", "ms", "dma_val", "is_ms")

    def __init__(self, eng, fn, dma_key, ndma):
        self.eng = eng
        self.fn = fn
        self.dma_key = dma_key
        self.ndma = ndma
        self.deps = None
        self.ms = 0
        self.is_ms = False
        self.dma_val = 0


class Sched:
    ENGS = ("pe", "act", "dve", "pool", "sp")

    def __init__(self):
        self.streams = {e: [] for e in self.ENGS}
        self.last_w = {}
        self.readers = {}
        self.dma_cnt = {}
        self.gen = {}

    def fresh(self, kind, slot):
        g = self.gen.get((kind, slot), 0)
        old = (kind, slot, g)
        new = (kind, slot, g + 1)
        self.gen[(kind, slot)] = g + 1
        if old in self.last_w:
            self.last_w[new] = self.last_w.pop(old)
        if old in self.readers:
            self.readers[new] = self.readers.pop(old)
        return new

    def _check(self, r):
        if len(r) == 3 and r[0] in ("t", "ps"):
            assert self.gen.get((r[0], r[1]), 0) == r[2], f"stale resource {r}"

    def add(self, eng, fn, reads=(), writes=(), dma_key=None, ndma=1):
        op = Op(eng, fn, dma_key, ndma)
        deps = {}

        def dep(o):
            if o is None or o is op:
                return
            k = o.dma_key if o.dma_key is not None else o.eng
            p = deps.get(k)
            if p is None or self._later(o, p):
                deps[k] = o

        for r in reads:
            self._check(r)
            dep(self.last_w.get(r))
        for r in writes:
            self._check(r)
            dep(self.last_w.get(r))
            for o in self.readers.get(r, {}).values():
                dep(o)
        op.idx = len(self.streams[eng])
        self.streams[eng].append(op)
        if dma_key is not None:
            c = self.dma_cnt.get(dma_key, 0) + ndma
            self.dma_cnt[dma_key] = c
            op.dma_val = 16 * c
        for r in reads:
            d = self.readers.setdefault(r, {})
            k = dma_key if dma_key is not None else eng
            d[k] = op
        for r in writes:
            self.last_w[r] = op
            self.readers[r] = {}
        op.deps = list(deps.values())
        return op

    @staticmethod
    def _later(a, b):
        if a.dma_key is not None:
            return a.dma_val > b.dma_val
        return a.idx > b.idx

    def finalize(self):
        for e in self.ENGS:
            for op in self.streams[e]:
                for d in op.deps:
                    if d.dma_key is None:
                        if d.eng == op.eng and (d.eng == "pe" or not SAME_ENG_SYNC):
                            continue
                        d.is_ms = True
        for e in self.ENGS:
            c = 0
            for op in self.streams[e]:
                if op.is_ms:
                    c += 1
                    op.ms = c

    def emit(self, eng_name, eng, esems, dsems):
        waited = {}
        for op in self.streams[eng_name]:
            for d in op.deps:
                if d.dma_key is not None:
                    key, val, sem = ("d", d.dma_key), d.dma_val, dsems[d.dma_key]
                else:
                    if d.eng == op.eng and (d.eng == "pe" or not SAME_ENG_SYNC):
                        continue
                    key, val, sem = ("e", d.eng), d.ms, esems[d.eng]
                if waited.get(key, 0) >= val:
                    continue
                waited[key] = val
                eng.wait_ge(sem, val)
            if op.dma_key is not None:
                op.fn(eng, dsems[op.dma_key])
            else:
                ins = op.fn(eng)
                if op.is_ms:
                    ins.then_inc(esems[eng_name], 1)


class Cfg:
    def __init__(self, layers=(0, 1, 2, 3), npt=16, ns=2):
        self.layers = tuple(layers)
        self.npt = npt
        self.ns = ns


def build(cfg):
    nc = bass.Bass("TRN2", target_bir_lowering=False)
    LY = cfg.layers
    NL = len(LY)
    NPT = cfg.npt
    NS = cfg.ns
    SEQP = NPT * TT
    pool_layers = [i for i in LY if i % 2 == 0]
    hg_layers = [i for i in LY if i % 2 == 1]
    NPL = max(1, len(pool_layers))
    NHL = max(1, len(hg_layers))

    def dt(name, shape, kind):
        return nc.dram_tensor(name, list(shape), F32, kind=kind).ap()

    I, O = "ExternalInput", "ExternalOutput"
    d_xp = dt("xp", [max(SEQP, 1), D], I)
    d_xs = dt("xs", [max(NS, 1) * 64, D], I)
    d_spool = dt("s_pool", [2, max(NS, 1), 15, D], I)
    d_shg = dt("s_hgrn", [2, max(NS, 1), NH, 128, 128], I)
    d_sconv = dt("s_conv", [4, max(NS, 1), 2, DFF], I)
    d_gmix = dt("norm_mix_g", [4, D], I)
    d_poolw = dt("pool_w", [2, 4, 256, 256], I)
    d_pscale = dt("pool_scale", [2, D], I)
    d_win = dt("hgrn_w_in", [2, D, 4 * D], I)
    d_lbl = dt("hgrn_lb_logits", [4, D], I)
    d_gn = dt("hgrn_norm_g", [2, D], I)
    d_wout = dt("hgrn_w_out", [2, D, D], I)
    d_gffn = dt("norm_ffn_g", [4, D], I)
    d_wup = dt("ffn_w_up", [4, D, 2 * DFF], I)
    d_cw = dt("ffn_conv_w", [4, 3, DFF], I)
    d_cb = dt("ffn_conv_b", [4, DFF], I)
    d_wdn = dt("ffn_w_down", [4, DFF, D], I)
    d_gout = dt("norm_out_g", [1, D], I)
    o_yp = dt("yp", [max(SEQP, 1), D], O)
    o_ys = dt("ys", [max(NS, 1) * 64, D], O)
    o_poolp = dt("npool_p", [2, 15, D], O)
    o_pools = dt("npool_s", [2, max(NS, 1), 15, D], O)
    o_hgp = dt("nhgrn_p", [2, NH, 128, 128], O)
    o_hgs = dt("nhgrn_s", [2, max(NS, 1), NH, 128, 128], O)
    o_convp = dt("nconv_p", [4, 2, DFF], O)
    o_convs = dt("nconv_s", [4, max(NS, 1), 2, DFF], O)

    S = Sched()
    es = ExitStack()
    with es:
        def sb(name, shape, dtype=F32):
            return es.enter_context(nc.sbuf_tensor(name, list(shape), dtype))

        x = sb("x", [128, NDC, TT])
        E = sb("E", [128, NDC, 16 + TT], BF16)
        hnT = E
        rstd = sb("rstd", [128, TT])
        hid = sb("hid", [128, NFC, TT], BF16)
        sq = hid
        V = hid[:, 0:8, :].rearrange("p (a b) t -> p a (b t)", b=2)
        ofin = hid[:, 8:16, :]
        gateb = sb("gateb", [128, 2, TT], BF16)
        qtb = sb("qtb", [128, 2, TT], BF16)
        temps = sb("temps", [128, NTMP * TSLOT])
        wup = sb("wup", [128, 2, NDC, 2, 256], BF16)
        wdn = sb("wdn", [128, 2, NFC, 128], BF16)
        wqfg = sb("wqfg", [128, 2, NDC, 3, 128], BF16)
        wi = sb("wi", [128, NDC, D], BF16)
        wout = sb("wout", [128, 2, NDC, 128], BF16)
        poolw = sb("poolw", [128, 4, 2, 256], BF16)
        S32 = [sb(f"S32_{k}", [128, NH, 128]) for k in range(3)]
        Scar = [sb(f"Scar_{k}", [128, NH, 128], BF16) for k in range(3)]
        Scar2 = sb("Scar2", [128, 2, 3, 128], BF16)
        Sv = sb("Sv", [128, 2, 8, 128], BF16)
        khl = sb("khl", [128, 2, 2, 4, 128], BF16)
        ATs = sb("ATs", [128, 2, 4, 128], BF16)
        xin = sb("xin", [128, 4, D])
        yout = xin
        identb = sb("identb", [128, 128], BF16)
        identf = sb("identf", [128, 128])
        onesD = sb("onesD", [128, 128], BF16)
        onesH = sb("onesH", [128, 128], BF16)
        maskbd = sb("maskbd", [128, 4, 128], BF16)
        rmask = sb("rmask", [128, TT])
        corr = sb("corr", [128, 4, 16])
        c_gmix = sb("c_gmix", [128, 4, NDC])
        c_gffn = sb("c_gffn", [128, 4, NDC])
        c_gout = sb("c_gout", [128, 1, NDC])
        c_psc = sb("c_psc", [128, 2, NDC])
        c_gn = sb("c_gn", [128, 2, NDC])
        c_lbl = sb("c_lbl", [128, 4, NDC])
        c_lb = sb("c_lb", [128, 4, NDC])
        c_oml = sb("c_oml", [128, 2, NDC])
        c_noml = sb("c_noml", [128, 2, NDC])
        c_cw = sb("c_cw", [128, 4, 3, NFC])
        c_cb = sb("c_cb", [128, 4, NFC])
        c_eps = sb("c_eps", [128, 1])
        c_one = sb("c_one", [128, 1])
        ghP = sb("ghP", [128, 4, NFC, 2])
        ghS = sb("ghS", [128, 4, NFC, 2, 2])
        EhP = sb("EhP", [128, 2, NDC, 16], BF16)
        phist = sb("phist", [128, 2, NDC, 16])
        pout = sb("pout", [128, 3, 2, NDC, 16])
        psum = es.enter_context(nc.psum_tensor("psum", [128, 8, 512], F32))

        tmp_ctr = [0]
        ps_ctr = [0]

        def tmp(n=TT, dtype=F32):
            s = tmp_ctr[0] % NTMP
            tmp_ctr[0] += 1
            key = S.fresh("t", s)
            ap = temps[:, s * TSLOT:(s + 1) * TSLOT]
            if dtype == BF16:
                ap = ap.bitcast(BF16)
            return ap[:, 0:n], key

        def psb(dtype=F32):
            b = ps_ctr[0] % 8
            ps_ctr[0] += 1
            key = S.fresh("ps", b)
            ap = psum[:, b, :]
            if dtype == BF16:
                ap = ap.bitcast(BF16)
            return ap, key

        def A(out, in_, func, r, w, bias=None, scale=None):
            kw = {}
            if bias is not None:
                kw["bias"] = bias
            if scale is not None:
                kw["scale"] = scale
            S.add("act", lambda e: e.activation(out=out, in_=in_, func=func, **kw), r, w)

        def TS(out, in0, s1, s2, op0, op1, r, w, eng="dve"):
            if s2 is None:
                S.add(eng, lambda e: e.tensor_scalar(out=out, in0=in0, scalar1=s1, scalar2=None, op0=op0), r, w)
            else:
                S.add(eng, lambda e: e.tensor_scalar(out=out, in0=in0, scalar1=s1, scalar2=s2, op0=op0, op1=op1), r, w)

        def STT(out, in0, sc, in1, op0, op1, r, w):
            S.add("dve", lambda e: e.scalar_tensor_tensor(out=out, in0=in0, scalar=sc, in1=in1, op0=op0, op1=op1), r, w)

        def TTo(out, in0, in1, op, r, w, eng="dve"):
            S.add(eng, lambda e: e.tensor_tensor(out=out, in0=in0, in1=in1, op=op), r, w)

        def CP(out, in_, r, w, eng="dve"):
            if eng == "act":
                S.add(eng, lambda e: e.activation(out=out, in_=in_, func=AF.Copy), r, w)
            else:
                S.add(eng, lambda e: e.tensor_copy(out=out, in_=in_), r, w)

        def MS(ap, val, w, eng="pool"):
            S.add(eng, lambda e: e.memset(ap, val), (), w)

        def RECIP(out, in_, r, w):
            S.add("dve", lambda e: e.reciprocal(out=out, in_=in_), r, w)

        def MM(out, pairs, r, w):
            def fn(e):
                n = len(pairs)
                ins = None
                for i, (l, rr) in enumerate(pairs):
                    ins = e.matmul(out, lhsT=l, rhs=rr, start=(i == 0), stop=(i == n - 1))
                return ins
            S.add("pe", fn, r, w)

        def MMseq(items, r, w):
            def fn(e):
                ins = None
                for out, pairs, st in items:
                    n = len(pairs)
                    for i, (l, rr) in enumerate(pairs):
                        ins = e.matmul(out, lhsT=l, rhs=rr, start=(st and i == 0), stop=(i == n - 1))
                return ins
            S.add("pe", fn, r, w)

        def TR(items, r, w):
            def fn(e):
                ins = None
                for out, in_, ident in items:
                    ins = e.transpose(out=out, in_=in_, identity=ident)
                return ins
            S.add("pe", fn, r, w)

        def DMA(eng, key, pairs, r, w, slow=False):
            def fn(e, sem):
                for out, in_ in pairs:
                    if slow:
                        e.dma_start(out=out, in_=in_, allow_slow_non_contiguous=True).then_inc(sem, 16)
                    else:
                        e.dma_start(out=out, in_=in_).then_inc(sem, 16)
            S.add(eng, fn, r, w, dma_key=key, ndma=len(pairs))

        C = lambda n: ("c", n)

        MS(identb[:], 0.0, [C("identb")])
        S.add("pool", lambda e: e.affine_select(out=identb[:], in_=identb[:], pattern=[[-1, 128]], compare_op=ALU.not_equal,
                                                fill=1.0, base=0, channel_multiplier=1), [C("identb")], [C("identb")])
        MS(identf[:], 0.0, [C("identf")])
        S.add("pool", lambda e: e.affine_select(out=identf[:], in_=identf[:], pattern=[[-1, 128]], compare_op=ALU.not_equal,
                                                fill=1.0, base=0, channel_multiplier=1), [C("identf")], [C("identf")])
        MS(onesD[:], 1.0 / D, [C("onesD")])
        MS(onesH[:], 1.0 / 128, [C("onesH")])
        MS(c_eps[:], EPS, [C("eps")])
        MS(c_one[:], 1.0, [C("one")])
        MS(maskbd[:], 1.0, [C("maskbd")])
        S.add("pool", lambda e: e.affine_select(out=maskbd[:], in_=maskbd[:], pattern=[[0, 4], [1, 128]], compare_op=ALU.is_ge,
                                                fill=0.0, base=0, channel_multiplier=-1), [C("maskbd")], [C("maskbd")])
        MS(maskbd[0:64, :, 64:128], 0.0, [C("maskbd")])
        MS(rmask[:], 1.0, [C("rmask")])
        MS(rmask[:].rearrange("p (c l) -> p c l", l=64)[:, :, 0:1], 0.0, [C("rmask")])
        MS(corr[:], 1.0, [C("corr")])
        for g in range(4):
            win = 2 ** (g + 1)
            for t in range(win - 1):
                MS(corr[:, g, t:t + 1], float(win) / float(t + 1), [C("corr")])
        MS(khl[:], 0.0, [C("khl")])
        def vload(dst, src, name):
            DMA("sp", "setup", [(dst, src)], (), [C(name)], slow=True)
        def vload2(dst, src, nl, name):
            DMA("sp", "setup", [(dst[:, l, :], src[l].rearrange("(c p) -> p c", p=128)) for l in range(nl)], (), [C(name)], slow=True)
        vload2(c_gmix, d_gmix, 4, "gmix")
        vload2(c_gffn, d_gffn, 4, "gffn")
        vload2(c_gout, d_gout, 1, "gout")
        vload2(c_psc, d_pscale, 2, "psc")
        vload2(c_gn, d_gn, 2, "gn")
        vload2(c_lbl, d_lbl, 4, "lbl")
        vload2(c_cb, d_cb, 4, "cb")
        for l in range(4):
            vload2(c_cw[:, l], d_cw[l], 3, "cw")
        A(c_lb[:], c_lbl[:], AF.Exp, [C("lbl")], [C("lb")])
        ssum, k_ssum = tmp(NDC)
        TTo(ssum, c_lb[:, 0, :], c_lb[:, 1, :], ALU.add, [C("lb")], [k_ssum])
        TTo(ssum, ssum, c_lb[:, 2, :], ALU.add, [C("lb"), k_ssum], [k_ssum])
        TTo(ssum, ssum, c_lb[:, 3, :], ALU.add, [C("lb"), k_ssum], [k_ssum])
        RECIP(ssum, ssum, [k_ssum], [k_ssum])
        num, k_num = tmp(NDC)
        TTo(c_noml[:, 0, :], c_lb[:, 1, :], ssum, ALU.mult, [C("lb"), k_ssum], [C("oml")])
        TTo(num, c_lb[:, 1, :], c_lb[:, 2, :], ALU.add, [C("lb")], [k_num])
        TTo(num, num, c_lb[:, 3, :], ALU.add, [C("lb"), k_num], [k_num])
        TTo(c_noml[:, 1, :], num, ssum, ALU.mult, [k_num, k_ssum, C("oml")], [C("oml")])
        TS(c_oml[:], c_noml[:], -1.0, 1.0, ALU.mult, ALU.add, [C("oml")], [C("oml")])
        TS(c_noml[:], c_noml[:], -1.0, None, ALU.add, None, [C("oml")], [C("oml")])

        wslot = {"wup": 0, "wdn": 0, "wqfg": 0, "wout": 0}

        def load_wup(i, grp):
            s = wslot["wup"] % 2
            wslot["wup"] += 1
            src = d_wup[i].rearrange("(c p) n -> p c n", p=128)
            c0 = grp * 256
            DMA("pool", ("wup", s), [(wup[:, s, :, 0, :], src[:, :, c0:c0 + 256]),
                                    (wup[:, s, :, 1, :], src[:, :, DFF + c0:DFF + c0 + 256])], (), [("wup", s)])
            return s

        def load_wdn(i, ec):
            s = wslot["wdn"] % 2
            wslot["wdn"] += 1
            src = d_wdn[i].rearrange("(c p) n -> p c n", p=128)
            DMA("pool", ("wdn", s), [(wdn[:, s, :, :], src[:, :, ec * 128:(ec + 1) * 128])], (), [("wdn", s)])
            return s

        def load_wqfg(j, h):
            s = wslot["wqfg"] % 2
            wslot["wqfg"] += 1
            src = d_win[j].rearrange("(c p) n -> p c n", p=128)
            prs = []
            for k, base in enumerate((0, D, 3 * D)):
                prs.append((wqfg[:, s, :, k, :], src[:, :, base + h * 128: base + (h + 1) * 128]))
            DMA("pool", ("wqfg", s), prs, (), [("wqfg", s)])
            return s

        def load_wi(j):
            src = d_win[j].rearrange("(c p) n -> p c n", p=128)
            DMA("pool", "wi", [(wi[:, :, :], src[:, :, 2 * D:3 * D])], (), ["wi"])

        def load_wout(j, ec):
            s = wslot["wout"] % 2
            wslot["wout"] += 1
            src = d_wout[j].rearrange("(c p) n -> p c n", p=128)
            DMA("pool", ("wout", s), [(wout[:, s, :, :], src[:, :, ec * 128:(ec + 1) * 128])], (), [("wout", s)])
            return s

        def load_poolw(j):
            src = d_poolw[j].rearrange("g (k p) n -> p g k n", p=128)
            DMA("pool", "poolw", [(poolw[:, :, :, :], src)], (), ["poolw"])

        class Tile:
            pass

        tiles = []
        for k in range(NPT):
            t = Tile()
            t.kind, t.k, t.nseg, t.L, t.T = "p", k, 1, TT, TT
            t.first, t.last = (k == 0), (k == NPT - 1)
            tiles.append(t)
        if NS > 0:
            t = Tile()
            t.kind, t.k, t.nseg, t.L, t.T = "s", 0, NS, 64, NS * 64
            t.first, t.last = True, True
            tiles.append(t)

        def v3(ap2d, t):
            return ap2d.rearrange("p (s l) -> p s l", s=t.nseg)

        def norm(t, gvec, dest, gkey):
            T = t.T
            ps, kps = psb()
            for dc in range(NDC):
                A(sq[:, dc, 0:T], x[:, dc, 0:T], AF.Square, [("x", dc)], [("hid", dc)])
            MM(ps[:, 0:T], [(onesD[:], sq[:, dc, 0:T]) for dc in range(NDC)],
               [("hid", dc) for dc in range(NDC)] + [C("onesD")], [kps])
            A(rstd[:, 0:T], ps[:, 0:T], AF.Ln, [kps, C("eps")], ["rstd"], bias=c_eps[:])
            A(rstd[:, 0:T], rstd[:, 0:T], AF.Exp, ["rstd"], ["rstd"], scale=-0.5)
            for dc in range(NDC):
                dst, wk = dest(dc)
                STT(dst, v3(x[:, dc, 0:T], t), gvec[:, dc:dc + 1], v3(rstd[:, 0:T], t), ALU.mult, ALU.mult,
                    [("x", dc), "rstd", C(gkey)], [wk])

        def hn_dest(t):
            return lambda dc: (v3(hnT[:, dc, 0:t.T], t), ("hn", dc))

        def pool_layer(t, i):
            j = i // 2
            T, L, ns = t.T, t.L, t.nseg
            W = 16 + L
            load_poolw(j)

            def Ev(dc):
                return E[:, dc, 0:ns * W].rearrange("p (s w) -> p s w", s=ns)

            if t.kind == "p":
                if t.first:
                    MS(E[:, :, 0:16], 0.0, [("hn", dc) for dc in range(NDC)], eng="dve")
                else:
                    CP(E[:, :, 1:16], EhP[:, j, :, 1:16], [("EhP", j)], [("hn", dc) for dc in range(NDC)], eng="dve")
            else:
                for s in range(ns):
                    DMA("sp", ("phist", s), [(phist[:, s, dc, 1:16], d_spool[j, s].rearrange("t (c p) -> p c t", p=128)[:, dc, :]) for dc in range(NDC)],
                        (), [("phist", s)], slow=True)
                    CP(E[:, :, s * W + 1: s * W + 16], phist[:, s, :, 1:16], [("phist", s)],
                       [("hn", dc) for dc in range(NDC)], eng="dve")
            norm(t, c_gmix[:, i, :], lambda dc: (Ev(dc)[:, :, 16:16 + L], ("hn", dc)), "gmix")
            if t.last:
                for s in range(ns):
                    st = 0 if t.kind == "p" else 1 + s
                    for dc in range(NDC):
                        a = s * L + L - 16
                        STT(pout[:, st, j, dc, :], x[:, dc, a:a + 16], c_gmix[:, i, dc:dc + 1], rstd[:, a:a + 16],
                            ALU.mult, ALU.mult, [("x", dc), "rstd"], [("pout", st, j)])
                    dstd = (o_poolp[j] if t.kind == "p" else o_pools[j, s]).rearrange("t (c p) -> p c t", p=128)
                    DMA("sp", "outs", [(dstd[:, dc, :], pout[:, st, j, dc, 1:16]) for dc in range(NDC)], [("pout", st, j)], [], slow=True)
            for dc in range(NDC):
                g = dc // 2
                win = 2 ** (g + 1)
                cur = Ev(dc)
                ck = ("hn", dc)
                sh = 1
                for lev in range(g + 1):
                    lo = 2 * sh
                    nt, kn = tmp(ns * W)
                    nv = nt.rearrange("p (s w) -> p s w", s=ns)
                    TTo(nv[:, :, lo:W], cur[:, :, lo:W], cur[:, :, lo - sh:W - sh], ALU.add, [ck], [kn])
                    cur, ck = nv, kn
                    sh *= 2
                if t.kind == "p" and t.first:
                    TTo(cur[:, 0, 16:32], cur[:, 0, 16:32], corr[:, g, :], ALU.mult, [ck, C("corr")], [ck])
                STT(v3(sq[:, dc, 0:T], t), cur[:, :, 16:W], 1.0 / win, Ev(dc)[:, :, 16:W], ALU.mult, ALU.subtract,
                    [ck, ("hn", dc)], [("hid", dc)])
            if t.kind == "p" and not t.last:
                CP(EhP[:, j, :, 1:16], E[:, :, L + 1:L + 16], [("hn", dc) for dc in range(NDC)], [("EhP", j)], eng="act")
            for ec in range(NDC):
                g, eo = ec // 2, ec % 2
                ps, kps = psb()
                MM(ps[:, 0:T], [(poolw[:, g, kc, eo * 128:(eo + 1) * 128], sq[:, 2 * g + kc, 0:T]) for kc in range(2)],
                   [("hid", 2 * g), ("hid", 2 * g + 1), "poolw"], [kps])
                STT(x[:, ec, 0:T], ps[:, 0:T], c_psc[:, j, ec:ec + 1], x[:, ec, 0:T], ALU.mult, ALU.add,
                    [kps, ("x", ec), C("psc")], [("x", ec)])

        def ffn_layer(t, i):
            T, L, ns = t.T, t.L, t.nseg
            norm(t, c_gffn[:, i, :], hn_dest(t), "gffn")
            if t.kind == "p":
                gh = lambda fc: ghP[:, i, fc, :].rearrange("p (s j) -> p s j", s=1)
                ghk = lambda fc: ("ghP", i, fc)
                if t.first:
                    MS(ghP[:, i, :, :], 0.0, [ghk(fc) for fc in range(NFC)], eng="dve")
            else:
                gh = lambda fc: ghS[:, i, fc, 0:ns, :]
                ghk = lambda fc: ("ghS", i, fc)
                for s in range(ns):
                    DMA("sp", ("ghS", i), [(ghS[:, i, :, s, jj], d_sconv[i, s, jj].rearrange("(c p) -> p c", p=128)) for jj in range(2)],
                        (), [ghk(fc) for fc in range(NFC)], slow=True)
            W = 2 + L
            for grp in range(NFC // 2):
                ws = load_wup(i, grp)
                for fl in range(2):
                    fc = 2 * grp + fl
                    pg, kg = psb()
                    pv, kv = psb()
                    rd = [("hn", dc) for dc in range(NDC)] + [("wup", ws)]
                    MM(pg[:, 0:T], [(wup[:, ws, dc, 0, fl * 128:(fl + 1) * 128], hnT[:, dc, 0:T]) for dc in range(NDC)], rd, [kg])
                    MM(pv[:, 0:T], [(wup[:, ws, dc, 1, fl * 128:(fl + 1) * 128], hnT[:, dc, 0:T]) for dc in range(NDC)], rd, [kv])
                    G, kG = tmp(ns * W)
                    Gv = G.rearrange("p (s w) -> p s w", s=ns)
                    acc, kacc = tmp(T)
                    accv = v3(acc, t)
                    A(Gv[:, :, 2:W], v3(pg[:, 0:T], t), AF.Copy, [kg], [kG])
                    CP(Gv[:, :, 0:2], gh(fc), [ghk(fc)], [kG], eng="dve")
                    A(acc, pg[:, 0:T], AF.Identity, [kg, C("cw"), C("cb")], [kacc],
                      bias=c_cb[:, i, fc:fc + 1], scale=c_cw[:, i, 2, fc:fc + 1])
                    STT(accv, Gv[:, :, 1:1 + L], c_cw[:, i, 1, fc:fc + 1], accv, ALU.mult, ALU.add, [kG, kacc], [kacc])
                    STT(accv, Gv[:, :, 0:L], c_cw[:, i, 0, fc:fc + 1], accv, ALU.mult, ALU.add, [kG, kacc], [kacc])
                    CP(gh(fc), Gv[:, :, L:L + 2], [kG], [ghk(fc)], eng="dve")
                    sl, ksl = tmp(T)
                    A(sl, acc, AF.Silu, [kacc], [ksl])
                    TTo(hid[:, fc, 0:T], sl, pv[:, 0:T], ALU.mult, [ksl, kv], [("hid", fc)])
            if t.last:
                for s in range(ns):
                    dstd = (o_convp[i] if t.kind == "p" else o_convs[i, s])
                    src = ghP[:, i, :, :] if t.kind == "p" else ghS[:, i, :, s, :]
                    DMA("sp", "outs", [(dstd[jj].rearrange("(c p) -> p c", p=128), src[:, :, jj]) for jj in range(2)],
                        [ghk(fc) for fc in range(NFC)], [], slow=True)
            for ec in range(NDC):
                ws = load_wdn(i, ec)
                ps, kps = psb()
                MM(ps[:, 0:T], [(wdn[:, ws, fc, :], hid[:, fc, 0:T]) for fc in range(NFC)],
                   [("hid", fc) for fc in range(NFC)] + [("wdn", ws)], [kps])
                TTo(x[:, ec, 0:T], x[:, ec, 0:T], ps[:, 0:T], ALU.add, [kps, ("x", ec)], [("x", ec)])

        def hgrn_layer(t, i):
            j = i // 2
            T, L, ns = t.T, t.L, t.nseg
            nch = T // 64
            npair = T // 128
            norm(t, c_gmix[:, i, :], hn_dest(t), "gmix")
            load_wi(j)
            if t.kind == "p":
                sid = [j] * nch
            else:
                sid = [0 if s == 0 else 2 for s in range(ns)] if j == 0 else [1 if s == 0 else 2 for s in range(ns)]
            if t.kind == "p":
                if t.first:
                    MS(S32[j][:], 0.0, [("S", j, h) for h in range(NH)], eng="dve")
                    MS(Scar[j][:], 0.0, [("Sc", j, h) for h in range(NH)], eng="dve")
            else:
                for s in range(ns):
                    b = sid[s]
                    DMA("sp", ("Sld", b), [(S32[b][:], d_shg[j, s].rearrange("h k v -> k h v"))], (),
                        [("S", b, h) for h in range(NH)])
                    CP(Scar[b][:], S32[b][:], [("S", b, h) for h in range(NH)], [("Sc", b, h) for h in range(NH)], eng="act")
            for tb in range(npair):
                for vh in range(2):
                    ps, kps = psb()
                    MM(ps[:, :], [(hnT[:, dc, tb * 128:(tb + 1) * 128], wi[:, dc, vh * 512:(vh + 1) * 512]) for dc in range(NDC)],
                       [("hn", dc) for dc in range(NDC)] + ["wi"], [kps])
                    CP(V[:, tb, vh * 512:(vh + 1) * 512], ps[:, :], [kps], [("hid", 2 * tb + vh)], eng="act")

            st = {}

            def stageA(h):
                ws = load_wqfg(j, h)
                se = h % 2
                pq, kq = psb()
                pf, kf = psb()
                pgt, kgt = psb()
                rd = [("hn", dc) for dc in range(NDC)] + [("wqfg", ws)]
                for k, (pp, kk) in enumerate(((pq, kq), (pf, kf), (pgt, kgt))):
                    MM(pp[:, 0:T], [(wqfg[:, ws, dc, k, :], hnT[:, dc, 0:T]) for dc in range(NDC)], rd, [kk])
                sn, ksn = tmp(T)
                A(sn, pf[:, 0:T], AF.Sigmoid, [kf], [ksn], scale=-1.0)
                sgq, ksgq = tmp(T)
                A(sgq, pq[:, 0:T], AF.Sigmoid, [kq], [ksgq])
                gate, kgate = gateb[:, se, 0:T], ("gate", se)
                sgg, ksgg = tmp(T)
                A(sgg, pgt[:, 0:T], AF.Sigmoid, [kgt], [ksgg])
                q, kq2 = tmp(T)
                TTo(q, sgq, pq[:, 0:T], ALU.mult, [ksgq, kq], [kq2])
                TTo(gate, sgg, pgt[:, 0:T], ALU.mult, [ksgg, kgt], [kgate])
                gl, kgl = tmp(T)
                A(gl, sn, AF.Ln, [ksn, C("oml"), C("one")], [kgl], bias=c_one[:], scale=c_noml[:, j, h:h + 1])
                b, kb = tmp(T)
                S.add("dve", lambda e: e.tensor_tensor_scan(out=b, data0=rmask[:, 0:T], data1=gl, initial=0.0,
                                                            op0=ALU.mult, op1=ALU.add), [kgl, C("rmask")], [kb])
                eb, keb = tmp(T)
                A(eb, b, AF.Exp, [kb], [keb])
                enb, kenb = tmp(T)
                A(enb, b, AF.Exp, [kb], [kenb], scale=-1.0)
                qt, kqt = qtb[:, se, 0:T], ("qt", se)
                TTo(qt, q, eb, ALU.mult, [kq2, keb], [kqt])
                kt, kkt = tmp(T, BF16)
                STT(kt, sn, c_oml[:, j, h:h + 1], enb, ALU.mult, ALU.mult, [ksn, kenb, C("oml")], [kkt])
                for c in range(nch):
                    p_, hf = c // 2, c % 2
                    TS(khl[:, se, hf, p_, hf * 64:(hf + 1) * 64], kt[:, c * 64:(c + 1) * 64], eb[:, c * 64 + 63:c * 64 + 64], None,
                       ALU.mult, None, [kkt, keb, C("khl")], [("khl", se)])
                pT, kpT = psb(BF16)
                TR([(pT[:, (2 * p_ + hf) * 128:(2 * p_ + hf + 1) * 128], khl[:, se, hf, p_, :], identb[:])
                    for p_ in range(npair) for hf in range(2)], [("khl", se), C("identb")], [kpT])
                kT, kkT = tmp(1024, BF16)
                CP(kT[:, 0:nch * 128], pT[:, 0:nch * 128], [kpT], [kkT], eng="act")
                pA, kpA = psb()
                MMseq([(pA[:, p_ * 128:(p_ + 1) * 128], [(kt[:, p_ * 128:(p_ + 1) * 128], qt[:, p_ * 128:(p_ + 1) * 128])], True)
                       for p_ in range(npair)], [kkt, kqt], [kpA])
                TTo(ATs[:, se, 0:npair, :], pA[:, 0:npair * 128].rearrange("p (a b) -> p a b", b=128), maskbd[:, 0:npair, :], ALU.mult,
                    [kpA, C("maskbd")], [("ATs", se)])
                pU = []
                for u0 in range(0, nch, 4):
                    pu, kpu = psb()
                    n = min(4, nch - u0)
                    MMseq([(pu[:, (c - u0) * 128:(c - u0 + 1) * 128], [(kT[:, c * 128:(c + 1) * 128], V[:, c // 2, h * 128:(h + 1) * 128])], True)
                           for c in range(u0, u0 + n)], [kkT] + [("hid", pg_) for pg_ in range(2 * (u0 // 2), 2 * ((u0 + n - 1) // 2) + 2)], [kpu])
                    pU.append((pu, kpu))
                for c in range(nch):
                    b_ = sid[c]
                    pu, kpu = pU[c // 4]
                    STT(S32[b_][:, h, :], S32[b_][:, h, :], eb[:, c * 64 + 63:c * 64 + 64], pu[:, (c % 4) * 128:(c % 4 + 1) * 128],
                        ALU.mult, ALU.add, [("S", b_, h), keb, kpu], [("S", b_, h)])
                    lastc = (c == nch - 1) or (t.kind == "s")
                    if lastc:
                        CP(Scar2[:, se, b_, :], S32[b_][:, h, :], [("S", b_, h)], [("Sc2", se, b_)], eng="act")
                    else:
                        CP(Sv[:, se, c, :], S32[b_][:, h, :], [("S", b_, h)], [("Sv", se, c)], eng="act")
                st[h] = dict(se=se, qt=qt, kqt=kqt, gate=gate, kgate=kgate)

            def stageB(h):
                d = st.pop(h)
                se, qt, kqt, gate, kgate = d["se"], d["qt"], d["kqt"], d["gate"], d["kgate"]
                po, kpo = psb()
                items = []
                rd = [("ATs", se), kqt]
                for c in range(nch):
                    p_ = c // 2
                    if c % 2 == 0:
                        items.append((po[:, p_ * 128:(p_ + 1) * 128], [(V[:, p_, h * 128:(h + 1) * 128], ATs[:, se, p_, :])], True))
                        rd += [("hid", 2 * p_), ("hid", 2 * p_ + 1)]
                    b_ = sid[c]
                    if c == 0 or t.kind == "s":
                        lhs = Scar[b_][:, h, :]
                        rd.append(("Sc", b_, h))
                    else:
                        lhs = Sv[:, se, c - 1, :]
                        rd.append(("Sv", se, c - 1))
                    items.append((po[:, c * 64:(c + 1) * 64], [(lhs, qt[:, c * 64:(c + 1) * 64])], False))
                MMseq(items, rd, [kpo])
                osq, kosq = tmp(T, BF16)
                A(osq, po[:, 0:T], AF.Square, [kpo], [kosq])
                pss, kpss = psb()
                MM(pss[:, 0:T], [(onesH[:], osq)], [kosq, C("onesH")], [kpss])
                rs, krs = tmp(T)
                A(rs, pss[:, 0:T], AF.Ln, [kpss, C("eps")], [krs], bias=c_eps[:])
                A(rs, rs, AF.Exp, [krs], [krs], scale=-0.5)
                t1, kt1 = tmp(T)
                STT(t1, po[:, 0:T], c_gn[:, j, h:h + 1], rs, ALU.mult, ALU.mult, [kpo, krs, C("gn")], [kt1])
                TTo(ofin[:, h, 0:T], t1, gate, ALU.mult, [kt1, kgate], [("hid", 8 + h)])
                for b_ in sorted(set(sid)):
                    CP(Scar[b_][:, h, :], Scar2[:, se, b_, :], [("Sc2", se, b_)], [("Sc", b_, h)], eng="act")

            stageA(0)
            for h in range(NH):
                if h + 1 < NH:
                    stageA(h + 1)
                stageB(h)
            if t.last:
                for s in range(ns):
                    b_ = sid[0] if t.kind == "p" else sid[s]
                    dstd = (o_hgp[j] if t.kind == "p" else o_hgs[j, s]).rearrange("h k v -> k h v")
                    DMA("sp", "outs", [(dstd, S32[b_][:])], [("S", b_, h) for h in range(NH)], [])
            for ec in range(NDC):
                ws = load_wout(j, ec)
                ps, kps = psb()
                MM(ps[:, 0:T], [(wout[:, ws, dc, :], ofin[:, dc, 0:T]) for dc in range(NDC)],
                   [("hid", 8 + dc) for dc in range(NDC)] + [("wout", ws)], [kps])
                TTo(x[:, ec, 0:T], x[:, ec, 0:T], ps[:, 0:T], ALU.add, [kps, ("x", ec)], [("x", ec)])


        xin_ctr = [0]

        def load_x(t):
            ntb = t.T // 128
            src = d_xp if t.kind == "p" else d_xs
            r0 = t.k * TT if t.kind == "p" else 0
            for tb in range(ntb):
                s = xin_ctr[0] % 4
                xin_ctr[0] += 1
                DMA("sp", ("io", s), [(xin[:, s, :], src[r0 + tb * 128: r0 + (tb + 1) * 128, :])], (), [("io", s)])
                for hb in range(2):
                    ps, kps = psb()
                    TR([(ps[:, q_ * 128:(q_ + 1) * 128], xin[:, s, (hb * 4 + q_) * 128:(hb * 4 + q_ + 1) * 128], identf[:]) for q_ in range(4)],
                       [("io", s), C("identf")], [kps])
                    CP(x[:, hb * 4:hb * 4 + 4, tb * 128:(tb + 1) * 128], ps[:, :].rearrange("p (a b) -> p a b", b=128), [kps],
                       [("x", hb * 4 + q_) for q_ in range(4)], eng="act")

        yo_ctr = [0]

        def store_y(t):
            T = t.T
            ntb = T // 128
            norm(t, c_gout[:, 0, :], lambda dc: (v3(x[:, dc, 0:T], t), ("x", dc)), "gout")
            dst = o_yp if t.kind == "p" else o_ys
            r0 = t.k * TT if t.kind == "p" else 0
            for tb in range(ntb):
                s = xin_ctr[0] % 4
                xin_ctr[0] += 1
                for hb in range(2):
                    ps, kps = psb()
                    TR([(ps[:, q_ * 128:(q_ + 1) * 128], x[:, hb * 4 + q_, tb * 128:(tb + 1) * 128], identf[:]) for q_ in range(4)],
                       [("x", hb * 4 + q_) for q_ in range(4)] + [C("identf")], [kps])
                    CP(yout[:, s, hb * 512:(hb + 1) * 512], ps[:, :], [kps], [("io", s)], eng="act")
                DMA("sp", ("io", s), [(dst[r0 + tb * 128: r0 + (tb + 1) * 128, :], yout[:, s, :])], [("io", s)], [])

        for t in tiles:
            load_x(t)
            for i in LY:
                if i % 2 == 0:
                    pool_layer(t, i)
                else:
                    hgrn_layer(t, i)
                ffn_layer(t, i)
            store_y(t)

        S.finalize()
        esems = {e: es.enter_context(nc.semaphore("se_" + e)) for e in Sched.ENGS}
        dsems = {}
        for n, k in enumerate(S.dma_cnt):
            dsems[k] = es.enter_context(nc.semaphore(f"sd_{n}"))
        out_keys = ["outs"] + [("io", s) for s in range(4)]
        with nc.Block() as block:
            @block.tensor
            def _(e):
                S.emit("pe", e, esems, dsems)

            @block.scalar
            def _(e):
                S.emit("act", e, esems, dsems)

            @block.vector
            def _(e):
                S.emit("dve", e, esems, dsems)

            @block.gpsimd
            def _(e):
                S.emit("pool", e, esems, dsems)

            @block.sync
            def _(e):
                S.emit("sp", e, esems, dsems)
                for k in out_keys:
                    if k in S.dma_cnt:
                        e.wait_ge(dsems[k], 16 * S.dma_cnt[k])
    return nc


_NC_CACHE = {}


def run_cores(cfg, in_maps):
    key = (cfg.layers, cfg.npt, cfg.ns)
    if key not in _NC_CACHE:
        _NC_CACHE[key] = build(cfg)
    nc = _NC_CACHE[key]
    return run_bass_kernel_spmd(nc, in_maps, core_ids=list(range(len(in_maps))))


def kernel(x_prompt, x_sample, state_pool, state_hgrn, state_ffn_conv, norm_mix_g, pool_w, pool_scale,
           hgrn_w_in, hgrn_lb_logits, hgrn_norm_g, hgrn_w_out, norm_ffn_g, ffn_w_up, ffn_conv_w,
           ffn_conv_b, ffn_w_down, norm_out_g):
    f = lambda a: np.ascontiguousarray(np.asarray(a, dtype=np.float32))
    x_prompt, x_sample = f(x_prompt), f(x_sample)
    state_pool, state_hgrn, state_ffn_conv = f(state_pool), f(state_hgrn), f(state_ffn_conv)
    BP, SEQ, _ = x_prompt.shape
    NSB = x_sample.shape[0]
    ncores = 8
    nsc = NSB // ncores
    cfg = Cfg(layers=(0, 1, 2, 3), npt=SEQ // TT, ns=nsc)
    shared = {
        "norm_mix_g": f(norm_mix_g), "pool_w": f(pool_w), "pool_scale": f(pool_scale), "hgrn_w_in": f(hgrn_w_in),
        "hgrn_lb_logits": f(hgrn_lb_logits), "hgrn_norm_g": f(hgrn_norm_g), "hgrn_w_out": f(hgrn_w_out),
        "norm_ffn_g": f(norm_ffn_g), "ffn_w_up": f(ffn_w_up), "ffn_conv_w": f(ffn_conv_w), "ffn_conv_b": f(ffn_conv_b),
        "ffn_w_down": f(ffn_w_down), "norm_out_g": f(norm_out_g).reshape(1, D),
    }
    in_maps = []
    for c in range(ncores):
        sl = slice(c * nsc, (c + 1) * nsc)
        m = dict(shared)
        m["xp"] = x_prompt[c % BP]
        m["xs"] = x_sample[sl].reshape(nsc * 64, D)
        m["s_pool"] = np.ascontiguousarray(state_pool[:, sl])
        m["s_hgrn"] = np.ascontiguousarray(state_hgrn[:, sl])
        m["s_conv"] = np.ascontiguousarray(state_ffn_conv[:, sl])
        in_maps.append(m)
    res = run_cores(cfg, in_maps).results
    y_prompt = np.stack([res[b]["yp"] for b in range(BP)])
    y_sample = np.concatenate([res[c]["ys"].reshape(nsc, 64, D) for c in range(ncores)])
    pool_p = np.stack([res[b]["npool_p"] for b in range(BP)], axis=1)
    pool_s = np.concatenate([res[c]["npool_s"] for c in range(ncores)], axis=1)
    hg_p = np.stack([res[b]["nhgrn_p"] for b in range(BP)], axis=1)
    hg_s = np.concatenate([res[c]["nhgrn_s"] for c in range(ncores)], axis=1)
    cv_p = np.stack([res[b]["nconv_p"] for b in range(BP)], axis=1)
    cv_s = np.concatenate([res[c]["nconv_s"] for c in range(ncores)], axis=1)
    return tuple(np.ascontiguousarray(a, dtype=np.float32) for a in (y_prompt, y_sample, pool_p, pool_s, hg_p, hg_s, cv_p, cv_s))
```

```python
import sys
from contextlib import ExitStack
import numpy as np
import concourse.bass as bass
import concourse.mybir as mybir
from concourse.bass_utils import run_bass_kernel_spmd

F32 = mybir.dt.float32
BF16 = mybir.dt.bfloat16
AF = mybir.ActivationFunctionType
ALU = mybir.AluOpType

D = 1024
NDC = 8
DFF = 2816
NFC = 22
NH = 8
EPS = 1e-6
TT = 512
TSLOT = 528
NTMP = 10
import os
SAME_ENG_SYNC = os.environ.get("K_SES", "1") == "1"


class Op:
    __slots__ = ("eng", "fn", "deps", "dma_key", "ndma", "idx", "ms", "dma_val", "is_ms", "label", "ninst")

    def __init__(self, eng, fn, dma_key, ndma):
        self.eng = eng
        self.fn = fn
        self.dma_key = dma_key
        self.ndma = ndma
        self.deps = None
        self.ms = 0
        self.is_ms = False
        self.dma_val = 0


class Sched:
    ENGS = ("pe", "act", "dve", "pool", "sp")

    def __init__(self):
        self.streams = {e: [] for e in self.ENGS}
        self.last_w = {}
        self.readers = {}
        self.dma_cnt = {}
        self.gen = {}

    def fresh(self, kind, slot):
        g = self.gen.get((kind, slot), 0)
        old = (kind, slot, g)
        new = (kind, slot, g + 1)
        self.gen[(kind, slot)] = g + 1
        if old in self.last_w:
            self.last_w[new] = self.last_w.pop(old)
        if old in self.readers:
            self.readers[new] = self.readers.pop(old)
        return new

    def _check(self, r):
        if len(r) == 3 and r[0] in ("t", "ps"):
            assert self.gen.get((r[0], r[1]), 0) == r[2], f"stale resource {r}"

    def add(self, eng, fn, reads=(), writes=(), dma_key=None, ndma=1):
        op = Op(eng, fn, dma_key, ndma)
        try:
            f = sys._getframe(2)
            g = f.f_back
            op.label = f"{g.f_code.co_name}:{g.f_lineno}" if g is not None else f.f_code.co_name
        except Exception:
            op.label = "?"
        op.ninst = 1
        deps = {}

        def dep(o):
            if o is None or o is op:
                return
            k = o.dma_key if o.dma_key is not None else o.eng
            p = deps.get(k)
            if p is None or self._later(o, p):
                deps[k] = o

        for r in reads:
            self._check(r)
            dep(self.last_w.get(r))
        for r in writes:
            self._check(r)
            dep(self.last_w.get(r))
            for o in self.readers.get(r, {}).values():
                dep(o)
        op.idx = len(self.streams[eng])
        self.streams[eng].append(op)
        if dma_key is not None:
            c = self.dma_cnt.get(dma_key, 0) + ndma
            self.dma_cnt[dma_key] = c
            op.dma_val = 16 * c
        for r in reads:
            d = self.readers.setdefault(r, {})
            k = dma_key if dma_key is not None else eng
            d[k] = op
        for r in writes:
            self.last_w[r] = op
            self.readers[r] = {}
        op.deps = list(deps.values())
        return op

    @staticmethod
    def _later(a, b):
        if a.dma_key is not None:
            return a.dma_val > b.dma_val
        return a.idx > b.idx

    def finalize(self):
        for e in self.ENGS:
            for op in self.streams[e]:
                for d in op.deps:
                    if d.dma_key is None:
                        if d.eng == op.eng and (d.eng == "pe" or not SAME_ENG_SYNC):
                            continue
                        d.is_ms = True
        for e in self.ENGS:
            c = 0
            for op in self.streams[e]:
                if op.is_ms:
                    c += 1
                    op.ms = c

    def emit(self, eng_name, eng, esems, dsems):
        waited = {}
        for op in self.streams[eng_name]:
            for d in op.deps:
                if d.dma_key is not None:
                    key, val, sem = ("d", d.dma_key), d.dma_val, dsems[d.dma_key]
                else:
                    if d.eng == op.eng and (d.eng == "pe" or not SAME_ENG_SYNC):
                        continue
                    key, val, sem = ("e", d.eng), d.ms, esems[d.eng]
                if waited.get(key, 0) >= val:
                    continue
                waited[key] = val
                eng.wait_ge(sem, val)
            if op.dma_key is not None:
                op.fn(eng, dsems[op.dma_key])
            else:
                ins = op.fn(eng)
                if op.is_ms:
                    ins.then_inc(esems[eng_name], 1)


class Cfg:
    def __init__(self, layers=(0, 1, 2, 3), npt=16, ns=2):
        self.layers = tuple(layers)
        self.npt = npt
        self.ns = ns


def build(cfg):
    nc = bass.Bass("TRN2", target_bir_lowering=False)
    LY = cfg.layers
    NL = len(LY)
    NPT = cfg.npt
    NS = cfg.ns
    SEQP = NPT * TT
    pool_layers = [i for i in LY if i % 2 == 0]
    hg_layers = [i for i in LY if i % 2 == 1]
    NPL = max(1, len(pool_layers))
    NHL = max(1, len(hg_layers))

    def dt(name, shape, kind):
        return nc.dram_tensor(name, list(shape), F32, kind=kind).ap()

    I, O = "ExternalInput", "ExternalOutput"
    d_xp = dt("xp", [max(SEQP, 1), D], I)
    d_xs = dt("xs", [max(NS, 1) * 64, D], I)
    d_spool = dt("s_pool", [2, max(NS, 1), 15, D], I)
    d_shg = dt("s_hgrn", [2, max(NS, 1), NH, 128, 128], I)
    d_sconv = dt("s_conv", [4, max(NS, 1), 2, DFF], I)
    d_gmix = dt("norm_mix_g", [4, D], I)
    d_poolw = dt("pool_w", [2, 4, 256, 256], I)
    d_pscale = dt("pool_scale", [2, D], I)
    d_win = dt("hgrn_w_in", [2, D, 4 * D], I)
    d_lbl = dt("hgrn_lb_logits", [4, D], I)
    d_gn = dt("hgrn_norm_g", [2, D], I)
    d_wout = dt("hgrn_w_out", [2, D, D], I)
    d_gffn = dt("norm_ffn_g", [4, D], I)
    d_wup = dt("ffn_w_up", [4, D, 2 * DFF], I)
    d_cw = dt("ffn_conv_w", [4, 3, DFF], I)
    d_cb = dt("ffn_conv_b", [4, DFF], I)
    d_wdn = dt("ffn_w_down", [4, DFF, D], I)
    d_gout = dt("norm_out_g", [1, D], I)
    o_yp = dt("yp", [max(SEQP, 1), D], O)
    o_ys = dt("ys", [max(NS, 1) * 64, D], O)
    o_poolp = dt("npool_p", [2, 15, D], O)
    o_pools = dt("npool_s", [2, max(NS, 1), 15, D], O)
    o_hgp = dt("nhgrn_p", [2, NH, 128, 128], O)
    o_hgs = dt("nhgrn_s", [2, max(NS, 1), NH, 128, 128], O)
    o_convp = dt("nconv_p", [4, 2, DFF], O)
    o_convs = dt("nconv_s", [4, max(NS, 1), 2, DFF], O)

    S = Sched()
    es = ExitStack()
    with es:
        def sb(name, shape, dtype=F32):
            return es.enter_context(nc.sbuf_tensor(name, list(shape), dtype))

        x = sb("x", [128, NDC, TT])
        E = sb("E", [128, NDC, 16 + TT], BF16)
        hnT = E
        rstd = sb("rstd", [128, TT])
        hid = sb("hid", [128, NFC, TT], BF16)
        sq = hid
        V = hid[:, 0:8, :].rearrange("p (a b) t -> p a (b t)", b=2)
        ofin = hid[:, 8:16, :]
        gateb = sb("gateb", [128, 4, TT], BF16)
        qtb = sb("qtb", [128, 4, TT], BF16)
        ktb = sb("ktb", [128, 2, TT], BF16)
        ebb = sb("ebb", [128, 2, TT])
        temps = sb("temps", [128, NTMP * TSLOT])
        wup = sb("wup", [128, 2, NDC, 2, 256], BF16)
        wdn = sb("wdn", [128, 2, NFC, 128], BF16)
        wqfg = sb("wqfg", [128, 2, NDC, 3, 128], BF16)
        wi = sb("wi", [128, NDC, D], BF16)
        wout = sb("wout", [128, 2, NDC, 128], BF16)
        poolw = sb("poolw", [128, 4, 2, 256], BF16)
        S32 = [sb(f"S32_{k}", [128, NH, 128]) for k in range(3)]
        Scar = [sb(f"Scar_{k}", [128, NH, 128], BF16) for k in range(3)]
        Scar2 = sb("Scar2", [128, 3, 3, 128], BF16)
        Sv = sb("Sv", [128, 3, 8, 128], BF16)
        khl = sb("khl", [128, 2, 2, 4, 128], BF16)
        ATs = sb("ATs", [128, 3, 4, 128], BF16)
        xin = sb("xin", [128, 2, D])
        yout = xin
        identb = sb("identb", [128, 128], BF16)
        identf = sb("identf", [128, 128])
        onesD = sb("onesD", [128, 128], BF16)
        onesH = sb("onesH", [128, 128], BF16)
        maskbd = sb("maskbd", [128, 4, 128], BF16)
        rmask = sb("rmask", [128, TT])
        corr = sb("corr", [128, 4, 16])
        c_gmix = sb("c_gmix", [128, 4, NDC])
        c_gffn = sb("c_gffn", [128, 4, NDC])
        c_gout = sb("c_gout", [128, 1, NDC])
        c_psc = sb("c_psc", [128, 2, NDC])
        c_gn = sb("c_gn", [128, 2, NDC])
        c_lbl = sb("c_lbl", [128, 4, NDC])
        c_lb = sb("c_lb", [128, 4, NDC])
        c_oml = sb("c_oml", [128, 2, NDC])
        c_noml = sb("c_noml", [128, 2, NDC])
        c_cw = sb("c_cw", [128, 4, 3, NFC])
        c_cb = sb("c_cb", [128, 4, NFC])
        c_eps = sb("c_eps", [128, 1])
        c_one = sb("c_one", [128, 1])
        ghP = sb("ghP", [128, 4, NFC, 2])
        ghS = sb("ghS", [128, 4, NFC, 2, 2])
        EhP = sb("EhP", [128, 2, NDC, 16], BF16)
        phist = sb("phist", [128, 2, NDC, 16])
        pout = sb("pout", [128, 3, 2, NDC, 16])
        psum = es.enter_context(nc.psum_tensor("psum", [128, 8, 512], F32))

        tmp_ctr = [0]
        ps_ctr = [0]

        def tmp(n=TT, dtype=F32):
            s = tmp_ctr[0] % NTMP
            tmp_ctr[0] += 1
            key = S.fresh("t", s)
            ap = temps[:, s * TSLOT:(s + 1) * TSLOT]
            if dtype == BF16:
                ap = ap.bitcast(BF16)
            return ap[:, 0:n], key

        def psb(dtype=F32):
            b = ps_ctr[0] % 8
            ps_ctr[0] += 1
            key = S.fresh("ps", b)
            ap = psum[:, b, :]
            if dtype == BF16:
                ap = ap.bitcast(BF16)
            return ap, key

        def A(out, in_, func, r, w, bias=None, scale=None):
            kw = {}
            if bias is not None:
                kw["bias"] = bias
            if scale is not None:
                kw["scale"] = scale
            S.add("act", lambda e: e.activation(out=out, in_=in_, func=func, **kw), r, w)

        def TS(out, in0, s1, s2, op0, op1, r, w, eng="dve"):
            if s2 is None:
                S.add(eng, lambda e: e.tensor_scalar(out=out, in0=in0, scalar1=s1, scalar2=None, op0=op0), r, w)
            else:
                S.add(eng, lambda e: e.tensor_scalar(out=out, in0=in0, scalar1=s1, scalar2=s2, op0=op0, op1=op1), r, w)

        def STT(out, in0, sc, in1, op0, op1, r, w):
            S.add("dve", lambda e: e.scalar_tensor_tensor(out=out, in0=in0, scalar=sc, in1=in1, op0=op0, op1=op1), r, w)

        def TTo(out, in0, in1, op, r, w, eng="dve"):
            S.add(eng, lambda e: e.tensor_tensor(out=out, in0=in0, in1=in1, op=op), r, w)

        def CP(out, in_, r, w, eng="dve"):
            if eng == "act":
                S.add(eng, lambda e: e.activation(out=out, in_=in_, func=AF.Copy), r, w)
            else:
                S.add(eng, lambda e: e.tensor_copy(out=out, in_=in_), r, w)

        def MS(ap, val, w, eng="pool"):
            S.add(eng, lambda e: e.memset(ap, val), (), w)

        def RECIP(out, in_, r, w):
            S.add("dve", lambda e: e.reciprocal(out=out, in_=in_), r, w)

        def MM(out, pairs, r, w):
            def fn(e):
                n = len(pairs)
                ins = None
                for i, (l, rr) in enumerate(pairs):
                    ins = e.matmul(out, lhsT=l, rhs=rr, start=(i == 0), stop=(i == n - 1))
                return ins
            S.add("pe", fn, r, w).ninst = len(pairs)

        def MMseq(items, r, w):
            def fn(e):
                ins = None
                for out, pairs, st in items:
                    n = len(pairs)
                    for i, (l, rr) in enumerate(pairs):
                        ins = e.matmul(out, lhsT=l, rhs=rr, start=(st and i == 0), stop=(i == n - 1))
                return ins
            S.add("pe", fn, r, w).ninst = sum(len(p) for _, p, _ in items)

        def TR(items, r, w):
            def fn(e):
                ins = None
                for out, in_, ident in items:
                    ins = e.transpose(out=out, in_=in_, identity=ident)
                return ins
            S.add("pe", fn, r, w).ninst = len(items)

        def DMA(eng, key, pairs, r, w, slow=False):
            def fn(e, sem):
                for out, in_ in pairs:
                    if slow:
                        e.dma_start(out=out, in_=in_, allow_slow_non_contiguous=True).then_inc(sem, 16)
                    else:
                        e.dma_start(out=out, in_=in_).then_inc(sem, 16)
            S.add(eng, fn, r, w, dma_key=key, ndma=len(pairs))

        C = lambda n: ("c", n)

        MS(identb[:], 0.0, [C("identb")])
        S.add("pool", lambda e: e.affine_select(out=identb[:], in_=identb[:], pattern=[[-1, 128]], compare_op=ALU.not_equal,
                                                fill=1.0, base=0, channel_multiplier=1), [C("identb")], [C("identb")])
        MS(identf[:], 0.0, [C("identf")])
        S.add("pool", lambda e: e.affine_select(out=identf[:], in_=identf[:], pattern=[[-1, 128]], compare_op=ALU.not_equal,
                                                fill=1.0, base=0, channel_multiplier=1), [C("identf")], [C("identf")])
        MS(onesD[:], 1.0 / D, [C("onesD")])
        MS(onesH[:], 1.0 / 128, [C("onesH")])
        MS(c_eps[:], EPS, [C("eps")])
        MS(c_one[:], 1.0, [C("one")])
        MS(maskbd[:], 1.0, [C("maskbd")])
        S.add("pool", lambda e: e.affine_select(out=maskbd[:], in_=maskbd[:], pattern=[[0, 4], [1, 128]], compare_op=ALU.is_ge,
                                                fill=0.0, base=0, channel_multiplier=-1), [C("maskbd")], [C("maskbd")])
        MS(maskbd[0:64, :, 64:128], 0.0, [C("maskbd")])
        MS(rmask[:], 1.0, [C("rmask")])
        MS(rmask[:].rearrange("p (c l) -> p c l", l=64)[:, :, 0:1], 0.0, [C("rmask")])
        MS(corr[:], 1.0, [C("corr")])
        for g in range(4):
            win = 2 ** (g + 1)
            for t in range(win - 1):
                MS(corr[:, g, t:t + 1], float(win) / float(t + 1), [C("corr")])
        MS(khl[:], 0.0, [C("khl")])
        def vload(dst, src, name):
            DMA("sp", "setup", [(dst, src)], (), [C(name)], slow=True)
        def vload2(dst, src, nl, name):
            DMA("sp", "setup", [(dst[:, l, :], src[l].rearrange("(c p) -> p c", p=128)) for l in range(nl)], (), [C(name)], slow=True)
        vload2(c_gmix, d_gmix, 4, "gmix")
        vload2(c_gffn, d_gffn, 4, "gffn")
        vload2(c_gout, d_gout, 1, "gout")
        vload2(c_psc, d_pscale, 2, "psc")
        vload2(c_gn, d_gn, 2, "gn")
        vload2(c_lbl, d_lbl, 4, "lbl")
        vload2(c_cb, d_cb, 4, "cb")
        for l in range(4):
            vload2(c_cw[:, l], d_cw[l], 3, "cw")
        A(c_lb[:], c_lbl[:], AF.Exp, [C("lbl")], [C("lb")])
        ssum, k_ssum = tmp(NDC)
        TTo(ssum, c_lb[:, 0, :], c_lb[:, 1, :], ALU.add, [C("lb")], [k_ssum])
        TTo(ssum, ssum, c_lb[:, 2, :], ALU.add, [C("lb"), k_ssum], [k_ssum])
        TTo(ssum, ssum, c_lb[:, 3, :], ALU.add, [C("lb"), k_ssum], [k_ssum])
        RECIP(ssum, ssum, [k_ssum], [k_ssum])
        num, k_num = tmp(NDC)
        TTo(c_noml[:, 0, :], c_lb[:, 1, :], ssum, ALU.mult, [C("lb"), k_ssum], [C("oml")])
        TTo(num, c_lb[:, 1, :], c_lb[:, 2, :], ALU.add, [C("lb")], [k_num])
        TTo(num, num, c_lb[:, 3, :], ALU.add, [C("lb"), k_num], [k_num])
        TTo(c_noml[:, 1, :], num, ssum, ALU.mult, [k_num, k_ssum, C("oml")], [C("oml")])
        TS(c_oml[:], c_noml[:], -1.0, 1.0, ALU.mult, ALU.add, [C("oml")], [C("oml")])
        TS(c_noml[:], c_noml[:], -1.0, None, ALU.add, None, [C("oml")], [C("oml")])

        wslot = {"wup": 0, "wdn": 0, "wqfg": 0, "wout": 0}
        first_pass = [True]

        def scr(name, shape):
            return nc.dram_tensor(name, list(shape), BF16, kind="Internal").ap()

        scr_wup = scr("scr_wup", [4, NFC // 2, 128, NDC * 2 * 256])
        scr_wdn = scr("scr_wdn", [4, NDC, 128, NFC * 128])
        scr_wqfg = scr("scr_wqfg", [2, NH, 128, NDC * 3 * 128])
        scr_wi = scr("scr_wi", [2, 128, NDC * D])
        scr_wout = scr("scr_wout", [2, NDC, 128, NDC * 128])
        scr_poolw = scr("scr_poolw", [2, 128, 4 * 2 * 256])

        def wload(slot_key, slot2d, scr2d, scr_key, cast_pairs):
            if first_pass[0]:
                DMA("pool", slot_key, cast_pairs, (), [slot_key])
                sk = ("st",) + (slot_key if isinstance(slot_key, tuple) else (slot_key,))
                DMA("sp", sk, [(scr2d, slot2d)], [slot_key], [scr_key])
            else:
                DMA(os.environ.get("K_WQ", "pool"), slot_key, [(slot2d, scr2d)], [scr_key], [slot_key])

        def load_wup(i, grp):
            s = wslot["wup"] % 2
            wslot["wup"] += 1
            src = d_wup[i].rearrange("(c p) n -> p c n", p=128)
            c0 = grp * 256
            wload(("wup", s), wup[:, s].rearrange("p c k n -> p (c k n)"), scr_wup[i, grp], ("scr", "wup", i, grp),
                  [(wup[:, s, :, 0, :], src[:, :, c0:c0 + 256]), (wup[:, s, :, 1, :], src[:, :, DFF + c0:DFF + c0 + 256])])
            return s

        def load_wdn(i, ec):
            s = wslot["wdn"] % 2
            wslot["wdn"] += 1
            src = d_wdn[i].rearrange("(c p) n -> p c n", p=128)
            wload(("wdn", s), wdn[:, s].rearrange("p c n -> p (c n)"), scr_wdn[i, ec], ("scr", "wdn", i, ec),
                  [(wdn[:, s, :, :], src[:, :, ec * 128:(ec + 1) * 128])])
            return s

        def load_wqfg(j, h):
            s = wslot["wqfg"] % 2
            wslot["wqfg"] += 1
            src = d_win[j].rearrange("(c p) n -> p c n", p=128)
            prs = []
            for k, base in enumerate((0, D, 3 * D)):
                prs.append((wqfg[:, s, :, k, :], src[:, :, base + h * 128: base + (h + 1) * 128]))
            wload(("wqfg", s), wqfg[:, s].rearrange("p c k n -> p (c k n)"), scr_wqfg[j, h], ("scr", "wqfg", j, h), prs)
            return s

        def load_wi(j):
            src = d_win[j].rearrange("(c p) n -> p c n", p=128)
            wload("wi", wi[:, :, :].rearrange("p c n -> p (c n)"), scr_wi[j], ("scr", "wi", j),
                  [(wi[:, :, :], src[:, :, 2 * D:3 * D])])

        def load_wout(j, ec):
            s = wslot["wout"] % 2
            wslot["wout"] += 1
            src = d_wout[j].rearrange("(c p) n -> p c n", p=128)
            wload(("wout", s), wout[:, s].rearrange("p c n -> p (c n)"), scr_wout[j, ec], ("scr", "wout", j, ec),
                  [(wout[:, s, :, :], src[:, :, ec * 128:(ec + 1) * 128])])
            return s

        def load_poolw(j):
            src = d_poolw[j].rearrange("g (k p) n -> p g k n", p=128)
            wload("poolw", poolw[:, :, :, :].rearrange("p g k n -> p (g k n)"), scr_poolw[j], ("scr", "poolw", j),
                  [(poolw[:, :, :, :], src)])

        class Tile:
            pass

        tiles = []
        for k in range(NPT):
            t = Tile()
            t.kind, t.k, t.nseg, t.L, t.T = "p", k, 1, TT, TT
            t.first, t.last = (k == 0), (k == NPT - 1)
            tiles.append(t)
        if NS > 0:
            t = Tile()
            t.kind, t.k, t.nseg, t.L, t.T = "s", 0, NS, 64, NS * 64
            t.first, t.last = True, True
            tiles.append(t)

        def v3(ap2d, t):
            return ap2d.rearrange("p (s l) -> p s l", s=t.nseg)

        def norm(t, gvec, dest, gkey):
            T = t.T
            ps, kps = psb()
            for dc in range(NDC):
                A(sq[:, dc, 0:T], x[:, dc, 0:T], AF.Square, [("x", dc)], [("hid", dc)])
            MM(ps[:, 0:T], [(onesD[:], sq[:, dc, 0:T]) for dc in range(NDC)],
               [("hid", dc) for dc in range(NDC)] + [C("onesD")], [kps])
            A(rstd[:, 0:T], ps[:, 0:T], AF.Ln, [kps, C("eps")], ["rstd"], bias=c_eps[:])
            A(rstd[:, 0:T], rstd[:, 0:T], AF.Exp, ["rstd"], ["rstd"], scale=-0.5)
            for dc in range(NDC):
                dst, wk = dest(dc)
                STT(dst, v3(x[:, dc, 0:T], t), gvec[:, dc:dc + 1], v3(rstd[:, 0:T], t), ALU.mult, ALU.mult,
                    [("x", dc), "rstd", C(gkey)], [wk])

        def hn_dest(t):
            return lambda dc: (v3(hnT[:, dc, 0:t.T], t), ("hn", dc))

        def pool_layer(t, i):
            j = i // 2
            T, L, ns = t.T, t.L, t.nseg
            W = 16 + L
            load_poolw(j)

            def Ev(dc):
                return E[:, dc, 0:ns * W].rearrange("p (s w) -> p s w", s=ns)

            if t.kind == "p":
                if t.first:
                    MS(E[:, :, 0:16], 0.0, [("hn", dc) for dc in range(NDC)], eng="dve")
                else:
                    CP(E[:, :, 1:16], EhP[:, j, :, 1:16], [("EhP", j)], [("hn", dc) for dc in range(NDC)], eng="dve")
            else:
                for s in range(ns):
                    DMA("sp", ("phist", s), [(phist[:, s, dc, 1:16], d_spool[j, s].rearrange("t (c p) -> p c t", p=128)[:, dc, :]) for dc in range(NDC)],
                        (), [("phist", s)], slow=True)
                    CP(E[:, :, s * W + 1: s * W + 16], phist[:, s, :, 1:16], [("phist", s)],
                       [("hn", dc) for dc in range(NDC)], eng="dve")
            norm(t, c_gmix[:, i, :], lambda dc: (Ev(dc)[:, :, 16:16 + L], ("hn", dc)), "gmix")
            if t.last:
                for s in range(ns):
                    st = 0 if t.kind == "p" else 1 + s
                    for dc in range(NDC):
                        a = s * L + L - 16
                        STT(pout[:, st, j, dc, :], x[:, dc, a:a + 16], c_gmix[:, i, dc:dc + 1], rstd[:, a:a + 16],
                            ALU.mult, ALU.mult, [("x", dc), "rstd"], [("pout", st, j)])
                    dstd = (o_poolp[j] if t.kind == "p" else o_pools[j, s]).rearrange("t (c p) -> p c t", p=128)
                    DMA("sp", "outs", [(dstd[:, dc, :], pout[:, st, j, dc, 1:16]) for dc in range(NDC)], [("pout", st, j)], [], slow=True)
            for dc in range(NDC):
                g = dc // 2
                win = 2 ** (g + 1)
                cur = Ev(dc)
                ck = ("hn", dc)
                sh = 1
                for lev in range(g + 1):
                    lo = 2 * sh
                    nt, kn = tmp(ns * W)
                    nv = nt.rearrange("p (s w) -> p s w", s=ns)
                    TTo(nv[:, :, lo:W], cur[:, :, lo:W], cur[:, :, lo - sh:W - sh], ALU.add, [ck], [kn])
                    cur, ck = nv, kn
                    sh *= 2
                if t.kind == "p" and t.first:
                    TTo(cur[:, 0, 16:32], cur[:, 0, 16:32], corr[:, g, :], ALU.mult, [ck, C("corr")], [ck])
                STT(v3(sq[:, dc, 0:T], t), cur[:, :, 16:W], 1.0 / win, Ev(dc)[:, :, 16:W], ALU.mult, ALU.subtract,
                    [ck, ("hn", dc)], [("hid", dc)])
            if t.kind == "p" and not t.last:
                CP(EhP[:, j, :, 1:16], E[:, :, L + 1:L + 16], [("hn", dc) for dc in range(NDC)], [("EhP", j)], eng="act")
            for ec in range(NDC):
                g, eo = ec // 2, ec % 2
                ps, kps = psb()
                MM(ps[:, 0:T], [(poolw[:, g, kc, eo * 128:(eo + 1) * 128], sq[:, 2 * g + kc, 0:T]) for kc in range(2)],
                   [("hid", 2 * g), ("hid", 2 * g + 1), "poolw"], [kps])
                STT(x[:, ec, 0:T], ps[:, 0:T], c_psc[:, j, ec:ec + 1], x[:, ec, 0:T], ALU.mult, ALU.add,
                    [kps, ("x", ec), C("psc")], [("x", ec)])

        def ffn_layer(t, i):
            T, L, ns = t.T, t.L, t.nseg
            norm(t, c_gffn[:, i, :], hn_dest(t), "gffn")
            if t.kind == "p":
                gh = lambda fc: ghP[:, i, fc, :].rearrange("p (s j) -> p s j", s=1)
                ghk = lambda fc: ("ghP", i, fc)
                if t.first:
                    MS(ghP[:, i, :, :], 0.0, [ghk(fc) for fc in range(NFC)], eng="dve")
            else:
                gh = lambda fc: ghS[:, i, fc, 0:ns, :]
                ghk = lambda fc: ("ghS", i, fc)
                for s in range(ns):
                    DMA("sp", ("ghS", i), [(ghS[:, i, :, s, jj], d_sconv[i, s, jj].rearrange("(c p) -> p c", p=128)) for jj in range(2)],
                        (), [ghk(fc) for fc in range(NFC)], slow=True)
            W = 2 + L
            pend = []

            def ffn_stage2():
                fc_, acc_, kacc_, pv_, kv_ = pend.pop(0)
                sl, ksl = tmp(T)
                A(sl, acc_, AF.Silu, [kacc_], [ksl])
                TTo(hid[:, fc_, 0:T], sl, pv_[:, 0:T], ALU.mult, [ksl, kv_], [("hid", fc_)])

            for grp in range(NFC // 2):
                ws = load_wup(i, grp)
                for fl in range(2):
                    fc = 2 * grp + fl
                    pg, kg = psb()
                    pv, kv = psb()
                    rd = [("hn", dc) for dc in range(NDC)] + [("wup", ws)]
                    MM(pg[:, 0:T], [(wup[:, ws, dc, 0, fl * 128:(fl + 1) * 128], hnT[:, dc, 0:T]) for dc in range(NDC)], rd, [kg])
                    MM(pv[:, 0:T], [(wup[:, ws, dc, 1, fl * 128:(fl + 1) * 128], hnT[:, dc, 0:T]) for dc in range(NDC)], rd, [kv])
                    G, kG = tmp(ns * W)
                    Gv = G.rearrange("p (s w) -> p s w", s=ns)
                    acc, kacc = tmp(T)
                    accv = v3(acc, t)
                    A(Gv[:, :, 2:W], v3(pg[:, 0:T], t), AF.Copy, [kg], [kG])
                    CP(Gv[:, :, 0:2], gh(fc), [ghk(fc)], [kG], eng="dve")
                    A(acc, pg[:, 0:T], AF.Identity, [kg, C("cw"), C("cb")], [kacc],
                      bias=c_cb[:, i, fc:fc + 1], scale=c_cw[:, i, 2, fc:fc + 1])
                    STT(accv, Gv[:, :, 1:1 + L], c_cw[:, i, 1, fc:fc + 1], accv, ALU.mult, ALU.add, [kG, kacc], [kacc])
                    STT(accv, Gv[:, :, 0:L], c_cw[:, i, 0, fc:fc + 1], accv, ALU.mult, ALU.add, [kG, kacc], [kacc])
                    CP(gh(fc), Gv[:, :, L:L + 2], [kG], [ghk(fc)], eng="dve")
                    pend.append((fc, acc, kacc, pv, kv))
                    if len(pend) > 1:
                        ffn_stage2()
            while pend:
                ffn_stage2()
            if t.last:
                for s in range(ns):
                    dstd = (o_convp[i] if t.kind == "p" else o_convs[i, s])
                    src = ghP[:, i, :, :] if t.kind == "p" else ghS[:, i, :, s, :]
                    DMA("sp", "outs", [(dstd[jj].rearrange("(c p) -> p c", p=128), src[:, :, jj]) for jj in range(2)],
                        [ghk(fc) for fc in range(NFC)], [], slow=True)
            for ec in range(NDC):
                ws = load_wdn(i, ec)
                ps, kps = psb()
                MM(ps[:, 0:T], [(wdn[:, ws, fc, :], hid[:, fc, 0:T]) for fc in range(NFC)],
                   [("hid", fc) for fc in range(NFC)] + [("wdn", ws)], [kps])
                TTo(x[:, ec, 0:T], x[:, ec, 0:T], ps[:, 0:T], ALU.add, [kps, ("x", ec)], [("x", ec)])

        def hgrn_layer(t, i):
            j = i // 2
            T, L, ns = t.T, t.L, t.nseg
            nch = T // 64
            npair = T // 128
            norm(t, c_gmix[:, i, :], hn_dest(t), "gmix")
            load_wi(j)
            if t.kind == "p":
                sid = [j] * nch
            else:
                sid = [0 if s == 0 else 2 for s in range(ns)] if j == 0 else [1 if s == 0 else 2 for s in range(ns)]
            if t.kind == "p":
                if t.first:
                    MS(S32[j][:], 0.0, [("S", j, h) for h in range(NH)], eng="dve")
                    MS(Scar[j][:], 0.0, [("Sc", j, h) for h in range(NH)], eng="dve")
            else:
                for s in range(ns):
                    b = sid[s]
                    DMA("sp", ("Sld", b), [(S32[b][:], d_shg[j, s].rearrange("h k v -> k h v"))], (),
                        [("S", b, h) for h in range(NH)])
                    CP(Scar[b][:], S32[b][:], [("S", b, h) for h in range(NH)], [("Sc", b, h) for h in range(NH)], eng="act")
            for tb in range(npair):
                for vh in range(2):
                    ps, kps = psb()
                    MM(ps[:, :], [(hnT[:, dc, tb * 128:(tb + 1) * 128], wi[:, dc, vh * 512:(vh + 1) * 512]) for dc in range(NDC)],
                       [("hn", dc) for dc in range(NDC)] + ["wi"], [kps])
                    CP(V[:, tb, vh * 512:(vh + 1) * 512], ps[:, :], [kps], [("hid", 2 * tb + vh)], eng="act")

            st = {}

            def partA1_pe(h):
                ws = load_wqfg(j, h)
                pq, kq = psb()
                pf, kf = psb()
                pgt, kgt = psb()
                rd = [("hn", dc) for dc in range(NDC)] + [("wqfg", ws)]
                for k, (pp, kk) in enumerate(((pq, kq), (pf, kf), (pgt, kgt))):
                    MM(pp[:, 0:T], [(wqfg[:, ws, dc, k, :], hnT[:, dc, 0:T]) for dc in range(NDC)], rd, [kk])
                st[h] = dict(pq=pq, kq=kq, pf=pf, kf=kf, pgt=pgt, kgt=kgt)

            def partA1_ew(h):
                d = st[h]
                pq, kq, pf, kf, pgt, kgt = d["pq"], d["kq"], d["pf"], d["kf"], d["pgt"], d["kgt"]
                s2, s4 = h % 2, h % 4
                sn, ksn = tmp(T)
                A(sn, pf[:, 0:T], AF.Sigmoid, [kf], [ksn], scale=-1.0)
                sgq, ksgq = tmp(T)
                A(sgq, pq[:, 0:T], AF.Sigmoid, [kq], [ksgq])
                sgg, ksgg = tmp(T)
                A(sgg, pgt[:, 0:T], AF.Sigmoid, [kgt], [ksgg])
                q, kq2 = tmp(T)
                TTo(q, sgq, pq[:, 0:T], ALU.mult, [ksgq, kq], [kq2])
                gate, kgate = gateb[:, s4, 0:T], ("gate", s4)
                TTo(gate, sgg, pgt[:, 0:T], ALU.mult, [ksgg, kgt], [kgate])
                gl, kgl = tmp(T)
                A(gl, sn, AF.Ln, [ksn, C("oml"), C("one")], [kgl], bias=c_one[:], scale=c_noml[:, j, h:h + 1])
                b, kb = tmp(T)
                S.add("dve", lambda e: e.tensor_tensor_scan(out=b, data0=rmask[:, 0:T], data1=gl, initial=0.0,
                                                            op0=ALU.mult, op1=ALU.add), [kgl, C("rmask")], [kb])
                eb, keb = ebb[:, s2, 0:T], ("eb", s2)
                A(eb, b, AF.Exp, [kb], [keb])
                enb, kenb = tmp(T)
                A(enb, b, AF.Exp, [kb], [kenb], scale=-1.0)
                qt, kqt = qtb[:, s4, 0:T], ("qt", s4)
                TTo(qt, q, eb, ALU.mult, [kq2, keb], [kqt])
                kt, kkt = ktb[:, s2, 0:T], ("kt", s2)
                STT(kt, sn, c_oml[:, j, h:h + 1], enb, ALU.mult, ALU.mult, [ksn, kenb, C("oml")], [kkt])
                for c in range(nch):
                    p_, hf = c // 2, c % 2
                    TS(khl[:, s2, hf, p_, hf * 64:(hf + 1) * 64], kt[:, c * 64:(c + 1) * 64], eb[:, c * 64 + 63:c * 64 + 64], None,
                       ALU.mult, None, [kkt, keb, C("khl")], [("khl", s2)])
                st[h] = dict(gate=gate, kgate=kgate, qt=qt, kqt=kqt, kt=kt, kkt=kkt, eb=eb, keb=keb)

            def partA2(h):
                d = st[h]
                kt, kkt, eb, keb, qt, kqt = d["kt"], d["kkt"], d["eb"], d["keb"], d["qt"], d["kqt"]
                s2, s3 = h % 2, h % 3
                pT, kpT = psb(BF16)
                TR([(pT[:, (2 * p_ + hf) * 128:(2 * p_ + hf + 1) * 128], khl[:, s2, hf, p_, :], identb[:])
                    for p_ in range(npair) for hf in range(2)], [("khl", s2), C("identb")], [kpT])
                kT, kkT = tmp(1024, BF16)
                CP(kT[:, 0:nch * 128], pT[:, 0:nch * 128], [kpT], [kkT], eng="act")
                pA, kpA = psb()
                MMseq([(pA[:, p_ * 128:(p_ + 1) * 128], [(kt[:, p_ * 128:(p_ + 1) * 128], qt[:, p_ * 128:(p_ + 1) * 128])], True)
                       for p_ in range(npair)], [kkt, kqt], [kpA])
                TTo(ATs[:, s3, 0:npair, :], pA[:, 0:npair * 128].rearrange("p (a b) -> p a b", b=128), maskbd[:, 0:npair, :], ALU.mult,
                    [kpA, C("maskbd")], [("ATs", s3)])
                pU = []
                for u0 in range(0, nch, 4):
                    pu, kpu = psb()
                    n = min(4, nch - u0)
                    MMseq([(pu[:, (c - u0) * 128:(c - u0 + 1) * 128], [(kT[:, c * 128:(c + 1) * 128], V[:, c // 2, h * 128:(h + 1) * 128])], True)
                           for c in range(u0, u0 + n)], [kkT] + [("hid", pg_) for pg_ in range(2 * (u0 // 2), 2 * ((u0 + n - 1) // 2) + 2)], [kpu])
                    pU.append((pu, kpu))
                for c in range(nch):
                    b_ = sid[c]
                    pu, kpu = pU[c // 4]
                    STT(S32[b_][:, h, :], S32[b_][:, h, :], eb[:, c * 64 + 63:c * 64 + 64], pu[:, (c % 4) * 128:(c % 4 + 1) * 128],
                        ALU.mult, ALU.add, [("S", b_, h), keb, kpu], [("S", b_, h)])
                    lastc = (c == nch - 1) or (t.kind == "s")
                    if lastc:
                        CP(Scar2[:, s3, b_, :], S32[b_][:, h, :], [("S", b_, h)], [("Sc2", s3, b_)], eng="act")
                    else:
                        CP(Sv[:, s3, c, :], S32[b_][:, h, :], [("S", b_, h)], [("Sv", s3, c)], eng="act")

            def partB1(h):
                d = st[h]
                qt, kqt = d["qt"], d["kqt"]
                s3 = h % 3
                po, kpo = psb()
                items = []
                rd = [("ATs", s3), kqt]
                for c in range(nch):
                    p_ = c // 2
                    if c % 2 == 0:
                        items.append((po[:, p_ * 128:(p_ + 1) * 128], [(V[:, p_, h * 128:(h + 1) * 128], ATs[:, s3, p_, :])], True))
                        rd += [("hid", 2 * p_), ("hid", 2 * p_ + 1)]
                    b_ = sid[c]
                    if c == 0 or t.kind == "s":
                        lhs = Scar[b_][:, h, :]
                        rd.append(("Sc", b_, h))
                    else:
                        lhs = Sv[:, s3, c - 1, :]
                        rd.append(("Sv", s3, c - 1))
                    items.append((po[:, c * 64:(c + 1) * 64], [(lhs, qt[:, c * 64:(c + 1) * 64])], False))
                MMseq(items, rd, [kpo])
                osq, kosq = tmp(T, BF16)
                A(osq, po[:, 0:T], AF.Square, [kpo], [kosq])
                d["po"], d["kpo"], d["osq"], d["kosq"] = po, kpo, osq, kosq

            def partB2(h):
                d = st.pop(h)
                po, kpo, osq, kosq, gate, kgate = d["po"], d["kpo"], d["osq"], d["kosq"], d["gate"], d["kgate"]
                s3 = h % 3
                pss, kpss = psb()
                MM(pss[:, 0:T], [(onesH[:], osq)], [kosq, C("onesH")], [kpss])
                rs, krs = tmp(T)
                A(rs, pss[:, 0:T], AF.Ln, [kpss, C("eps")], [krs], bias=c_eps[:])
                A(rs, rs, AF.Exp, [krs], [krs], scale=-0.5)
                t1, kt1 = tmp(T)
                STT(t1, po[:, 0:T], c_gn[:, j, h:h + 1], rs, ALU.mult, ALU.mult, [kpo, krs, C("gn")], [kt1])
                TTo(ofin[:, h, 0:T], t1, gate, ALU.mult, [kt1, kgate], [("hid", 8 + h)])
                for b_ in sorted(set(sid)):
                    CP(Scar[b_][:, h, :], Scar2[:, s3, b_, :], [("Sc2", s3, b_)], [("Sc", b_, h)], eng="act")

            for k in range(NH + 3):
                if 0 <= k - 3 < NH:
                    partB1(k - 3)
                if k < NH:
                    partA1_pe(k)
                if 0 <= k - 3 < NH:
                    partB2(k - 3)
                if 0 <= k - 1 < NH:
                    partA2(k - 1)
                if k < NH:
                    partA1_ew(k)
            if t.last:
                for s in range(ns):
                    b_ = sid[0] if t.kind == "p" else sid[s]
                    dstd = (o_hgp[j] if t.kind == "p" else o_hgs[j, s]).rearrange("h k v -> k h v")
                    DMA("sp", "outs", [(dstd, S32[b_][:])], [("S", b_, h) for h in range(NH)], [])
            for ec in range(NDC):
                ws = load_wout(j, ec)
                ps, kps = psb()
                MM(ps[:, 0:T], [(wout[:, ws, dc, :], ofin[:, dc, 0:T]) for dc in range(NDC)],
                   [("hid", 8 + dc) for dc in range(NDC)] + [("wout", ws)], [kps])
                TTo(x[:, ec, 0:T], x[:, ec, 0:T], ps[:, 0:T], ALU.add, [kps, ("x", ec)], [("x", ec)])


        xin_ctr = [0]

        def load_x(t):
            ntb = t.T // 128
            src = d_xp if t.kind == "p" else d_xs
            r0 = t.k * TT if t.kind == "p" else 0
            for tb in range(ntb):
                s = xin_ctr[0] % 2
                xin_ctr[0] += 1
                DMA("sp", ("io", s), [(xin[:, s, :], src[r0 + tb * 128: r0 + (tb + 1) * 128, :])], (), [("io", s)])
                for hb in range(2):
                    ps, kps = psb()
                    TR([(ps[:, q_ * 128:(q_ + 1) * 128], xin[:, s, (hb * 4 + q_) * 128:(hb * 4 + q_ + 1) * 128], identf[:]) for q_ in range(4)],
                       [("io", s), C("identf")], [kps])
                    CP(x[:, hb * 4:hb * 4 + 4, tb * 128:(tb + 1) * 128], ps[:, :].rearrange("p (a b) -> p a b", b=128), [kps],
                       [("x", hb * 4 + q_) for q_ in range(4)], eng="act")

        yo_ctr = [0]

        def store_y(t):
            T = t.T
            ntb = T // 128
            norm(t, c_gout[:, 0, :], lambda dc: (v3(x[:, dc, 0:T], t), ("x", dc)), "gout")
            dst = o_yp if t.kind == "p" else o_ys
            r0 = t.k * TT if t.kind == "p" else 0
            for tb in range(ntb):
                s = xin_ctr[0] % 2
                xin_ctr[0] += 1
                for hb in range(2):
                    ps, kps = psb()
                    TR([(ps[:, q_ * 128:(q_ + 1) * 128], x[:, hb * 4 + q_, tb * 128:(tb + 1) * 128], identf[:]) for q_ in range(4)],
                       [("x", hb * 4 + q_) for q_ in range(4)] + [C("identf")], [kps])
                    CP(yout[:, s, hb * 512:(hb + 1) * 512], ps[:, :], [kps], [("io", s)], eng="act")
                DMA("sp", ("io", s), [(dst[r0 + tb * 128: r0 + (tb + 1) * 128, :], yout[:, s, :])], [("io", s)], [])

        for ti, t in enumerate(tiles):
            first_pass[0] = (ti == 0)
            load_x(t)
            for i in LY:
                if i % 2 == 0:
                    pool_layer(t, i)
                else:
                    hgrn_layer(t, i)
                ffn_layer(t, i)
            store_y(t)

        S.finalize()
        esems = {e: es.enter_context(nc.semaphore("se_" + e)) for e in Sched.ENGS}
        dsems = {}
        for n, k in enumerate(S.dma_cnt):
            dsems[k] = es.enter_context(nc.semaphore(f"sd_{n}"))
        out_keys = ["outs"] + [("io", s) for s in range(2)]
        with nc.Block() as block:
            @block.tensor
            def _(e):
                S.emit("pe", e, esems, dsems)

            @block.scalar
            def _(e):
                S.emit("act", e, esems, dsems)

            @block.vector
            def _(e):
                S.emit("dve", e, esems, dsems)

            @block.gpsimd
            def _(e):
                S.emit("pool", e, esems, dsems)

            @block.sync
            def _(e):
                S.emit("sp", e, esems, dsems)
                for k in out_keys:
                    if k in S.dma_cnt:
                        e.wait_ge(dsems[k], 16 * S.dma_cnt[k])
    nc._sched = S
    return nc


_NC_CACHE = {}


def run_cores(cfg, in_maps):
    key = (cfg.layers, cfg.npt, cfg.ns)
    if key not in _NC_CACHE:
        _NC_CACHE[key] = build(cfg)
    nc = _NC_CACHE[key]
    return run_bass_kernel_spmd(nc, in_maps, core_ids=list(range(len(in_maps))))


def kernel(x_prompt, x_sample, state_pool, state_hgrn, state_ffn_conv, norm_mix_g, pool_w, pool_scale,
           hgrn_w_in, hgrn_lb_logits, hgrn_norm_g, hgrn_w_out, norm_ffn_g, ffn_w_up, ffn_conv_w,
           ffn_conv_b, ffn_w_down, norm_out_g):
    f = lambda a: np.ascontiguousarray(np.asarray(a, dtype=np.float32))
    x_prompt, x_sample = f(x_prompt), f(x_sample)
    state_pool, state_hgrn, state_ffn_conv = f(state_pool), f(state_hgrn), f(state_ffn_conv)
    BP, SEQ, _ = x_prompt.shape
    NSB = x_sample.shape[0]
    ncores = 8
    nsc = NSB // ncores
    cfg = Cfg(layers=(0, 1, 2, 3), npt=SEQ // TT, ns=nsc)
    shared = {
        "norm_mix_g": f(norm_mix_g), "pool_w": f(pool_w), "pool_scale": f(pool_scale), "hgrn_w_in": f(hgrn_w_in),
        "hgrn_lb_logits": f(hgrn_lb_logits), "hgrn_norm_g": f(hgrn_norm_g), "hgrn_w_out": f(hgrn_w_out),
        "norm_ffn_g": f(norm_ffn_g), "ffn_w_up": f(ffn_w_up), "ffn_conv_w": f(ffn_conv_w), "ffn_conv_b": f(ffn_conv_b),
        "ffn_w_down": f(ffn_w_down), "norm_out_g": f(norm_out_g).reshape(1, D),
    }
    in_maps = []
    for c in range(ncores):
        sl = slice(c * nsc, (c + 1) * nsc)
        m = dict(shared)
        m["xp"] = x_prompt[c % BP]
        m["xs"] = x_sample[sl].reshape(nsc * 64, D)
        m["s_pool"] = np.ascontiguousarray(state_pool[:, sl])
        m["s_hgrn"] = np.ascontiguousarray(state_hgrn[:, sl])
        m["s_conv"] = np.ascontiguousarray(state_ffn_conv[:, sl])
        in_maps.append(m)
    res = run_cores(cfg, in_maps).results
    y_prompt = np.stack([res[b]["yp"] for b in range(BP)])
    y_sample = np.concatenate([res[c]["ys"].reshape(nsc, 64, D) for c in range(ncores)])
    pool_p = np.stack([res[b]["npool_p"] for b in range(BP)], axis=1)
    pool_s = np.concatenate([res[c]["npool_s"] for c in range(ncores)], axis=1)
    hg_p = np.stack([res[b]["nhgrn_p"] for b in range(BP)], axis=1)
    hg_s = np.concatenate([res[c]["nhgrn_s"] for c in range(ncores)], axis=1)
    cv_p = np.stack([res[b]["nconv_p"] for b in range(BP)], axis=1)
    cv_s = np.concatenate([res[c]["nconv_s"] for c in range(ncores)], axis=1)
    return tuple(np.ascontiguousarray(a, dtype=np.float32) for a in (y_prompt, y_sample, pool_p, pool_s, hg_p, hg_s, cv_p, cv_s))
```

```python
import sys
from contextlib import ExitStack
import numpy as np
import concourse.bass as bass
import concourse.mybir as mybir
from concourse.bass_utils import run_bass_kernel_spmd

F32 = mybir.dt.float32
BF16 = mybir.dt.bfloat16
AF = mybir.ActivationFunctionType
ALU = mybir.AluOpType

D = 1024
NDC = 8
DFF = 2816
NFC = 22
NH = 8
EPS = 1e-6
TT = 512
TSLOT = 528
NTMP = 11
import os
SAME_ENG_SYNC = os.environ.get("K_SES", "1") == "1"


class Op:
    __slots__ = ("eng", "fn", "deps", "dma_key", "ndma", "idx", "ms", "dma_val", "is_ms", "label", "ninst")

    def __init__(self, eng, fn, dma_key, ndma):
        self.eng = eng
        self.fn = fn
        self.dma_key = dma_key
        self.ndma = ndma
        self.deps = None
        self.ms = 0
        self.is_ms = False
        self.dma_val = 0


class Sched:
    ENGS = ("pe", "act", "dve", "pool", "sp")

    def __init__(self):
        self.streams = {e: [] for e in self.ENGS}
        self.last_w = {}
        self.readers = {}
        self.dma_cnt = {}
        self.gen = {}

    def fresh(self, kind, slot):
        g = self.gen.get((kind, slot), 0)
        old = (kind, slot, g)
        new = (kind, slot, g + 1)
        self.gen[(kind, slot)] = g + 1
        if old in self.last_w:
            self.last_w[new] = self.last_w.pop(old)
        if old in self.readers:
            self.readers[new] = self.readers.pop(old)
        return new

    def _check(self, r):
        if len(r) == 3 and r[0] in ("t", "ps"):
            assert self.gen.get((r[0], r[1]), 0) == r[2], f"stale resource {r}"

    def add(self, eng, fn, reads=(), writes=(), dma_key=None, ndma=1):
        op = Op(eng, fn, dma_key, ndma)
        try:
            f = sys._getframe(2)
            g = f.f_back
            op.label = f"{g.f_code.co_name}:{g.f_lineno}" if g is not None else f.f_code.co_name
        except Exception:
            op.label = "?"
        op.ninst = 1
        deps = {}

        def dep(o):
            if o is None or o is op:
                return
            k = o.dma_key if o.dma_key is not None else o.eng
            p = deps.get(k)
            if p is None or self._later(o, p):
                deps[k] = o

        for r in reads:
            self._check(r)
            dep(self.last_w.get(r))
        for r in writes:
            self._check(r)
            dep(self.last_w.get(r))
            for o in self.readers.get(r, {}).values():
                dep(o)
        op.idx = len(self.streams[eng])
        self.streams[eng].append(op)
        if dma_key is not None:
            c = self.dma_cnt.get(dma_key, 0) + ndma
            self.dma_cnt[dma_key] = c
            op.dma_val = 16 * c
        for r in reads:
            d = self.readers.setdefault(r, {})
            k = dma_key if dma_key is not None else eng
            d[k] = op
        for r in writes:
            self.last_w[r] = op
            self.readers[r] = {}
        op.deps = list(deps.values())
        return op

    @staticmethod
    def _later(a, b):
        if a.dma_key is not None:
            return a.dma_val > b.dma_val
        return a.idx > b.idx

    def finalize(self):
        for e in self.ENGS:
            for op in self.streams[e]:
                for d in op.deps:
                    if d.dma_key is None:
                        if d.eng == op.eng and (d.eng == "pe" or not SAME_ENG_SYNC):
                            continue
                        d.is_ms = True
        for e in self.ENGS:
            c = 0
            for op in self.streams[e]:
                if op.is_ms:
                    c += 1
                    op.ms = c

    def emit(self, eng_name, eng, esems, dsems):
        waited = {}
        for op in self.streams[eng_name]:
            for d in op.deps:
                if d.dma_key is not None:
                    key, val, sem = ("d", d.dma_key), d.dma_val, dsems[d.dma_key]
                else:
                    if d.eng == op.eng and (d.eng == "pe" or not SAME_ENG_SYNC):
                        continue
                    key, val, sem = ("e", d.eng), d.ms, esems[d.eng]
                if waited.get(key, 0) >= val:
                    continue
                waited[key] = val
                eng.wait_ge(sem, val)
            if op.dma_key is not None:
                op.fn(eng, dsems[op.dma_key])
            else:
                ins = op.fn(eng)
                if op.is_ms:
                    ins.then_inc(esems[eng_name], 1)


class Cfg:
    def __init__(self, layers=(0, 1, 2, 3), npt=16, ns=2):
        self.layers = tuple(layers)
        self.npt = npt
        self.ns = ns


def build(cfg):
    nc = bass.Bass("TRN2", target_bir_lowering=False)
    LY = cfg.layers
    NL = len(LY)
    NPT = cfg.npt
    NS = cfg.ns
    SEQP = NPT * TT
    pool_layers = [i for i in LY if i % 2 == 0]
    hg_layers = [i for i in LY if i % 2 == 1]
    NPL = max(1, len(pool_layers))
    NHL = max(1, len(hg_layers))

    def dt(name, shape, kind):
        return nc.dram_tensor(name, list(shape), F32, kind=kind).ap()

    I, O = "ExternalInput", "ExternalOutput"
    d_xp = dt("xp", [max(SEQP, 1), D], I)
    d_xs = dt("xs", [max(NS, 1) * 64, D], I)
    d_spool = dt("s_pool", [2, max(NS, 1), 15, D], I)
    d_shg = dt("s_hgrn", [2, max(NS, 1), NH, 128, 128], I)
    d_sconv = dt("s_conv", [4, max(NS, 1), 2, DFF], I)
    d_gmix = dt("norm_mix_g", [4, D], I)
    d_poolw = dt("pool_w", [2, 4, 256, 256], I)
    d_pscale = dt("pool_scale", [2, D], I)
    d_win = dt("hgrn_w_in", [2, D, 4 * D], I)
    d_lbl = dt("hgrn_lb_logits", [4, D], I)
    d_gn = dt("hgrn_norm_g", [2, D], I)
    d_wout = dt("hgrn_w_out", [2, D, D], I)
    d_gffn = dt("norm_ffn_g", [4, D], I)
    d_wup = dt("ffn_w_up", [4, D, 2 * DFF], I)
    d_cw = dt("ffn_conv_w", [4, 3, DFF], I)
    d_cb = dt("ffn_conv_b", [4, DFF], I)
    d_wdn = dt("ffn_w_down", [4, DFF, D], I)
    d_gout = dt("norm_out_g", [1, D], I)
    o_yp = dt("yp", [max(SEQP, 1), D], O)
    o_ys = dt("ys", [max(NS, 1) * 64, D], O)
    o_poolp = dt("npool_p", [2, 15, D], O)
    o_pools = dt("npool_s", [2, max(NS, 1), 15, D], O)
    o_hgp = dt("nhgrn_p", [2, NH, 128, 128], O)
    o_hgs = dt("nhgrn_s", [2, max(NS, 1), NH, 128, 128], O)
    o_convp = dt("nconv_p", [4, 2, DFF], O)
    o_convs = dt("nconv_s", [4, max(NS, 1), 2, DFF], O)

    S = Sched()
    es = ExitStack()
    with es:
        def sb(name, shape, dtype=F32):
            return es.enter_context(nc.sbuf_tensor(name, list(shape), dtype))

        x = sb("x", [128, NDC, TT])
        E = sb("E", [128, NDC, 16 + TT], BF16)
        hnT = E
        rstd = sb("rstd", [128, TT])
        hid = sb("hid", [128, NFC, TT], BF16)
        sq = hid
        V = hid[:, 0:8, :].rearrange("p (a b) t -> p a (b t)", b=2)
        ofin = hid[:, 8:16, :]
        gateb = sb("gateb", [128, 4, TT], BF16)
        qtb = sb("qtb", [128, 4, TT], BF16)
        ktb = sb("ktb", [128, 2, TT], BF16)
        ebb = sb("ebb", [128, 2, TT])
        temps = sb("temps", [128, NTMP * TSLOT])
        wup = sb("wup", [128, 3, NDC, 2, 128], BF16)
        wdn = sb("wdn", [128, 3, NFC, 128], BF16)
        wqfg = sb("wqfg", [128, 2, NDC, 3, 128], BF16)
        wi = sb("wi", [128, NDC, D], BF16)
        wout = sb("wout", [128, 2, NDC, 128], BF16)
        poolw = sb("poolw", [128, 4, 2, 256], BF16)
        S32 = [sb(f"S32_{k}", [128, NH, 128]) for k in range(3)]
        Scar = [sb(f"Scar_{k}", [128, NH, 128], BF16) for k in range(3)]
        Scar2 = sb("Scar2", [128, 3, 3, 128], BF16)
        Sv = sb("Sv", [128, 3, 8, 128], BF16)
        khl = sb("khl", [128, 2, 2, 4, 128], BF16)
        ATs = sb("ATs", [128, 3, 4, 128], BF16)
        xin = sb("xin", [128, 2, D])
        yout = xin
        identb = sb("identb", [128, 128], BF16)
        identf = sb("identf", [128, 128])
        onesD = sb("onesD", [128, 128], BF16)
        onesH = sb("onesH", [128, 128], BF16)
        maskbd = sb("maskbd", [128, 4, 128], BF16)
        rmask = sb("rmask", [128, TT])
        corr = sb("corr", [128, 4, 16])
        c_gmix = sb("c_gmix", [128, 4, NDC])
        c_gffn = sb("c_gffn", [128, 4, NDC])
        c_gout = sb("c_gout", [128, 1, NDC])
        c_psc = sb("c_psc", [128, 2, NDC])
        c_gn = sb("c_gn", [128, 2, NDC])
        c_lbl = sb("c_lbl", [128, 4, NDC])
        c_lb = sb("c_lb", [128, 4, NDC])
        c_oml = sb("c_oml", [128, 2, NDC])
        c_noml = sb("c_noml", [128, 2, NDC])
        c_cw = sb("c_cw", [128, 4, 3, NFC])
        c_cb = sb("c_cb", [128, 4, NFC])
        c_eps = sb("c_eps", [128, 1])
        c_one = sb("c_one", [128, 1])
        ghP = sb("ghP", [128, 4, NFC, 2])
        ghS = sb("ghS", [128, 4, NFC, 2, 2])
        EhP = sb("EhP", [128, 2, NDC, 16], BF16)
        phist = sb("phist", [128, 2, NDC, 16])
        pout = sb("pout", [128, 3, 2, NDC, 16])
        psum = es.enter_context(nc.psum_tensor("psum", [128, 8, 512], F32))

        tmp_ctr = [0]
        ps_ctr = [0]

        def tmp(n=TT, dtype=F32):
            s = tmp_ctr[0] % NTMP
            tmp_ctr[0] += 1
            key = S.fresh("t", s)
            ap = temps[:, s * TSLOT:(s + 1) * TSLOT]
            if dtype == BF16:
                ap = ap.bitcast(BF16)
            return ap[:, 0:n], key

        def tmp_at(s, n=TT, dtype=F32):
            key = S.fresh("t", s)
            ap = temps[:, s * TSLOT:(s + 1) * TSLOT]
            if dtype == BF16:
                ap = ap.bitcast(BF16)
            return ap[:, 0:n], key

        def psb_at(b, dtype=F32):
            key = S.fresh("ps", b)
            ap = psum[:, b, :]
            if dtype == BF16:
                ap = ap.bitcast(BF16)
            return ap, key

        def psb(dtype=F32):
            b = ps_ctr[0] % 8
            ps_ctr[0] += 1
            key = S.fresh("ps", b)
            ap = psum[:, b, :]
            if dtype == BF16:
                ap = ap.bitcast(BF16)
            return ap, key

        def A(out, in_, func, r, w, bias=None, scale=None):
            kw = {}
            if bias is not None:
                kw["bias"] = bias
            if scale is not None:
                kw["scale"] = scale
            S.add("act", lambda e: e.activation(out=out, in_=in_, func=func, **kw), r, w)

        def TS(out, in0, s1, s2, op0, op1, r, w, eng="dve"):
            if s2 is None:
                S.add(eng, lambda e: e.tensor_scalar(out=out, in0=in0, scalar1=s1, scalar2=None, op0=op0), r, w)
            else:
                S.add(eng, lambda e: e.tensor_scalar(out=out, in0=in0, scalar1=s1, scalar2=s2, op0=op0, op1=op1), r, w)

        def STT(out, in0, sc, in1, op0, op1, r, w):
            S.add("dve", lambda e: e.scalar_tensor_tensor(out=out, in0=in0, scalar=sc, in1=in1, op0=op0, op1=op1), r, w)

        def TTo(out, in0, in1, op, r, w, eng="dve"):
            S.add(eng, lambda e: e.tensor_tensor(out=out, in0=in0, in1=in1, op=op), r, w)

        def CP(out, in_, r, w, eng="dve"):
            if eng == "act":
                S.add(eng, lambda e: e.activation(out=out, in_=in_, func=AF.Copy), r, w)
            else:
                S.add(eng, lambda e: e.tensor_copy(out=out, in_=in_), r, w)

        def MS(ap, val, w, eng="pool"):
            S.add(eng, lambda e: e.memset(ap, val), (), w)

        def RECIP(out, in_, r, w):
            S.add("dve", lambda e: e.reciprocal(out=out, in_=in_), r, w)

        def MM(out, pairs, r, w):
            def fn(e):
                n = len(pairs)
                ins = None
                for i, (l, rr) in enumerate(pairs):
                    ins = e.matmul(out, lhsT=l, rhs=rr, start=(i == 0), stop=(i == n - 1))
                return ins
            S.add("pe", fn, r, w).ninst = len(pairs)

        def MMseq(items, r, w):
            def fn(e):
                ins = None
                for out, pairs, st in items:
                    n = len(pairs)
                    for i, (l, rr) in enumerate(pairs):
                        ins = e.matmul(out, lhsT=l, rhs=rr, start=(st and i == 0), stop=(i == n - 1))
                return ins
            S.add("pe", fn, r, w).ninst = sum(len(p) for _, p, _ in items)

        def TR(items, r, w):
            def fn(e):
                ins = None
                for out, in_, ident in items:
                    ins = e.transpose(out=out, in_=in_, identity=ident)
                return ins
            S.add("pe", fn, r, w).ninst = len(items)

        def DMA(eng, key, pairs, r, w, slow=False):
            def fn(e, sem):
                for out, in_ in pairs:
                    if slow:
                        e.dma_start(out=out, in_=in_, allow_slow_non_contiguous=True).then_inc(sem, 16)
                    else:
                        e.dma_start(out=out, in_=in_).then_inc(sem, 16)
            S.add(eng, fn, r, w, dma_key=key, ndma=len(pairs))

        C = lambda n: ("c", n)

        MS(identb[:], 0.0, [C("identb")])
        S.add("pool", lambda e: e.affine_select(out=identb[:], in_=identb[:], pattern=[[-1, 128]], compare_op=ALU.not_equal,
                                                fill=1.0, base=0, channel_multiplier=1), [C("identb")], [C("identb")])
        MS(identf[:], 0.0, [C("identf")])
        S.add("pool", lambda e: e.affine_select(out=identf[:], in_=identf[:], pattern=[[-1, 128]], compare_op=ALU.not_equal,
                                                fill=1.0, base=0, channel_multiplier=1), [C("identf")], [C("identf")])
        MS(onesD[:], 1.0 / D, [C("onesD")])
        MS(onesH[:], 1.0 / 128, [C("onesH")])
        MS(c_eps[:], EPS, [C("eps")])
        MS(c_one[:], 1.0, [C("one")])
        MS(maskbd[:], 1.0, [C("maskbd")])
        S.add("pool", lambda e: e.affine_select(out=maskbd[:], in_=maskbd[:], pattern=[[0, 4], [1, 128]], compare_op=ALU.is_ge,
                                                fill=0.0, base=0, channel_multiplier=-1), [C("maskbd")], [C("maskbd")])
        MS(maskbd[0:64, :, 64:128], 0.0, [C("maskbd")])
        MS(rmask[:], 1.0, [C("rmask")])
        MS(rmask[:].rearrange("p (c l) -> p c l", l=64)[:, :, 0:1], 0.0, [C("rmask")])
        MS(corr[:], 1.0, [C("corr")])
        for g in range(4):
            win = 2 ** (g + 1)
            for t in range(win - 1):
                MS(corr[:, g, t:t + 1], float(win) / float(t + 1), [C("corr")])
        MS(khl[:], 0.0, [C("khl")])
        def vload(dst, src, name):
            DMA("sp", "setup", [(dst, src)], (), [C(name)], slow=True)
        def vload2(dst, src, nl, name):
            DMA("sp", "setup", [(dst[:, l, :], src[l].rearrange("(c p) -> p c", p=128)) for l in range(nl)], (), [C(name)], slow=True)
        vload2(c_gmix, d_gmix, 4, "gmix")
        vload2(c_gffn, d_gffn, 4, "gffn")
        vload2(c_gout, d_gout, 1, "gout")
        vload2(c_psc, d_pscale, 2, "psc")
        vload2(c_gn, d_gn, 2, "gn")
        vload2(c_lbl, d_lbl, 4, "lbl")
        vload2(c_cb, d_cb, 4, "cb")
        for l in range(4):
            vload2(c_cw[:, l], d_cw[l], 3, "cw")
        A(c_lb[:], c_lbl[:], AF.Exp, [C("lbl")], [C("lb")])
        ssum, k_ssum = tmp(NDC)
        TTo(ssum, c_lb[:, 0, :], c_lb[:, 1, :], ALU.add, [C("lb")], [k_ssum])
        TTo(ssum, ssum, c_lb[:, 2, :], ALU.add, [C("lb"), k_ssum], [k_ssum])
        TTo(ssum, ssum, c_lb[:, 3, :], ALU.add, [C("lb"), k_ssum], [k_ssum])
        RECIP(ssum, ssum, [k_ssum], [k_ssum])
        num, k_num = tmp(NDC)
        TTo(c_noml[:, 0, :], c_lb[:, 1, :], ssum, ALU.mult, [C("lb"), k_ssum], [C("oml")])
        TTo(num, c_lb[:, 1, :], c_lb[:, 2, :], ALU.add, [C("lb")], [k_num])
        TTo(num, num, c_lb[:, 3, :], ALU.add, [C("lb"), k_num], [k_num])
        TTo(c_noml[:, 1, :], num, ssum, ALU.mult, [k_num, k_ssum, C("oml")], [C("oml")])
        TS(c_oml[:], c_noml[:], -1.0, 1.0, ALU.mult, ALU.add, [C("oml")], [C("oml")])
        TS(c_noml[:], c_noml[:], -1.0, None, ALU.add, None, [C("oml")], [C("oml")])

        wslot = {"wup": 0, "wdn": 0, "wqfg": 0, "wout": 0}
        first_pass = [True]

        def scr(name, shape):
            return nc.dram_tensor(name, list(shape), BF16, kind="Internal").ap()

        scr_wup = scr("scr_wup", [4, NFC, 128, NDC * 2 * 128])
        scr_wdn = scr("scr_wdn", [4, NDC, 128, NFC * 128])
        scr_wqfg = scr("scr_wqfg", [2, NH, 128, NDC * 3 * 128])
        scr_wi = scr("scr_wi", [2, 128, NDC * D])
        scr_wout = scr("scr_wout", [2, NDC, 128, NDC * 128])
        scr_poolw = scr("scr_poolw", [2, 128, 4 * 2 * 256])

        def wload(slot_key, slot2d, scr2d, scr_key, cast_pairs):
            if first_pass[0]:
                DMA("pool", slot_key, cast_pairs, (), [slot_key])
                sk = ("st",) + (slot_key if isinstance(slot_key, tuple) else (slot_key,))
                DMA("sp", sk, [(scr2d, slot2d)], [slot_key], [scr_key])
            else:
                DMA(os.environ.get("K_WQ", "pool"), slot_key, [(slot2d, scr2d)], [scr_key], [slot_key])

        def load_wup(i, fc):
            s = wslot["wup"] % 3
            wslot["wup"] += 1
            src = d_wup[i].rearrange("(c p) n -> p c n", p=128)
            c0 = fc * 128
            wload(("wup", s), wup[:, s].rearrange("p c k n -> p (c k n)"), scr_wup[i, fc], ("scr", "wup", i, fc),
                  [(wup[:, s, :, 0, :], src[:, :, c0:c0 + 128]), (wup[:, s, :, 1, :], src[:, :, DFF + c0:DFF + c0 + 128])])
            return s

        def load_wdn(i, ec):
            s = wslot["wdn"] % 3
            wslot["wdn"] += 1
            src = d_wdn[i].rearrange("(c p) n -> p c n", p=128)
            wload(("wdn", s), wdn[:, s].rearrange("p c n -> p (c n)"), scr_wdn[i, ec], ("scr", "wdn", i, ec),
                  [(wdn[:, s, :, :], src[:, :, ec * 128:(ec + 1) * 128])])
            return s

        def load_wqfg(j, h):
            s = wslot["wqfg"] % 2
            wslot["wqfg"] += 1
            src = d_win[j].rearrange("(c p) n -> p c n", p=128)
            prs = []
            for k, base in enumerate((0, D, 3 * D)):
                prs.append((wqfg[:, s, :, k, :], src[:, :, base + h * 128: base + (h + 1) * 128]))
            wload(("wqfg", s), wqfg[:, s].rearrange("p c k n -> p (c k n)"), scr_wqfg[j, h], ("scr", "wqfg", j, h), prs)
            return s

        def load_wi(j):
            src = d_win[j].rearrange("(c p) n -> p c n", p=128)
            wload("wi", wi[:, :, :].rearrange("p c n -> p (c n)"), scr_wi[j], ("scr", "wi", j),
                  [(wi[:, :, :], src[:, :, 2 * D:3 * D])])

        def load_wout(j, ec):
            s = wslot["wout"] % 2
            wslot["wout"] += 1
            src = d_wout[j].rearrange("(c p) n -> p c n", p=128)
            wload(("wout", s), wout[:, s].rearrange("p c n -> p (c n)"), scr_wout[j, ec], ("scr", "wout", j, ec),
                  [(wout[:, s, :, :], src[:, :, ec * 128:(ec + 1) * 128])])
            return s

        def load_poolw(j):
            src = d_poolw[j].rearrange("g (k p) n -> p g k n", p=128)
            wload("poolw", poolw[:, :, :, :].rearrange("p g k n -> p (g k n)"), scr_poolw[j], ("scr", "poolw", j),
                  [(poolw[:, :, :, :], src)])

        class Tile:
            pass

        tiles = []
        for k in range(NPT):
            t = Tile()
            t.kind, t.k, t.nseg, t.L, t.T = "p", k, 1, TT, TT
            t.first, t.last = (k == 0), (k == NPT - 1)
            tiles.append(t)
        if NS > 0:
            t = Tile()
            t.kind, t.k, t.nseg, t.L, t.T = "s", 0, NS, 64, NS * 64
            t.first, t.last = True, True
            tiles.append(t)

        def v3(ap2d, t):
            return ap2d.rearrange("p (s l) -> p s l", s=t.nseg)

        def norm(t, gvec, dest, gkey):
            T = t.T
            ps, kps = psb()
            for dc in range(NDC):
                A(sq[:, dc, 0:T], x[:, dc, 0:T], AF.Square, [("x", dc)], [("hid", dc)])
            MM(ps[:, 0:T], [(onesD[:], sq[:, dc, 0:T]) for dc in range(NDC)],
               [("hid", dc) for dc in range(NDC)] + [C("onesD")], [kps])
            A(rstd[:, 0:T], ps[:, 0:T], AF.Ln, [kps, C("eps")], ["rstd"], bias=c_eps[:])
            A(rstd[:, 0:T], rstd[:, 0:T], AF.Exp, ["rstd"], ["rstd"], scale=-0.5)
            for dc in range(NDC):
                dst, wk = dest(dc)
                STT(dst, v3(x[:, dc, 0:T], t), gvec[:, dc:dc + 1], v3(rstd[:, 0:T], t), ALU.mult, ALU.mult,
                    [("x", dc), "rstd", C(gkey)], [wk])

        def hn_dest(t):
            return lambda dc: (v3(hnT[:, dc, 0:t.T], t), ("hn", dc))

        def pool_layer(t, i):
            j = i // 2
            T, L, ns = t.T, t.L, t.nseg
            W = 16 + L
            load_poolw(j)

            def Ev(dc):
                return E[:, dc, 0:ns * W].rearrange("p (s w) -> p s w", s=ns)

            if t.kind == "p":
                if t.first:
                    MS(E[:, :, 0:16], 0.0, [("hn", dc) for dc in range(NDC)], eng="dve")
                else:
                    CP(E[:, :, 1:16], EhP[:, j, :, 1:16], [("EhP", j)], [("hn", dc) for dc in range(NDC)], eng="dve")
            else:
                for s in range(ns):
                    DMA("sp", ("phist", s), [(phist[:, s, dc, 1:16], d_spool[j, s].rearrange("t (c p) -> p c t", p=128)[:, dc, :]) for dc in range(NDC)],
                        (), [("phist", s)], slow=True)
                    CP(E[:, :, s * W + 1: s * W + 16], phist[:, s, :, 1:16], [("phist", s)],
                       [("hn", dc) for dc in range(NDC)], eng="dve")
            norm(t, c_gmix[:, i, :], lambda dc: (Ev(dc)[:, :, 16:16 + L], ("hn", dc)), "gmix")
            if t.last:
                for s in range(ns):
                    st = 0 if t.kind == "p" else 1 + s
                    for dc in range(NDC):
                        a = s * L + L - 16
                        STT(pout[:, st, j, dc, :], x[:, dc, a:a + 16], c_gmix[:, i, dc:dc + 1], rstd[:, a:a + 16],
                            ALU.mult, ALU.mult, [("x", dc), "rstd"], [("pout", st, j)])
                    dstd = (o_poolp[j] if t.kind == "p" else o_pools[j, s]).rearrange("t (c p) -> p c t", p=128)
                    DMA("sp", "outs", [(dstd[:, dc, :], pout[:, st, j, dc, 1:16]) for dc in range(NDC)], [("pout", st, j)], [], slow=True)
            for dc in range(NDC):
                g = dc // 2
                win = 2 ** (g + 1)
                cur = Ev(dc)
                ck = ("hn", dc)
                sh = 1
                for lev in range(g + 1):
                    lo = 2 * sh
                    nt, kn = tmp(ns * W)
                    nv = nt.rearrange("p (s w) -> p s w", s=ns)
                    TTo(nv[:, :, lo:W], cur[:, :, lo:W], cur[:, :, lo - sh:W - sh], ALU.add, [ck], [kn])
                    cur, ck = nv, kn
                    sh *= 2
                if t.kind == "p" and t.first:
                    TTo(cur[:, 0, 16:32], cur[:, 0, 16:32], corr[:, g, :], ALU.mult, [ck, C("corr")], [ck])
                STT(v3(sq[:, dc, 0:T], t), cur[:, :, 16:W], 1.0 / win, Ev(dc)[:, :, 16:W], ALU.mult, ALU.subtract,
                    [ck, ("hn", dc)], [("hid", dc)])
            if t.kind == "p" and not t.last:
                CP(EhP[:, j, :, 1:16], E[:, :, L + 1:L + 16], [("hn", dc) for dc in range(NDC)], [("EhP", j)], eng="act")
            for ec in range(NDC):
                g, eo = ec // 2, ec % 2
                ps, kps = psb()
                MM(ps[:, 0:T], [(poolw[:, g, kc, eo * 128:(eo + 1) * 128], sq[:, 2 * g + kc, 0:T]) for kc in range(2)],
                   [("hid", 2 * g), ("hid", 2 * g + 1), "poolw"], [kps])
                STT(x[:, ec, 0:T], ps[:, 0:T], c_psc[:, j, ec:ec + 1], x[:, ec, 0:T], ALU.mult, ALU.add,
                    [kps, ("x", ec), C("psc")], [("x", ec)])

        def ffn_layer(t, i):
            T, L, ns = t.T, t.L, t.nseg
            norm(t, c_gffn[:, i, :], hn_dest(t), "gffn")
            if t.kind == "p":
                gh = lambda fc: ghP[:, i, fc, :].rearrange("p (s j) -> p s j", s=1)
                ghk = lambda fc: ("ghP", i, fc)
                if t.first:
                    MS(ghP[:, i, :, :], 0.0, [ghk(fc) for fc in range(NFC)], eng="dve")
            else:
                gh = lambda fc: ghS[:, i, fc, 0:ns, :]
                ghk = lambda fc: ("ghS", i, fc)
                for s in range(ns):
                    DMA("sp", ("ghS", i), [(ghS[:, i, :, s, jj], d_sconv[i, s, jj].rearrange("(c p) -> p c", p=128)) for jj in range(2)],
                        (), [ghk(fc) for fc in range(NFC)], slow=True)
            W = 2 + L
            pend = []

            def ffn_stage2():
                fc_, acc_, kacc_, pv_, kv_ = pend.pop(0)
                sl, ksl = tmp(T)
                A(sl, acc_, AF.Silu, [kacc_], [ksl])
                TTo(hid[:, fc_, 0:T], sl, pv_[:, 0:T], ALU.mult, [ksl, kv_], [("hid", fc_)])

            for fc in range(NFC):
                ws = load_wup(i, fc)
                for fl in range(1):
                    pg, kg = psb()
                    pv, kv = psb()
                    rd = [("hn", dc) for dc in range(NDC)] + [("wup", ws)]
                    MM(pg[:, 0:T], [(wup[:, ws, dc, 0, :], hnT[:, dc, 0:T]) for dc in range(NDC)], rd, [kg])
                    MM(pv[:, 0:T], [(wup[:, ws, dc, 1, :], hnT[:, dc, 0:T]) for dc in range(NDC)], rd, [kv])
                    G, kG = tmp(ns * W)
                    Gv = G.rearrange("p (s w) -> p s w", s=ns)
                    acc, kacc = tmp(T)
                    accv = v3(acc, t)
                    A(Gv[:, :, 2:W], v3(pg[:, 0:T], t), AF.Copy, [kg], [kG])
                    CP(Gv[:, :, 0:2], gh(fc), [ghk(fc)], [kG], eng="dve")
                    A(acc, pg[:, 0:T], AF.Identity, [kg, C("cw"), C("cb")], [kacc],
                      bias=c_cb[:, i, fc:fc + 1], scale=c_cw[:, i, 2, fc:fc + 1])
                    STT(accv, Gv[:, :, 1:1 + L], c_cw[:, i, 1, fc:fc + 1], accv, ALU.mult, ALU.add, [kG, kacc], [kacc])
                    STT(accv, Gv[:, :, 0:L], c_cw[:, i, 0, fc:fc + 1], accv, ALU.mult, ALU.add, [kG, kacc], [kacc])
                    CP(gh(fc), Gv[:, :, L:L + 2], [kG], [ghk(fc)], eng="dve")
                    pend.append((fc, acc, kacc, pv, kv))
                    if len(pend) > 1:
                        ffn_stage2()
            while pend:
                ffn_stage2()
            if t.last:
                for s in range(ns):
                    dstd = (o_convp[i] if t.kind == "p" else o_convs[i, s])
                    src = ghP[:, i, :, :] if t.kind == "p" else ghS[:, i, :, s, :]
                    DMA("sp", "outs", [(dstd[jj].rearrange("(c p) -> p c", p=128), src[:, :, jj]) for jj in range(2)],
                        [ghk(fc) for fc in range(NFC)], [], slow=True)
            for ec in range(NDC):
                ws = load_wdn(i, ec)
                ps, kps = psb()
                MM(ps[:, 0:T], [(wdn[:, ws, fc, :], hid[:, fc, 0:T]) for fc in range(NFC)],
                   [("hid", fc) for fc in range(NFC)] + [("wdn", ws)], [kps])
                TTo(x[:, ec, 0:T], x[:, ec, 0:T], ps[:, 0:T], ALU.add, [kps, ("x", ec)], [("x", ec)])

        def hgrn_layer(t, i):
            j = i // 2
            T, L, ns = t.T, t.L, t.nseg
            nch = T // 64
            npair = T // 128
            norm(t, c_gmix[:, i, :], hn_dest(t), "gmix")
            load_wi(j)
            if t.kind == "p":
                sid = [j] * nch
            else:
                sid = [0 if s == 0 else 2 for s in range(ns)] if j == 0 else [1 if s == 0 else 2 for s in range(ns)]
            if t.kind == "p":
                if t.first:
                    MS(S32[j][:], 0.0, [("S", j, h) for h in range(NH)], eng="dve")
                    MS(Scar[j][:], 0.0, [("Sc", j, h) for h in range(NH)], eng="dve")
            else:
                for s in range(ns):
                    b = sid[s]
                    DMA("sp", ("Sld", b), [(S32[b][:], d_shg[j, s].rearrange("h k v -> k h v"))], (),
                        [("S", b, h) for h in range(NH)])
                    CP(Scar[b][:], S32[b][:], [("S", b, h) for h in range(NH)], [("Sc", b, h) for h in range(NH)], eng="act")
            for tb in range(npair):
                for vh in range(2):
                    ps, kps = psb()
                    MM(ps[:, :], [(hnT[:, dc, tb * 128:(tb + 1) * 128], wi[:, dc, vh * 512:(vh + 1) * 512]) for dc in range(NDC)],
                       [("hn", dc) for dc in range(NDC)] + ["wi"], [kps])
                    CP(V[:, tb, vh * 512:(vh + 1) * 512], ps[:, :], [kps], [("hid", 2 * tb + vh)], eng="act")

            st = {}

            def partA1_pe(h):
                ws = load_wqfg(j, h)
                pq, kq = psb_at(0)
                pf, kf = psb_at(1)
                pgt, kgt = psb_at(2)
                rd = [("hn", dc) for dc in range(NDC)] + [("wqfg", ws)]
                for k, (pp, kk) in enumerate(((pq, kq), (pf, kf), (pgt, kgt))):
                    MM(pp[:, 0:T], [(wqfg[:, ws, dc, k, :], hnT[:, dc, 0:T]) for dc in range(NDC)], rd, [kk])
                st[h] = dict(pq=pq, kq=kq, pf=pf, kf=kf, pgt=pgt, kgt=kgt)

            def partA1_ew(h):
                d = st[h]
                pq, kq, pf, kf, pgt, kgt = d["pq"], d["kq"], d["pf"], d["kf"], d["pgt"], d["kgt"]
                s2, s4 = h % 2, h % 4
                sn, ksn = tmp_at(0, T)
                A(sn, pf[:, 0:T], AF.Sigmoid, [kf], [ksn], scale=-1.0)
                yield
                sgq, ksgq = tmp_at(1, T)
                A(sgq, pq[:, 0:T], AF.Sigmoid, [kq], [ksgq])
                yield
                sgg, ksgg = tmp_at(2, T)
                A(sgg, pgt[:, 0:T], AF.Sigmoid, [kgt], [ksgg])
                yield
                q, kq2 = tmp_at(3, T)
                TTo(q, sgq, pq[:, 0:T], ALU.mult, [ksgq, kq], [kq2])
                yield
                gate, kgate = gateb[:, s4, 0:T], ("gate", s4)
                TTo(gate, sgg, pgt[:, 0:T], ALU.mult, [ksgg, kgt], [kgate])
                yield
                gl, kgl = tmp_at(4, T)
                A(gl, sn, AF.Ln, [ksn, C("oml"), C("one")], [kgl], bias=c_one[:], scale=c_noml[:, j, h:h + 1])
                yield
                b, kb = tmp_at(5, T)
                S.add("dve", lambda e: e.tensor_tensor_scan(out=b, data0=rmask[:, 0:T], data1=gl, initial=0.0,
                                                            op0=ALU.mult, op1=ALU.add), [kgl, C("rmask")], [kb])
                yield
                eb, keb = ebb[:, s2, 0:T], ("eb", s2)
                A(eb, b, AF.Exp, [kb], [keb])
                yield
                enb, kenb = tmp_at(6, T)
                A(enb, b, AF.Exp, [kb], [kenb], scale=-1.0)
                yield
                qt, kqt = qtb[:, s4, 0:T], ("qt", s4)
                TTo(qt, q, eb, ALU.mult, [kq2, keb], [kqt])
                yield
                kt, kkt = ktb[:, s2, 0:T], ("kt", s2)
                STT(kt, sn, c_oml[:, j, h:h + 1], enb, ALU.mult, ALU.mult, [ksn, kenb, C("oml")], [kkt])
                yield
                for c in range(nch):
                    p_, hf = c // 2, c % 2
                    TS(khl[:, s2, hf, p_, hf * 64:(hf + 1) * 64], kt[:, c * 64:(c + 1) * 64], eb[:, c * 64 + 63:c * 64 + 64], None,
                       ALU.mult, None, [kkt, keb, C("khl")], [("khl", s2)])
                    yield
                st[h] = dict(gate=gate, kgate=kgate, qt=qt, kqt=kqt, kt=kt, kkt=kkt, eb=eb, keb=keb)

            def partA2(h):
                d = st[h]
                kt, kkt, eb, keb, qt, kqt = d["kt"], d["kkt"], d["eb"], d["keb"], d["qt"], d["kqt"]
                s2, s3 = h % 2, h % 3
                pT, kpT = psb_at(3, BF16)
                TR([(pT[:, (2 * p_ + hf) * 128:(2 * p_ + hf + 1) * 128], khl[:, s2, hf, p_, :], identb[:])
                    for p_ in range(npair) for hf in range(2)], [("khl", s2), C("identb")], [kpT])
                yield
                kT, kkT = tmp_at(7, 1024, BF16)
                CP(kT[:, 0:nch * 128], pT[:, 0:nch * 128], [kpT], [kkT], eng="act")
                yield
                pA, kpA = psb_at(4)
                MMseq([(pA[:, p_ * 128:(p_ + 1) * 128], [(kt[:, p_ * 128:(p_ + 1) * 128], qt[:, p_ * 128:(p_ + 1) * 128])], True)
                       for p_ in range(npair)], [kkt, kqt], [kpA])
                yield
                TTo(ATs[:, s3, 0:npair, :], pA[:, 0:npair * 128].rearrange("p (a b) -> p a b", b=128), maskbd[:, 0:npair, :], ALU.mult,
                    [kpA, C("maskbd")], [("ATs", s3)])
                yield
                pU = []
                for u0 in range(0, nch, 4):
                    pu, kpu = psb_at(5 + u0 // 4)
                    n = min(4, nch - u0)
                    MMseq([(pu[:, (c - u0) * 128:(c - u0 + 1) * 128], [(kT[:, c * 128:(c + 1) * 128], V[:, c // 2, h * 128:(h + 1) * 128])], True)
                           for c in range(u0, u0 + n)], [kkT] + [("hid", pg_) for pg_ in range(2 * (u0 // 2), 2 * ((u0 + n - 1) // 2) + 2)], [kpu])
                    pU.append((pu, kpu))
                    yield
                for c in range(nch):
                    b_ = sid[c]
                    pu, kpu = pU[c // 4]
                    STT(S32[b_][:, h, :], S32[b_][:, h, :], eb[:, c * 64 + 63:c * 64 + 64], pu[:, (c % 4) * 128:(c % 4 + 1) * 128],
                        ALU.mult, ALU.add, [("S", b_, h), keb, kpu], [("S", b_, h)])
                    yield
                    lastc = (c == nch - 1) or (t.kind == "s")
                    if lastc:
                        CP(Scar2[:, s3, b_, :], S32[b_][:, h, :], [("S", b_, h)], [("Sc2", s3, b_)], eng="act")
                    else:
                        CP(Sv[:, s3, c, :], S32[b_][:, h, :], [("S", b_, h)], [("Sv", s3, c)], eng="act")
                    yield

            def partB1(h):
                d = st[h]
                qt, kqt = d["qt"], d["kqt"]
                s3 = h % 3
                po, kpo = psb_at(7)
                items = []
                rd = [("ATs", s3), kqt]
                for c in range(nch):
                    p_ = c // 2
                    if c % 2 == 0:
                        items.append((po[:, p_ * 128:(p_ + 1) * 128], [(V[:, p_, h * 128:(h + 1) * 128], ATs[:, s3, p_, :])], True))
                        rd += [("hid", 2 * p_), ("hid", 2 * p_ + 1)]
                    b_ = sid[c]
                    if c == 0 or t.kind == "s":
                        lhs = Scar[b_][:, h, :]
                        rd.append(("Sc", b_, h))
                    else:
                        lhs = Sv[:, s3, c - 1, :]
                        rd.append(("Sv", s3, c - 1))
                    items.append((po[:, c * 64:(c + 1) * 64], [(lhs, qt[:, c * 64:(c + 1) * 64])], False))
                MMseq(items, rd, [kpo])
                osq, kosq = tmp_at(8, T, BF16)
                A(osq, po[:, 0:T], AF.Square, [kpo], [kosq])
                d["po"], d["kpo"], d["osq"], d["kosq"] = po, kpo, osq, kosq

            def partB2(h):
                d = st.pop(h)
                po, kpo, osq, kosq, gate, kgate = d["po"], d["kpo"], d["osq"], d["kosq"], d["gate"], d["kgate"]
                s3 = h % 3
                pss, kpss = psb_at(6)
                MM(pss[:, 0:T], [(onesH[:], osq)], [kosq, C("onesH")], [kpss])
                rs, krs = tmp_at(9, T)
                A(rs, pss[:, 0:T], AF.Ln, [kpss, C("eps")], [krs], bias=c_eps[:])
                A(rs, rs, AF.Exp, [krs], [krs], scale=-0.5)
                t1, kt1 = tmp_at(10, T)
                STT(t1, po[:, 0:T], c_gn[:, j, h:h + 1], rs, ALU.mult, ALU.mult, [kpo, krs, C("gn")], [kt1])
                TTo(ofin[:, h, 0:T], t1, gate, ALU.mult, [kt1, kgate], [("hid", 8 + h)])
                for b_ in sorted(set(sid)):
                    CP(Scar[b_][:, h, :], Scar2[:, s3, b_, :], [("Sc2", s3, b_)], [("Sc", b_, h)], eng="act")

            def merge(gens):
                gens = list(gens)
                while gens:
                    for g_ in list(gens):
                        try:
                            next(g_)
                        except StopIteration:
                            gens.remove(g_)

            for k in range(NH + 3):
                if 0 <= k - 3 < NH:
                    partB1(k - 3)
                if k < NH:
                    partA1_pe(k)
                if 0 <= k - 3 < NH:
                    partB2(k - 3)
                gl_ = []
                if 0 <= k - 1 < NH:
                    gl_.append(partA2(k - 1))
                if k < NH:
                    gl_.append(partA1_ew(k))
                merge(gl_)
            if t.last:
                for s in range(ns):
                    b_ = sid[0] if t.kind == "p" else sid[s]
                    dstd = (o_hgp[j] if t.kind == "p" else o_hgs[j, s]).rearrange("h k v -> k h v")
                    DMA("sp", "outs", [(dstd, S32[b_][:])], [("S", b_, h) for h in range(NH)], [])
            for ec in range(NDC):
                ws = load_wout(j, ec)
                ps, kps = psb()
                MM(ps[:, 0:T], [(wout[:, ws, dc, :], ofin[:, dc, 0:T]) for dc in range(NDC)],
                   [("hid", 8 + dc) for dc in range(NDC)] + [("wout", ws)], [kps])
                TTo(x[:, ec, 0:T], x[:, ec, 0:T], ps[:, 0:T], ALU.add, [kps, ("x", ec)], [("x", ec)])


        xin_ctr = [0]

        def load_x(t):
            ntb = t.T // 128
            src = d_xp if t.kind == "p" else d_xs
            r0 = t.k * TT if t.kind == "p" else 0
            for tb in range(ntb):
                s = xin_ctr[0] % 2
                xin_ctr[0] += 1
                DMA("sp", ("io", s), [(xin[:, s, :], src[r0 + tb * 128: r0 + (tb + 1) * 128, :])], (), [("io", s)])
                for hb in range(2):
                    ps, kps = psb()
                    TR([(ps[:, q_ * 128:(q_ + 1) * 128], xin[:, s, (hb * 4 + q_) * 128:(hb * 4 + q_ + 1) * 128], identf[:]) for q_ in range(4)],
                       [("io", s), C("identf")], [kps])
                    CP(x[:, hb * 4:hb * 4 + 4, tb * 128:(tb + 1) * 128], ps[:, :].rearrange("p (a b) -> p a b", b=128), [kps],
                       [("x", hb * 4 + q_) for q_ in range(4)], eng="act")

        yo_ctr = [0]

        def store_y(t):
            T = t.T
            ntb = T // 128
            norm(t, c_gout[:, 0, :], lambda dc: (v3(x[:, dc, 0:T], t), ("x", dc)), "gout")
            dst = o_yp if t.kind == "p" else o_ys
            r0 = t.k * TT if t.kind == "p" else 0
            for tb in range(ntb):
                s = xin_ctr[0] % 2
                xin_ctr[0] += 1
                for hb in range(2):
                    ps, kps = psb()
                    TR([(ps[:, q_ * 128:(q_ + 1) * 128], x[:, hb * 4 + q_, tb * 128:(tb + 1) * 128], identf[:]) for q_ in range(4)],
                       [("x", hb * 4 + q_) for q_ in range(4)] + [C("identf")], [kps])
                    CP(yout[:, s, hb * 512:(hb + 1) * 512], ps[:, :], [kps], [("io", s)], eng="act")
                DMA("sp", ("io", s), [(dst[r0 + tb * 128: r0 + (tb + 1) * 128, :], yout[:, s, :])], [("io", s)], [])

        for ti, t in enumerate(tiles):
            first_pass[0] = (ti == 0)
            load_x(t)
            for i in LY:
                if i % 2 == 0:
                    pool_layer(t, i)
                else:
                    hgrn_layer(t, i)
                ffn_layer(t, i)
            store_y(t)

        S.finalize()
        esems = {e: es.enter_context(nc.semaphore("se_" + e)) for e in Sched.ENGS}
        dsems = {}
        for n, k in enumerate(S.dma_cnt):
            dsems[k] = es.enter_context(nc.semaphore(f"sd_{n}"))
        out_keys = ["outs"] + [("io", s) for s in range(2)]
        with nc.Block() as block:
            @block.tensor
            def _(e):
                S.emit("pe", e, esems, dsems)

            @block.scalar
            def _(e):
                S.emit("act", e, esems, dsems)

            @block.vector
            def _(e):
                S.emit("dve", e, esems, dsems)

            @block.gpsimd
            def _(e):
                S.emit("pool", e, esems, dsems)

            @block.sync
            def _(e):
                S.emit("sp", e, esems, dsems)
                for k in out_keys:
                    if k in S.dma_cnt:
                        e.wait_ge(dsems[k], 16 * S.dma_cnt[k])
    nc._sched = S
    return nc


_NC_CACHE = {}


def run_cores(cfg, in_maps):
    key = (cfg.layers, cfg.npt, cfg.ns)
    if key not in _NC_CACHE:
        _NC_CACHE[key] = build(cfg)
    nc = _NC_CACHE[key]
    return run_bass_kernel_spmd(nc, in_maps, core_ids=list(range(len(in_maps))))


def kernel(x_prompt, x_sample, state_pool, state_hgrn, state_ffn_conv, norm_mix_g, pool_w, pool_scale,
           hgrn_w_in, hgrn_lb_logits, hgrn_norm_g, hgrn_w_out, norm_ffn_g, ffn_w_up, ffn_conv_w,
           ffn_conv_b, ffn_w_down, norm_out_g):
    f = lambda a: np.ascontiguousarray(np.asarray(a, dtype=np.float32))
    x_prompt, x_sample = f(x_prompt), f(x_sample)
    state_pool, state_hgrn, state_ffn_conv = f(state_pool), f(state_hgrn), f(state_ffn_conv)
    BP, SEQ, _ = x_prompt.shape
    NSB = x_sample.shape[0]
    ncores = 8
    nsc = NSB // ncores
    cfg = Cfg(layers=(0, 1, 2, 3), npt=SEQ // TT, ns=nsc)
    shared = {
        "norm_mix_g": f(norm_mix_g), "pool_w": f(pool_w), "pool_scale": f(pool_scale), "hgrn_w_in": f(hgrn_w_in),
        "hgrn_lb_logits": f(hgrn_lb_logits), "hgrn_norm_g": f(hgrn_norm_g), "hgrn_w_out": f(hgrn_w_out),
        "norm_ffn_g": f(norm_ffn_g), "ffn_w_up": f(ffn_w_up), "ffn_conv_w": f(ffn_conv_w), "ffn_conv_b": f(ffn_conv_b),
        "ffn_w_down": f(ffn_w_down), "norm_out_g": f(norm_out_g).reshape(1, D),
    }
    in_maps = []
    for c in range(ncores):
        sl = slice(c * nsc, (c + 1) * nsc)
        m = dict(shared)
        m["xp"] = x_prompt[c % BP]
        m["xs"] = x_sample[sl].reshape(nsc * 64, D)
        m["s_pool"] = np.ascontiguousarray(state_pool[:, sl])
        m["s_hgrn"] = np.ascontiguousarray(state_hgrn[:, sl])
        m["s_conv"] = np.ascontiguousarray(state_ffn_conv[:, sl])
        in_maps.append(m)
    res = run_cores(cfg, in_maps).results
    y_prompt = np.stack([res[b]["yp"] for b in range(BP)])
    y_sample = np.concatenate([res[c]["ys"].reshape(nsc, 64, D) for c in range(ncores)])
    pool_p = np.stack([res[b]["npool_p"] for b in range(BP)], axis=1)
    pool_s = np.concatenate([res[c]["npool_s"] for c in range(ncores)], axis=1)
    hg_p = np.stack([res[b]["nhgrn_p"] for b in range(BP)], axis=1)
    hg_s = np.concatenate([res[c]["nhgrn_s"] for c in range(ncores)], axis=1)
    cv_p = np.stack([res[b]["nconv_p"] for b in range(BP)], axis=1)
    cv_s = np.concatenate([res[c]["nconv_s"] for c in range(ncores)], axis=1)
    return tuple(np.ascontiguousarray(a, dtype=np.float32) for a in (y_prompt, y_sample, pool_p, pool_s, hg_p, hg_s, cv_p, cv_s))
```

```python
import sys
from contextlib import ExitStack
import numpy as np
import concourse.bass as bass
import concourse.mybir as mybir
from concourse.bass_utils import run_bass_kernel_spmd

F32 = mybir.dt.float32
BF16 = mybir.dt.bfloat16
AF = mybir.ActivationFunctionType
ALU = mybir.AluOpType

D = 1024
NDC = 8
DFF = 2816
NFC = 22
NH = 8
EPS = 1e-6
TT = 512
TSLOT = 528
NTMP = 11
import os
SAME_ENG_SYNC = os.environ.get("K_SES", "1") == "1"


class Op:
    __slots__ = ("eng", "fn", "deps", "dma_key", "ndma", "idx", "ms", "dma_val", "is_ms", "label", "ninst")

    def __init__(self, eng, fn, dma_key, ndma):
        self.eng = eng
        self.fn = fn
        self.dma_key = dma_key
        self.ndma = ndma
        self.deps = None
        self.ms = 0
        self.is_ms = False
        self.dma_val = 0


class Sched:
    ENGS = ("pe", "act", "dve", "pool", "sp")

    def __init__(self):
        self.streams = {e: [] for e in self.ENGS}
        self.last_w = {}
        self.readers = {}
        self.dma_cnt = {}
        self.gen = {}

    def fresh(self, kind, slot):
        g = self.gen.get((kind, slot), 0)
        old = (kind, slot, g)
        new = (kind, slot, g + 1)
        self.gen[(kind, slot)] = g + 1
        if old in self.last_w:
            self.last_w[new] = self.last_w.pop(old)
        if old in self.readers:
            self.readers[new] = self.readers.pop(old)
        return new

    def _check(self, r):
        if len(r) == 3 and r[0] in ("t", "ps"):
            assert self.gen.get((r[0], r[1]), 0) == r[2], f"stale resource {r}"

    def add(self, eng, fn, reads=(), writes=(), dma_key=None, ndma=1):
        op = Op(eng, fn, dma_key, ndma)
        try:
            f = sys._getframe(2)
            g = f.f_back
            op.label = f"{g.f_code.co_name}:{g.f_lineno}" if g is not None else f.f_code.co_name
        except Exception:
            op.label = "?"
        op.ninst = 1
        deps = {}

        def dep(o):
            if o is None or o is op:
                return
            k = o.dma_key if o.dma_key is not None else o.eng
            p = deps.get(k)
            if p is None or self._later(o, p):
                deps[k] = o

        for r in reads:
            self._check(r)
            dep(self.last_w.get(r))
        for r in writes:
            self._check(r)
            dep(self.last_w.get(r))
            for o in self.readers.get(r, {}).values():
                dep(o)
        op.idx = len(self.streams[eng])
        self.streams[eng].append(op)
        if dma_key is not None:
            c = self.dma_cnt.get(dma_key, 0) + ndma
            self.dma_cnt[dma_key] = c
            op.dma_val = 16 * c
        for r in reads:
            d = self.readers.setdefault(r, {})
            k = dma_key if dma_key is not None else eng
            d[k] = op
        for r in writes:
            self.last_w[r] = op
            self.readers[r] = {}
        op.deps = list(deps.values())
        return op

    @staticmethod
    def _later(a, b):
        if a.dma_key is not None:
            return a.dma_val > b.dma_val
        return a.idx > b.idx

    def finalize(self):
        for e in self.ENGS:
            for op in self.streams[e]:
                for d in op.deps:
                    if d.dma_key is None:
                        if d.eng == op.eng and (d.eng == "pe" or not SAME_ENG_SYNC):
                            continue
                        d.is_ms = True
        for e in self.ENGS:
            c = 0
            for op in self.streams[e]:
                if op.is_ms:
                    c += 1
                    op.ms = c

    def emit(self, eng_name, eng, esems, dsems):
        waited = {}
        for op in self.streams[eng_name]:
            for d in op.deps:
                if d.dma_key is not None:
                    key, val, sem = ("d", d.dma_key), d.dma_val, dsems[d.dma_key]
                else:
                    if d.eng == op.eng and (d.eng == "pe" or not SAME_ENG_SYNC):
                        continue
                    key, val, sem = ("e", d.eng), d.ms, esems[d.eng]
                if waited.get(key, 0) >= val:
                    continue
                waited[key] = val
                eng.wait_ge(sem, val)
            if op.dma_key is not None:
                op.fn(eng, dsems[op.dma_key])
            else:
                ins = op.fn(eng)
                if op.is_ms:
                    ins.then_inc(esems[eng_name], 1)


class Cfg:
    def __init__(self, layers=(0, 1, 2, 3), npt=16, ns=2):
        self.layers = tuple(layers)
        self.npt = npt
        self.ns = ns


def build(cfg):
    nc = bass.Bass("TRN2", target_bir_lowering=False)
    LY = cfg.layers
    NL = len(LY)
    NPT = cfg.npt
    NS = cfg.ns
    SEQP = NPT * TT
    pool_layers = [i for i in LY if i % 2 == 0]
    hg_layers = [i for i in LY if i % 2 == 1]
    NPL = max(1, len(pool_layers))
    NHL = max(1, len(hg_layers))

    def dt(name, shape, kind):
        return nc.dram_tensor(name, list(shape), F32, kind=kind).ap()

    I, O = "ExternalInput", "ExternalOutput"
    d_xp = dt("xp", [max(SEQP, 1), D], I)
    d_xs = dt("xs", [max(NS, 1) * 64, D], I)
    d_spool = dt("s_pool", [2, max(NS, 1), 15, D], I)
    d_shg = dt("s_hgrn", [2, max(NS, 1), NH, 128, 128], I)
    d_sconv = dt("s_conv", [4, max(NS, 1), 2, DFF], I)
    d_gmix = dt("norm_mix_g", [4, D], I)
    d_poolw = dt("pool_w", [2, 4, 256, 256], I)
    d_pscale = dt("pool_scale", [2, D], I)
    d_win = dt("hgrn_w_in", [2, D, 4 * D], I)
    d_lbl = dt("hgrn_lb_logits", [4, D], I)
    d_gn = dt("hgrn_norm_g", [2, D], I)
    d_wout = dt("hgrn_w_out", [2, D, D], I)
    d_gffn = dt("norm_ffn_g", [4, D], I)
    d_wup = dt("ffn_w_up", [4, D, 2 * DFF], I)
    d_cw = dt("ffn_conv_w", [4, 3, DFF], I)
    d_cb = dt("ffn_conv_b", [4, DFF], I)
    d_wdn = dt("ffn_w_down", [4, DFF, D], I)
    d_gout = dt("norm_out_g", [1, D], I)
    o_yp = dt("yp", [max(SEQP, 1), D], O)
    o_ys = dt("ys", [max(NS, 1) * 64, D], O)
    o_poolp = dt("npool_p", [2, 15, D], O)
    o_pools = dt("npool_s", [2, max(NS, 1), 15, D], O)
    o_hgp = dt("nhgrn_p", [2, NH, 128, 128], O)
    o_hgs = dt("nhgrn_s", [2, max(NS, 1), NH, 128, 128], O)
    o_convp = dt("nconv_p", [4, 2, DFF], O)
    o_convs = dt("nconv_s", [4, max(NS, 1), 2, DFF], O)

    S = Sched()
    es = ExitStack()
    with es:
        def sb(name, shape, dtype=F32):
            return es.enter_context(nc.sbuf_tensor(name, list(shape), dtype))

        x = sb("x", [128, NDC, TT])
        E = sb("E", [128, NDC, 16 + TT], BF16)
        hnT = E
        rstd = sb("rstd", [128, TT])
        hid = sb("hid", [128, NFC, TT], BF16)
        sq = hid
        V = hid[:, 0:8, :].rearrange("p (a b) t -> p a (b t)", b=2)
        ofin = hid[:, 8:16, :]
        gateb = sb("gateb", [128, 4, TT], BF16)
        qtb = sb("qtb", [128, 4, TT], BF16)
        ktb = sb("ktb", [128, 2, TT], BF16)
        ebb = sb("ebb", [128, 2, TT])
        temps = sb("temps", [128, NTMP * TSLOT])
        wup = sb("wup", [128, 3, NDC, 2, 128], BF16)
        wdn = sb("wdn", [128, 3, NFC, 128], BF16)
        wqfg = sb("wqfg", [128, 2, NDC, 3, 128], BF16)
        wi = sb("wi", [128, NDC, D], BF16)
        wout = sb("wout", [128, 2, NDC, 128], BF16)
        poolw = sb("poolw", [128, 4, 2, 256], BF16)
        S32 = [sb(f"S32_{k}", [128, NH, 128]) for k in range(3)]
        Scar = [sb(f"Scar_{k}", [128, NH, 128], BF16) for k in range(3)]
        Scar2 = sb("Scar2", [128, 3, 3, 128], BF16)
        Sv = sb("Sv", [128, 3, 8, 128], BF16)
        khl = sb("khl", [128, 2, 2, 4, 128], BF16)
        ATs = sb("ATs", [128, 3, 4, 128], BF16)
        xin = sb("xin", [128, 2, D])
        yout = xin
        identb = sb("identb", [128, 128], BF16)
        identf = sb("identf", [128, 128])
        onesD = sb("onesD", [128, 128], BF16)
        onesH = sb("onesH", [128, 128], BF16)
        maskbd = sb("maskbd", [128, 4, 128], BF16)
        rmask = sb("rmask", [128, TT])
        corr = sb("corr", [128, 4, 16])
        c_gmix = sb("c_gmix", [128, 4, NDC])
        c_gffn = sb("c_gffn", [128, 4, NDC])
        c_gout = sb("c_gout", [128, 1, NDC])
        c_psc = sb("c_psc", [128, 2, NDC])
        c_gn = sb("c_gn", [128, 2, NDC])
        c_lbl = sb("c_lbl", [128, 4, NDC])
        c_lb = sb("c_lb", [128, 4, NDC])
        c_oml = sb("c_oml", [128, 2, NDC])
        c_noml = sb("c_noml", [128, 2, NDC])
        c_cw = sb("c_cw", [128, 4, 3, NFC])
        c_cb = sb("c_cb", [128, 4, NFC])
        c_eps = sb("c_eps", [128, 1])
        c_one = sb("c_one", [128, 1])
        ghP = sb("ghP", [128, 4, NFC, 2])
        ghS = sb("ghS", [128, 4, NFC, 2, 2])
        EhP = sb("EhP", [128, 2, NDC, 16], BF16)
        phist = sb("phist", [128, 2, NDC, 16])
        pout = sb("pout", [128, 3, 2, NDC, 16])
        psum = es.enter_context(nc.psum_tensor("psum", [128, 8, 512], F32))

        tmp_ctr = [0]
        ps_ctr = [0]

        def tmp(n=TT, dtype=F32):
            s = tmp_ctr[0] % NTMP
            tmp_ctr[0] += 1
            key = S.fresh("t", s)
            ap = temps[:, s * TSLOT:(s + 1) * TSLOT]
            if dtype == BF16:
                ap = ap.bitcast(BF16)
            return ap[:, 0:n], key

        def tmp_at(s, n=TT, dtype=F32):
            key = S.fresh("t", s)
            ap = temps[:, s * TSLOT:(s + 1) * TSLOT]
            if dtype == BF16:
                ap = ap.bitcast(BF16)
            return ap[:, 0:n], key

        def psb_at(b, dtype=F32):
            key = S.fresh("ps", b)
            ap = psum[:, b, :]
            if dtype == BF16:
                ap = ap.bitcast(BF16)
            return ap, key

        def psb(dtype=F32):
            b = ps_ctr[0] % 8
            ps_ctr[0] += 1
            key = S.fresh("ps", b)
            ap = psum[:, b, :]
            if dtype == BF16:
                ap = ap.bitcast(BF16)
            return ap, key

        def A(out, in_, func, r, w, bias=None, scale=None):
            kw = {}
            if bias is not None:
                kw["bias"] = bias
            if scale is not None:
                kw["scale"] = scale
            S.add("act", lambda e: e.activation(out=out, in_=in_, func=func, **kw), r, w)

        def TS(out, in0, s1, s2, op0, op1, r, w, eng="dve"):
            if s2 is None:
                S.add(eng, lambda e: e.tensor_scalar(out=out, in0=in0, scalar1=s1, scalar2=None, op0=op0), r, w)
            else:
                S.add(eng, lambda e: e.tensor_scalar(out=out, in0=in0, scalar1=s1, scalar2=s2, op0=op0, op1=op1), r, w)

        def STT(out, in0, sc, in1, op0, op1, r, w):
            S.add("dve", lambda e: e.scalar_tensor_tensor(out=out, in0=in0, scalar=sc, in1=in1, op0=op0, op1=op1), r, w)

        def TTo(out, in0, in1, op, r, w, eng="dve"):
            S.add(eng, lambda e: e.tensor_tensor(out=out, in0=in0, in1=in1, op=op), r, w)

        def CP(out, in_, r, w, eng="dve"):
            if eng == "act":
                S.add(eng, lambda e: e.activation(out=out, in_=in_, func=AF.Copy), r, w)
            else:
                S.add(eng, lambda e: e.tensor_copy(out=out, in_=in_), r, w)

        def MS(ap, val, w, eng="pool"):
            S.add(eng, lambda e: e.memset(ap, val), (), w)

        def RECIP(out, in_, r, w):
            S.add("dve", lambda e: e.reciprocal(out=out, in_=in_), r, w)

        def MM(out, pairs, r, w):
            def fn(e):
                n = len(pairs)
                ins = None
                for i, (l, rr) in enumerate(pairs):
                    ins = e.matmul(out, lhsT=l, rhs=rr, start=(i == 0), stop=(i == n - 1))
                return ins
            S.add("pe", fn, r, w).ninst = len(pairs)

        def MMseq(items, r, w):
            def fn(e):
                ins = None
                for out, pairs, st in items:
                    n = len(pairs)
                    for i, (l, rr) in enumerate(pairs):
                        ins = e.matmul(out, lhsT=l, rhs=rr, start=(st and i == 0), stop=(i == n - 1))
                return ins
            S.add("pe", fn, r, w).ninst = sum(len(p) for _, p, _ in items)

        def TR(items, r, w):
            def fn(e):
                ins = None
                for out, in_, ident in items:
                    ins = e.transpose(out=out, in_=in_, identity=ident)
                return ins
            S.add("pe", fn, r, w).ninst = len(items)

        def DMA(eng, key, pairs, r, w, slow=False):
            def fn(e, sem):
                for out, in_ in pairs:
                    if slow:
                        e.dma_start(out=out, in_=in_, allow_slow_non_contiguous=True).then_inc(sem, 16)
                    else:
                        e.dma_start(out=out, in_=in_).then_inc(sem, 16)
            S.add(eng, fn, r, w, dma_key=key, ndma=len(pairs))

        C = lambda n: ("c", n)

        MS(identb[:], 0.0, [C("identb")])
        S.add("pool", lambda e: e.affine_select(out=identb[:], in_=identb[:], pattern=[[-1, 128]], compare_op=ALU.not_equal,
                                                fill=1.0, base=0, channel_multiplier=1), [C("identb")], [C("identb")])
        MS(identf[:], 0.0, [C("identf")])
        S.add("pool", lambda e: e.affine_select(out=identf[:], in_=identf[:], pattern=[[-1, 128]], compare_op=ALU.not_equal,
                                                fill=1.0, base=0, channel_multiplier=1), [C("identf")], [C("identf")])
        MS(onesD[:], 1.0 / D, [C("onesD")])
        MS(onesH[:], 1.0 / 128, [C("onesH")])
        MS(c_eps[:], EPS, [C("eps")])
        MS(c_one[:], 1.0, [C("one")])
        MS(maskbd[:], 1.0, [C("maskbd")])
        S.add("pool", lambda e: e.affine_select(out=maskbd[:], in_=maskbd[:], pattern=[[0, 4], [1, 128]], compare_op=ALU.is_ge,
                                                fill=0.0, base=0, channel_multiplier=-1), [C("maskbd")], [C("maskbd")])
        MS(maskbd[0:64, :, 64:128], 0.0, [C("maskbd")])
        MS(rmask[:], 1.0, [C("rmask")])
        MS(rmask[:].rearrange("p (c l) -> p c l", l=64)[:, :, 0:1], 0.0, [C("rmask")])
        MS(corr[:], 1.0, [C("corr")])
        for g in range(4):
            win = 2 ** (g + 1)
            for t in range(win - 1):
                MS(corr[:, g, t:t + 1], float(win) / float(t + 1), [C("corr")])
        MS(khl[:], 0.0, [C("khl")])
        def vload(dst, src, name):
            DMA("sp", "setup", [(dst, src)], (), [C(name)], slow=True)
        def vload2(dst, src, nl, name):
            DMA("sp", "setup_" + name, [(dst[:, l, :], src[l].rearrange("(c p) -> p c", p=128)) for l in range(nl)], (), [C(name)], slow=True)
        vload2(c_gmix, d_gmix, 4, "gmix")
        vload2(c_gffn, d_gffn, 4, "gffn")
        vload2(c_gout, d_gout, 1, "gout")
        vload2(c_psc, d_pscale, 2, "psc")
        vload2(c_gn, d_gn, 2, "gn")
        vload2(c_lbl, d_lbl, 4, "lbl")
        vload2(c_cb, d_cb, 4, "cb")
        for l in range(4):
            vload2(c_cw[:, l], d_cw[l], 3, "cw")
        A(c_lb[:], c_lbl[:], AF.Exp, [C("lbl")], [C("lb")])
        ssum, k_ssum = tmp(NDC)
        TTo(ssum, c_lb[:, 0, :], c_lb[:, 1, :], ALU.add, [C("lb")], [k_ssum])
        TTo(ssum, ssum, c_lb[:, 2, :], ALU.add, [C("lb"), k_ssum], [k_ssum])
        TTo(ssum, ssum, c_lb[:, 3, :], ALU.add, [C("lb"), k_ssum], [k_ssum])
        RECIP(ssum, ssum, [k_ssum], [k_ssum])
        num, k_num = tmp(NDC)
        TTo(c_noml[:, 0, :], c_lb[:, 1, :], ssum, ALU.mult, [C("lb"), k_ssum], [C("oml")])
        TTo(num, c_lb[:, 1, :], c_lb[:, 2, :], ALU.add, [C("lb")], [k_num])
        TTo(num, num, c_lb[:, 3, :], ALU.add, [C("lb"), k_num], [k_num])
        TTo(c_noml[:, 1, :], num, ssum, ALU.mult, [k_num, k_ssum, C("oml")], [C("oml")])
        TS(c_oml[:], c_noml[:], -1.0, 1.0, ALU.mult, ALU.add, [C("oml")], [C("oml")])
        TS(c_noml[:], c_noml[:], -1.0, None, ALU.add, None, [C("oml")], [C("oml")])

        wslot = {"wup": 0, "wdn": 0, "wqfg": 0, "wout": 0}
        first_pass = [True]

        def scr(name, shape):
            return nc.dram_tensor(name, list(shape), BF16, kind="Internal").ap()

        scr_wup = scr("scr_wup", [4, NFC, 128, NDC * 2 * 128])
        scr_wdn = scr("scr_wdn", [4, NDC, 128, NFC * 128])
        scr_wqfg = scr("scr_wqfg", [2, NH, 128, NDC * 3 * 128])
        scr_wi = scr("scr_wi", [2, 128, NDC * D])
        scr_wout = scr("scr_wout", [2, NDC, 128, NDC * 128])
        scr_poolw = scr("scr_poolw", [2, 128, 4 * 2 * 256])

        def wload(slot_key, slot2d, scr2d, scr_key, cast_pairs):
            if first_pass[0]:
                DMA("pool", slot_key, cast_pairs, (), [slot_key])
                sk = ("st",) + (slot_key if isinstance(slot_key, tuple) else (slot_key,))
                DMA("sp", sk, [(scr2d, slot2d)], [slot_key], [scr_key])
            else:
                DMA(os.environ.get("K_WQ", "pool"), slot_key, [(slot2d, scr2d)], [scr_key], [slot_key])

        def load_wup(i, fc):
            s = wslot["wup"] % 3
            wslot["wup"] += 1
            src = d_wup[i].rearrange("(c p) n -> p c n", p=128)
            c0 = fc * 128
            wload(("wup", s), wup[:, s].rearrange("p c k n -> p (c k n)"), scr_wup[i, fc], ("scr", "wup", i, fc),
                  [(wup[:, s, :, 0, :], src[:, :, c0:c0 + 128]), (wup[:, s, :, 1, :], src[:, :, DFF + c0:DFF + c0 + 128])])
            return s

        def load_wdn(i, ec):
            s = wslot["wdn"] % 3
            wslot["wdn"] += 1
            src = d_wdn[i].rearrange("(c p) n -> p c n", p=128)
            wload(("wdn", s), wdn[:, s].rearrange("p c n -> p (c n)"), scr_wdn[i, ec], ("scr", "wdn", i, ec),
                  [(wdn[:, s, :, :], src[:, :, ec * 128:(ec + 1) * 128])])
            return s

        def load_wqfg(j, h):
            s = wslot["wqfg"] % 2
            wslot["wqfg"] += 1
            src = d_win[j].rearrange("(c p) n -> p c n", p=128)
            prs = []
            for k, base in enumerate((0, D, 3 * D)):
                prs.append((wqfg[:, s, :, k, :], src[:, :, base + h * 128: base + (h + 1) * 128]))
            wload(("wqfg", s), wqfg[:, s].rearrange("p c k n -> p (c k n)"), scr_wqfg[j, h], ("scr", "wqfg", j, h), prs)
            return s

        def load_wi(j):
            src = d_win[j].rearrange("(c p) n -> p c n", p=128)
            wload("wi", wi[:, :, :].rearrange("p c n -> p (c n)"), scr_wi[j], ("scr", "wi", j),
                  [(wi[:, :, :], src[:, :, 2 * D:3 * D])])

        def load_wout(j, ec):
            s = wslot["wout"] % 2
            wslot["wout"] += 1
            src = d_wout[j].rearrange("(c p) n -> p c n", p=128)
            wload(("wout", s), wout[:, s].rearrange("p c n -> p (c n)"), scr_wout[j, ec], ("scr", "wout", j, ec),
                  [(wout[:, s, :, :], src[:, :, ec * 128:(ec + 1) * 128])])
            return s

        def load_poolw(j):
            src = d_poolw[j].rearrange("g (k p) n -> p g k n", p=128)
            wload("poolw", poolw[:, :, :, :].rearrange("p g k n -> p (g k n)"), scr_poolw[j], ("scr", "poolw", j),
                  [(poolw[:, :, :, :], src)])

        class Tile:
            pass

        tiles = []
        for k in range(NPT):
            t = Tile()
            t.kind, t.k, t.nseg, t.L, t.T = "p", k, 1, TT, TT
            t.first, t.last = (k == 0), (k == NPT - 1)
            tiles.append(t)
        if NS > 0:
            t = Tile()
            t.kind, t.k, t.nseg, t.L, t.T = "s", 0, NS, 64, NS * 64
            t.first, t.last = True, True
            tiles.append(t)

        def v3(ap2d, t):
            return ap2d.rearrange("p (s l) -> p s l", s=t.nseg)

        def norm(t, gvec, dest, gkey):
            T = t.T
            ps, kps = psb()
            for dc in range(NDC):
                A(sq[:, dc, 0:T], x[:, dc, 0:T], AF.Square, [("x", dc)], [("hid", dc)])
            MM(ps[:, 0:T], [(onesD[:], sq[:, dc, 0:T]) for dc in range(NDC)],
               [("hid", dc) for dc in range(NDC)] + [C("onesD")], [kps])
            A(rstd[:, 0:T], ps[:, 0:T], AF.Ln, [kps, C("eps")], ["rstd"], bias=c_eps[:])
            A(rstd[:, 0:T], rstd[:, 0:T], AF.Exp, ["rstd"], ["rstd"], scale=-0.5)
            for dc in range(NDC):
                dst, wk = dest(dc)
                STT(dst, v3(x[:, dc, 0:T], t), gvec[:, dc:dc + 1], v3(rstd[:, 0:T], t), ALU.mult, ALU.mult,
                    [("x", dc), "rstd", C(gkey)], [wk])

        def hn_dest(t):
            return lambda dc: (v3(hnT[:, dc, 0:t.T], t), ("hn", dc))

        def pool_layer(t, i):
            j = i // 2
            T, L, ns = t.T, t.L, t.nseg
            W = 16 + L
            load_poolw(j)

            def Ev(dc):
                return E[:, dc, 0:ns * W].rearrange("p (s w) -> p s w", s=ns)

            if t.kind == "p":
                if t.first:
                    MS(E[:, :, 0:16], 0.0, [("hn", dc) for dc in range(NDC)], eng="dve")
                else:
                    CP(E[:, :, 1:16], EhP[:, j, :, 1:16], [("EhP", j)], [("hn", dc) for dc in range(NDC)], eng="dve")
            else:
                for s in range(ns):
                    DMA("sp", ("phist", s), [(phist[:, s, dc, 1:16], d_spool[j, s].rearrange("t (c p) -> p c t", p=128)[:, dc, :]) for dc in range(NDC)],
                        (), [("phist", s)], slow=True)
                    CP(E[:, :, s * W + 1: s * W + 16], phist[:, s, :, 1:16], [("phist", s)],
                       [("hn", dc) for dc in range(NDC)], eng="dve")
            norm(t, c_gmix[:, i, :], lambda dc: (Ev(dc)[:, :, 16:16 + L], ("hn", dc)), "gmix")
            if t.last:
                for s in range(ns):
                    st = 0 if t.kind == "p" else 1 + s
                    for dc in range(NDC):
                        a = s * L + L - 16
                        STT(pout[:, st, j, dc, :], x[:, dc, a:a + 16], c_gmix[:, i, dc:dc + 1], rstd[:, a:a + 16],
                            ALU.mult, ALU.mult, [("x", dc), "rstd"], [("pout", st, j)])
                    dstd = (o_poolp[j] if t.kind == "p" else o_pools[j, s]).rearrange("t (c p) -> p c t", p=128)
                    DMA("sp", "outs", [(dstd[:, dc, :], pout[:, st, j, dc, 1:16]) for dc in range(NDC)], [("pout", st, j)], [], slow=True)
            for dc in range(NDC):
                g = dc // 2
                win = 2 ** (g + 1)
                cur = Ev(dc)
                ck = ("hn", dc)
                sh = 1
                for lev in range(g + 1):
                    lo = 2 * sh
                    nt, kn = tmp(ns * W)
                    nv = nt.rearrange("p (s w) -> p s w", s=ns)
                    TTo(nv[:, :, lo:W], cur[:, :, lo:W], cur[:, :, lo - sh:W - sh], ALU.add, [ck], [kn])
                    cur, ck = nv, kn
                    sh *= 2
                if t.kind == "p" and t.first:
                    TTo(cur[:, 0, 16:32], cur[:, 0, 16:32], corr[:, g, :], ALU.mult, [ck, C("corr")], [ck])
                STT(v3(sq[:, dc, 0:T], t), cur[:, :, 16:W], 1.0 / win, Ev(dc)[:, :, 16:W], ALU.mult, ALU.subtract,
                    [ck, ("hn", dc)], [("hid", dc)])
            if t.kind == "p" and not t.last:
                CP(EhP[:, j, :, 1:16], E[:, :, L + 1:L + 16], [("hn", dc) for dc in range(NDC)], [("EhP", j)], eng="act")
            for ec in range(NDC):
                g, eo = ec // 2, ec % 2
                ps, kps = psb()
                MM(ps[:, 0:T], [(poolw[:, g, kc, eo * 128:(eo + 1) * 128], sq[:, 2 * g + kc, 0:T]) for kc in range(2)],
                   [("hid", 2 * g), ("hid", 2 * g + 1), "poolw"], [kps])
                STT(x[:, ec, 0:T], ps[:, 0:T], c_psc[:, j, ec:ec + 1], x[:, ec, 0:T], ALU.mult, ALU.add,
                    [kps, ("x", ec), C("psc")], [("x", ec)])

        def ffn_layer(t, i):
            T, L, ns = t.T, t.L, t.nseg
            norm(t, c_gffn[:, i, :], hn_dest(t), "gffn")
            if t.kind == "p":
                gh = lambda fc: ghP[:, i, fc, :].rearrange("p (s j) -> p s j", s=1)
                ghk = lambda fc: ("ghP", i, fc)
                if t.first:
                    MS(ghP[:, i, :, :], 0.0, [ghk(fc) for fc in range(NFC)], eng="dve")
            else:
                gh = lambda fc: ghS[:, i, fc, 0:ns, :]
                ghk = lambda fc: ("ghS", i, fc)
                for s in range(ns):
                    DMA("sp", ("ghS", i), [(ghS[:, i, :, s, jj], d_sconv[i, s, jj].rearrange("(c p) -> p c", p=128)) for jj in range(2)],
                        (), [ghk(fc) for fc in range(NFC)], slow=True)
            W = 2 + L
            pend = []

            def ffn_stage2():
                fc_, acc_, kacc_, pv_, kv_ = pend.pop(0)
                sl, ksl = tmp(T)
                A(sl, acc_, AF.Silu, [kacc_], [ksl])
                TTo(hid[:, fc_, 0:T], sl, pv_[:, 0:T], ALU.mult, [ksl, kv_], [("hid", fc_)])

            for fc in range(NFC):
                ws = load_wup(i, fc)
                for fl in range(1):
                    pg, kg = psb()
                    pv, kv = psb()
                    rd = [("hn", dc) for dc in range(NDC)] + [("wup", ws)]
                    MM(pg[:, 0:T], [(wup[:, ws, dc, 0, :], hnT[:, dc, 0:T]) for dc in range(NDC)], rd, [kg])
                    MM(pv[:, 0:T], [(wup[:, ws, dc, 1, :], hnT[:, dc, 0:T]) for dc in range(NDC)], rd, [kv])
                    G, kG = tmp(ns * W)
                    Gv = G.rearrange("p (s w) -> p s w", s=ns)
                    acc, kacc = tmp(T)
                    accv = v3(acc, t)
                    A(Gv[:, :, 2:W], v3(pg[:, 0:T], t), AF.Copy, [kg], [kG])
                    CP(Gv[:, :, 0:2], gh(fc), [ghk(fc)], [kG], eng="dve")
                    A(acc, pg[:, 0:T], AF.Identity, [kg, C("cw"), C("cb")], [kacc],
                      bias=c_cb[:, i, fc:fc + 1], scale=c_cw[:, i, 2, fc:fc + 1])
                    STT(accv, Gv[:, :, 1:1 + L], c_cw[:, i, 1, fc:fc + 1], accv, ALU.mult, ALU.add, [kG, kacc], [kacc])
                    STT(accv, Gv[:, :, 0:L], c_cw[:, i, 0, fc:fc + 1], accv, ALU.mult, ALU.add, [kG, kacc], [kacc])
                    CP(gh(fc), Gv[:, :, L:L + 2], [kG], [ghk(fc)], eng="dve")
                    pend.append((fc, acc, kacc, pv, kv))
                    if len(pend) > 1:
                        ffn_stage2()
            while pend:
                ffn_stage2()
            if t.last:
                for s in range(ns):
                    dstd = (o_convp[i] if t.kind == "p" else o_convs[i, s])
                    src = ghP[:, i, :, :] if t.kind == "p" else ghS[:, i, :, s, :]
                    DMA("sp", "outs", [(dstd[jj].rearrange("(c p) -> p c", p=128), src[:, :, jj]) for jj in range(2)],
                        [ghk(fc) for fc in range(NFC)], [], slow=True)
            for ec in range(NDC):
                ws = load_wdn(i, ec)
                ps, kps = psb()
                MM(ps[:, 0:T], [(wdn[:, ws, fc, :], hid[:, fc, 0:T]) for fc in range(NFC)],
                   [("hid", fc) for fc in range(NFC)] + [("wdn", ws)], [kps])
                TTo(x[:, ec, 0:T], x[:, ec, 0:T], ps[:, 0:T], ALU.add, [kps, ("x", ec)], [("x", ec)])

        def hgrn_layer(t, i):
            j = i // 2
            T, L, ns = t.T, t.L, t.nseg
            nch = T // 64
            npair = T // 128
            norm(t, c_gmix[:, i, :], hn_dest(t), "gmix")
            load_wi(j)
            if t.kind == "p":
                sid = [j] * nch
            else:
                sid = [0 if s == 0 else 2 for s in range(ns)] if j == 0 else [1 if s == 0 else 2 for s in range(ns)]
            if t.kind == "p":
                if t.first:
                    MS(S32[j][:], 0.0, [("S", j, h) for h in range(NH)], eng="dve")
                    MS(Scar[j][:], 0.0, [("Sc", j, h) for h in range(NH)], eng="dve")
            else:
                for s in range(ns):
                    b = sid[s]
                    DMA("sp", ("Sld", b), [(S32[b][:], d_shg[j, s].rearrange("h k v -> k h v"))], (),
                        [("S", b, h) for h in range(NH)])
                    CP(Scar[b][:], S32[b][:], [("S", b, h) for h in range(NH)], [("Sc", b, h) for h in range(NH)], eng="act")
            for tb in range(npair):
                for vh in range(2):
                    ps, kps = psb()
                    MM(ps[:, :], [(hnT[:, dc, tb * 128:(tb + 1) * 128], wi[:, dc, vh * 512:(vh + 1) * 512]) for dc in range(NDC)],
                       [("hn", dc) for dc in range(NDC)] + ["wi"], [kps])
                    CP(V[:, tb, vh * 512:(vh + 1) * 512], ps[:, :], [kps], [("hid", 2 * tb + vh)], eng="act")

            st = {}

            def partA1_pe(h):
                ws = load_wqfg(j, h)
                pq, kq = psb_at(0)
                pf, kf = psb_at(1)
                pgt, kgt = psb_at(2)
                rd = [("hn", dc) for dc in range(NDC)] + [("wqfg", ws)]
                for k, (pp, kk) in enumerate(((pq, kq), (pf, kf), (pgt, kgt))):
                    MM(pp[:, 0:T], [(wqfg[:, ws, dc, k, :], hnT[:, dc, 0:T]) for dc in range(NDC)], rd, [kk])
                st[h] = dict(pq=pq, kq=kq, pf=pf, kf=kf, pgt=pgt, kgt=kgt)

            def partA1_ew(h):
                d = st[h]
                pq, kq, pf, kf, pgt, kgt = d["pq"], d["kq"], d["pf"], d["kf"], d["pgt"], d["kgt"]
                s2, s4 = h % 2, h % 4
                sn, ksn = tmp_at(0, T)
                A(sn, pf[:, 0:T], AF.Sigmoid, [kf], [ksn], scale=-1.0)
                yield
                sgq, ksgq = tmp_at(1, T)
                A(sgq, pq[:, 0:T], AF.Sigmoid, [kq], [ksgq])
                yield
                sgg, ksgg = tmp_at(2, T)
                A(sgg, pgt[:, 0:T], AF.Sigmoid, [kgt], [ksgg])
                yield
                q, kq2 = tmp_at(3, T)
                TTo(q, sgq, pq[:, 0:T], ALU.mult, [ksgq, kq], [kq2])
                yield
                gate, kgate = gateb[:, s4, 0:T], ("gate", s4)
                TTo(gate, sgg, pgt[:, 0:T], ALU.mult, [ksgg, kgt], [kgate])
                yield
                gl, kgl = tmp_at(4, T)
                A(gl, sn, AF.Ln, [ksn, C("oml"), C("one")], [kgl], bias=c_one[:], scale=c_noml[:, j, h:h + 1])
                yield
                b, kb = tmp_at(5, T)
                S.add("dve", lambda e: e.tensor_tensor_scan(out=b, data0=rmask[:, 0:T], data1=gl, initial=0.0,
                                                            op0=ALU.mult, op1=ALU.add), [kgl, C("rmask")], [kb])
                yield
                eb, keb = ebb[:, s2, 0:T], ("eb", s2)
                A(eb, b, AF.Exp, [kb], [keb])
                yield
                enb, kenb = tmp_at(6, T)
                A(enb, b, AF.Exp, [kb], [kenb], scale=-1.0)
                yield
                qt, kqt = qtb[:, s4, 0:T], ("qt", s4)
                TTo(qt, q, eb, ALU.mult, [kq2, keb], [kqt])
                yield
                kt, kkt = ktb[:, s2, 0:T], ("kt", s2)
                STT(kt, sn, c_oml[:, j, h:h + 1], enb, ALU.mult, ALU.mult, [ksn, kenb, C("oml")], [kkt])
                yield
                for c in range(nch):
                    p_, hf = c // 2, c % 2
                    TS(khl[:, s2, hf, p_, hf * 64:(hf + 1) * 64], kt[:, c * 64:(c + 1) * 64], eb[:, c * 64 + 63:c * 64 + 64], None,
                       ALU.mult, None, [kkt, keb, C("khl")], [("khl", s2)])
                    yield
                st[h] = dict(gate=gate, kgate=kgate, qt=qt, kqt=kqt, kt=kt, kkt=kkt, eb=eb, keb=keb)

            def partA2(h):
                d = st[h]
                kt, kkt, eb, keb, qt, kqt = d["kt"], d["kkt"], d["eb"], d["keb"], d["qt"], d["kqt"]
                s2, s3 = h % 2, h % 3
                pT, kpT = psb_at(3, BF16)
                TR([(pT[:, (2 * p_ + hf) * 128:(2 * p_ + hf + 1) * 128], khl[:, s2, hf, p_, :], identb[:])
                    for p_ in range(npair) for hf in range(2)], [("khl", s2), C("identb")], [kpT])
                yield
                kT, kkT = tmp_at(7, 1024, BF16)
                CP(kT[:, 0:nch * 128], pT[:, 0:nch * 128], [kpT], [kkT], eng="act")
                yield
                pA, kpA = psb_at(4)
                MMseq([(pA[:, p_ * 128:(p_ + 1) * 128], [(kt[:, p_ * 128:(p_ + 1) * 128], qt[:, p_ * 128:(p_ + 1) * 128])], True)
                       for p_ in range(npair)], [kkt, kqt], [kpA])
                yield
                TTo(ATs[:, s3, 0:npair, :], pA[:, 0:npair * 128].rearrange("p (a b) -> p a b", b=128), maskbd[:, 0:npair, :], ALU.mult,
                    [kpA, C("maskbd")], [("ATs", s3)])
                yield
                pU = []
                for u0 in range(0, nch, 4):
                    pu, kpu = psb_at(5 + u0 // 4)
                    n = min(4, nch - u0)
                    MMseq([(pu[:, (c - u0) * 128:(c - u0 + 1) * 128], [(kT[:, c * 128:(c + 1) * 128], V[:, c // 2, h * 128:(h + 1) * 128])], True)
                           for c in range(u0, u0 + n)], [kkT] + [("hid", pg_) for pg_ in range(2 * (u0 // 2), 2 * ((u0 + n - 1) // 2) + 2)], [kpu])
                    pU.append((pu, kpu))
                    yield
                for c in range(nch):
                    b_ = sid[c]
                    pu, kpu = pU[c // 4]
                    STT(S32[b_][:, h, :], S32[b_][:, h, :], eb[:, c * 64 + 63:c * 64 + 64], pu[:, (c % 4) * 128:(c % 4 + 1) * 128],
                        ALU.mult, ALU.add, [("S", b_, h), keb, kpu], [("S", b_, h)])
                    yield
                    lastc = (c == nch - 1) or (t.kind == "s")
                    if lastc:
                        CP(Scar2[:, s3, b_, :], S32[b_][:, h, :], [("S", b_, h)], [("Sc2", s3, b_)], eng="act")
                    else:
                        CP(Sv[:, s3, c, :], S32[b_][:, h, :], [("S", b_, h)], [("Sv", s3, c)], eng="act")
                    yield

            def partB1(h):
                d = st[h]
                qt, kqt = d["qt"], d["kqt"]
                s3 = h % 3
                po, kpo = psb_at(7)
                items = []
                rd = [("ATs", s3), kqt]
                for c in range(nch):
                    p_ = c // 2
                    if c % 2 == 0:
                        items.append((po[:, p_ * 128:(p_ + 1) * 128], [(V[:, p_, h * 128:(h + 1) * 128], ATs[:, s3, p_, :])], True))
                        rd += [("hid", 2 * p_), ("hid", 2 * p_ + 1)]
                    b_ = sid[c]
                    if c == 0 or t.kind == "s":
                        lhs = Scar[b_][:, h, :]
                        rd.append(("Sc", b_, h))
                    else:
                        lhs = Sv[:, s3, c - 1, :]
                        rd.append(("Sv", s3, c - 1))
                    items.append((po[:, c * 64:(c + 1) * 64], [(lhs, qt[:, c * 64:(c + 1) * 64])], False))
                MMseq(items, rd, [kpo])
                osq, kosq = tmp_at(8, T, BF16)
                A(osq, po[:, 0:T], AF.Square, [kpo], [kosq])
                d["po"], d["kpo"], d["osq"], d["kosq"] = po, kpo, osq, kosq

            def partB2(h):
                d = st.pop(h)
                po, kpo, osq, kosq, gate, kgate = d["po"], d["kpo"], d["osq"], d["kosq"], d["gate"], d["kgate"]
                s3 = h % 3
                pss, kpss = psb_at(6)
                MM(pss[:, 0:T], [(onesH[:], osq)], [kosq, C("onesH")], [kpss])
                rs, krs = tmp_at(9, T)
                A(rs, pss[:, 0:T], AF.Ln, [kpss, C("eps")], [krs], bias=c_eps[:])
                A(rs, rs, AF.Exp, [krs], [krs], scale=-0.5)
                t1, kt1 = tmp_at(10, T)
                STT(t1, po[:, 0:T], c_gn[:, j, h:h + 1], rs, ALU.mult, ALU.mult, [kpo, krs, C("gn")], [kt1])
                TTo(ofin[:, h, 0:T], t1, gate, ALU.mult, [kt1, kgate], [("hid", 8 + h)])
                for b_ in sorted(set(sid)):
                    CP(Scar[b_][:, h, :], Scar2[:, s3, b_, :], [("Sc2", s3, b_)], [("Sc", b_, h)], eng="act")

            def merge(gens):
                gens = list(gens)
                while gens:
                    for g_ in list(gens):
                        try:
                            next(g_)
                        except StopIteration:
                            gens.remove(g_)

            for k in range(NH + 3):
                if 0 <= k - 3 < NH:
                    partB1(k - 3)
                if k < NH:
                    partA1_pe(k)
                if 0 <= k - 3 < NH:
                    partB2(k - 3)
                gl_ = []
                if 0 <= k - 1 < NH:
                    gl_.append(partA2(k - 1))
                if k < NH:
                    gl_.append(partA1_ew(k))
                merge(gl_)
            if t.last:
                for s in range(ns):
                    b_ = sid[0] if t.kind == "p" else sid[s]
                    dstd = (o_hgp[j] if t.kind == "p" else o_hgs[j, s]).rearrange("h k v -> k h v")
                    DMA("sp", "outs", [(dstd, S32[b_][:])], [("S", b_, h) for h in range(NH)], [])
            for ec in range(NDC):
                ws = load_wout(j, ec)
                ps, kps = psb()
                MM(ps[:, 0:T], [(wout[:, ws, dc, :], ofin[:, dc, 0:T]) for dc in range(NDC)],
                   [("hid", 8 + dc) for dc in range(NDC)] + [("wout", ws)], [kps])
                TTo(x[:, ec, 0:T], x[:, ec, 0:T], ps[:, 0:T], ALU.add, [kps, ("x", ec)], [("x", ec)])


        xin_ctr = [0]

        def load_x(t):
            ntb = t.T // 128
            src = d_xp if t.kind == "p" else d_xs
            r0 = t.k * TT if t.kind == "p" else 0
            for tb in range(ntb):
                s = xin_ctr[0] % 2
                xin_ctr[0] += 1
                DMA("sp", ("io", s), [(xin[:, s, :], src[r0 + tb * 128: r0 + (tb + 1) * 128, :])], (), [("io", s)])
                for hb in range(2):
                    ps, kps = psb()
                    TR([(ps[:, q_ * 128:(q_ + 1) * 128], xin[:, s, (hb * 4 + q_) * 128:(hb * 4 + q_ + 1) * 128], identf[:]) for q_ in range(4)],
                       [("io", s), C("identf")], [kps])
                    CP(x[:, hb * 4:hb * 4 + 4, tb * 128:(tb + 1) * 128], ps[:, :].rearrange("p (a b) -> p a b", b=128), [kps],
                       [("x", hb * 4 + q_) for q_ in range(4)], eng="act")

        yo_ctr = [0]

        def store_y(t):
            T = t.T
            ntb = T // 128
            norm(t, c_gout[:, 0, :], lambda dc: (v3(x[:, dc, 0:T], t), ("x", dc)), "gout")
            dst = o_yp if t.kind == "p" else o_ys
            r0 = t.k * TT if t.kind == "p" else 0
            for tb in range(ntb):
                s = xin_ctr[0] % 2
                xin_ctr[0] += 1
                for hb in range(2):
                    ps, kps = psb()
                    TR([(ps[:, q_ * 128:(q_ + 1) * 128], x[:, hb * 4 + q_, tb * 128:(tb + 1) * 128], identf[:]) for q_ in range(4)],
                       [("x", hb * 4 + q_) for q_ in range(4)] + [C("identf")], [kps])
                    CP(yout[:, s, hb * 512:(hb + 1) * 512], ps[:, :], [kps], [("io", s)], eng="act")
                DMA("sp", ("io", s), [(dst[r0 + tb * 128: r0 + (tb + 1) * 128, :], yout[:, s, :])], [("io", s)], [])

        for ti, t in enumerate(tiles):
            first_pass[0] = (ti == 0)
            load_x(t)
            for i in LY:
                if i % 2 == 0:
                    pool_layer(t, i)
                else:
                    hgrn_layer(t, i)
                ffn_layer(t, i)
            store_y(t)

        S.finalize()
        esems = {e: es.enter_context(nc.semaphore("se_" + e)) for e in Sched.ENGS}
        dsems = {}
        for n, k in enumerate(S.dma_cnt):
            dsems[k] = es.enter_context(nc.semaphore(f"sd_{n}"))
        out_keys = ["outs"] + [("io", s) for s in range(2)]
        with nc.Block() as block:
            @block.tensor
            def _(e):
                S.emit("pe", e, esems, dsems)

            @block.scalar
            def _(e):
                S.emit("act", e, esems, dsems)

            @block.vector
            def _(e):
                S.emit("dve", e, esems, dsems)

            @block.gpsimd
            def _(e):
                S.emit("pool", e, esems, dsems)

            @block.sync
            def _(e):
                S.emit("sp", e, esems, dsems)
                for k in out_keys:
                    if k in S.dma_cnt:
                        e.wait_ge(dsems[k], 16 * S.dma_cnt[k])
    nc._sched = S
    return nc


_NC_CACHE = {}


def run_cores(cfg, in_maps):
    key = (cfg.layers, cfg.npt, cfg.ns)
    if key not in _NC_CACHE:
        _NC_CACHE[key] = build(cfg)
    nc = _NC_CACHE[key]
    return run_bass_kernel_spmd(nc, in_maps, core_ids=list(range(len(in_maps))))


def kernel(x_prompt, x_sample, state_pool, state_hgrn, state_ffn_conv, norm_mix_g, pool_w, pool_scale,
           hgrn_w_in, hgrn_lb_logits, hgrn_norm_g, hgrn_w_out, norm_ffn_g, ffn_w_up, ffn_conv_w,
           ffn_conv_b, ffn_w_down, norm_out_g):
    f = lambda a: np.ascontiguousarray(np.asarray(a, dtype=np.float32))
    x_prompt, x_sample = f(x_prompt), f(x_sample)
    state_pool, state_hgrn, state_ffn_conv = f(state_pool), f(state_hgrn), f(state_ffn_conv)
    BP, SEQ, _ = x_prompt.shape
    NSB = x_sample.shape[0]
    ncores = 8
    nsc = NSB // ncores
    cfg = Cfg(layers=(0, 1, 2, 3), npt=SEQ // TT, ns=nsc)
    shared = {
        "norm_mix_g": f(norm_mix_g), "pool_w": f(pool_w), "pool_scale": f(pool_scale), "hgrn_w_in": f(hgrn_w_in),
        "hgrn_lb_logits": f(hgrn_lb_logits), "hgrn_norm_g": f(hgrn_norm_g), "hgrn_w_out": f(hgrn_w_out),
        "norm_ffn_g": f(norm_ffn_g), "ffn_w_up": f(ffn_w_up), "ffn_conv_w": f(ffn_conv_w), "ffn_conv_b": f(ffn_conv_b),
        "ffn_w_down": f(ffn_w_down), "norm_out_g": f(norm_out_g).reshape(1, D),
    }
    in_maps = []
    for c in range(ncores):
        sl = slice(c * nsc, (c + 1) * nsc)
        m = dict(shared)
        m["xp"] = x_prompt[c % BP]
        m["xs"] = x_sample[sl].reshape(nsc * 64, D)
        m["s_pool"] = np.ascontiguousarray(state_pool[:, sl])
        m["s_hgrn"] = np.ascontiguousarray(state_hgrn[:, sl])
        m["s_conv"] = np.ascontiguousarray(state_ffn_conv[:, sl])
        in_maps.append(m)
    res = run_cores(cfg, in_maps).results
    y_prompt = np.stack([res[b]["yp"] for b in range(BP)])
    y_sample = np.concatenate([res[c]["ys"].reshape(nsc, 64, D) for c in range(ncores)])
    pool_p = np.stack([res[b]["npool_p"] for b in range(BP)], axis=1)
    pool_s = np.concatenate([res[c]["npool_s"] for c in range(ncores)], axis=1)
    hg_p = np.stack([res[b]["nhgrn_p"] for b in range(BP)], axis=1)
    hg_s = np.concatenate([res[c]["nhgrn_s"] for c in range(ncores)], axis=1)
    cv_p = np.stack([res[b]["nconv_p"] for b in range(BP)], axis=1)
    cv_s = np.concatenate([res[c]["nconv_s"] for c in range(ncores)], axis=1)
    return tuple(np.ascontiguousarray(a, dtype=np.float32) for a in (y_prompt, y_sample, pool_p, pool_s, hg_p, hg_s, cv_p, cv_s))
```

```python
import sys
from contextlib import ExitStack
import numpy as np
import concourse.bass as bass
import concourse.mybir as mybir
from concourse.bass_utils import run_bass_kernel_spmd

F32 = mybir.dt.float32
BF16 = mybir.dt.bfloat16
AF = mybir.ActivationFunctionType
ALU = mybir.AluOpType

D = 1024
NDC = 8
DFF = 2816
NFC = 22
NH = 8
EPS = 1e-6
TT = 512
TSLOT = 528
NTMP = 11
import os
SAME_ENG_SYNC = os.environ.get("K_SES", "1") == "1"


class Op:
    __slots__ = ("eng", "fn", "deps", "dma_key", "ndma", "idx", "ms", "dma_val", "is_ms", "label", "ninst")

    def __init__(self, eng, fn, dma_key, ndma):
        self.eng = eng
        self.fn = fn
        self.dma_key = dma_key
        self.ndma = ndma
        self.deps = None
        self.ms = 0
        self.is_ms = False
        self.dma_val = 0


class Sched:
    ENGS = ("pe", "act", "dve", "pool", "sp")

    def __init__(self):
        self.streams = {e: [] for e in self.ENGS}
        self.last_w = {}
        self.readers = {}
        self.dma_cnt = {}
        self.gen = {}

    def fresh(self, kind, slot):
        g = self.gen.get((kind, slot), 0)
        old = (kind, slot, g)
        new = (kind, slot, g + 1)
        self.gen[(kind, slot)] = g + 1
        if old in self.last_w:
            self.last_w[new] = self.last_w.pop(old)
        if old in self.readers:
            self.readers[new] = self.readers.pop(old)
        return new

    def _check(self, r):
        if len(r) == 3 and r[0] in ("t", "ps"):
            assert self.gen.get((r[0], r[1]), 0) == r[2], f"stale resource {r}"

    def add(self, eng, fn, reads=(), writes=(), dma_key=None, ndma=1):
        op = Op(eng, fn, dma_key, ndma)
        try:
            f = sys._getframe(2)
            g = f.f_back
            op.label = f"{g.f_code.co_name}:{g.f_lineno}" if g is not None else f.f_code.co_name
        except Exception:
            op.label = "?"
        op.ninst = 1
        deps = {}

        def dep(o):
            if o is None or o is op:
                return
            k = o.dma_key if o.dma_key is not None else o.eng
            p = deps.get(k)
            if p is None or self._later(o, p):
                deps[k] = o

        for r in reads:
            self._check(r)
            dep(self.last_w.get(r))
        for r in writes:
            self._check(r)
            dep(self.last_w.get(r))
            for o in self.readers.get(r, {}).values():
                dep(o)
        op.idx = len(self.streams[eng])
        self.streams[eng].append(op)
        if dma_key is not None:
            c = self.dma_cnt.get(dma_key, 0) + ndma
            self.dma_cnt[dma_key] = c
            op.dma_val = 16 * c
        for r in reads:
            d = self.readers.setdefault(r, {})
            k = dma_key if dma_key is not None else eng
            d[k] = op
        for r in writes:
            self.last_w[r] = op
            self.readers[r] = {}
        op.deps = list(deps.values())
        return op

    @staticmethod
    def _later(a, b):
        if a.dma_key is not None:
            return a.dma_val > b.dma_val
        return a.idx > b.idx

    def finalize(self):
        for e in self.ENGS:
            for op in self.streams[e]:
                for d in op.deps:
                    if d.dma_key is None:
                        if d.eng == op.eng and (d.eng == "pe" or not SAME_ENG_SYNC):
                            continue
                        d.is_ms = True
        for e in self.ENGS:
            c = 0
            for op in self.streams[e]:
                if op.is_ms:
                    c += 1
                    op.ms = c

    def emit(self, eng_name, eng, esems, dsems):
        waited = {}
        for op in self.streams[eng_name]:
            for d in op.deps:
                if d.dma_key is not None:
                    key, val, sem = ("d", d.dma_key), d.dma_val, dsems[d.dma_key]
                else:
                    if d.eng == op.eng and (d.eng == "pe" or not SAME_ENG_SYNC):
                        continue
                    key, val, sem = ("e", d.eng), d.ms, esems[d.eng]
                if waited.get(key, 0) >= val:
                    continue
                waited[key] = val
                eng.wait_ge(sem, val)
            if op.dma_key is not None:
                op.fn(eng, dsems[op.dma_key])
            else:
                ins = op.fn(eng)
                if op.is_ms:
                    ins.then_inc(esems[eng_name], 1)


class Cfg:
    def __init__(self, layers=(0, 1, 2, 3), npt=16, ns=2):
        self.layers = tuple(layers)
        self.npt = npt
        self.ns = ns


def build(cfg):
    nc = bass.Bass("TRN2", target_bir_lowering=False)
    LY = cfg.layers
    NL = len(LY)
    NPT = cfg.npt
    NS = cfg.ns
    SEQP = NPT * TT
    pool_layers = [i for i in LY if i % 2 == 0]
    hg_layers = [i for i in LY if i % 2 == 1]
    NPL = max(1, len(pool_layers))
    NHL = max(1, len(hg_layers))

    def dt(name, shape, kind):
        return nc.dram_tensor(name, list(shape), F32, kind=kind).ap()

    I, O = "ExternalInput", "ExternalOutput"
    d_xp = dt("xp", [max(SEQP, 1), D], I)
    d_xs = dt("xs", [max(NS, 1) * 64, D], I)
    d_spool = dt("s_pool", [2, max(NS, 1), 15, D], I)
    d_shg = dt("s_hgrn", [2, max(NS, 1), NH, 128, 128], I)
    d_sconv = dt("s_conv", [4, max(NS, 1), 2, DFF], I)
    d_gmix = dt("norm_mix_g", [4, D], I)
    d_poolw = dt("pool_w", [2, 4, 256, 256], I)
    d_pscale = dt("pool_scale", [2, D], I)
    d_win = dt("hgrn_w_in", [2, D, 4 * D], I)
    d_lbl = dt("hgrn_lb_logits", [4, D], I)
    d_gn = dt("hgrn_norm_g", [2, D], I)
    d_wout = dt("hgrn_w_out", [2, D, D], I)
    d_gffn = dt("norm_ffn_g", [4, D], I)
    d_wup = dt("ffn_w_up", [4, D, 2 * DFF], I)
    d_cw = dt("ffn_conv_w", [4, 3, DFF], I)
    d_cb = dt("ffn_conv_b", [4, DFF], I)
    d_wdn = dt("ffn_w_down", [4, DFF, D], I)
    d_gout = dt("norm_out_g", [1, D], I)
    o_yp = dt("yp", [max(SEQP, 1), D], O)
    o_ys = dt("ys", [max(NS, 1) * 64, D], O)
    o_poolp = dt("npool_p", [2, 15, D], O)
    o_pools = dt("npool_s", [2, max(NS, 1), 15, D], O)
    o_hgp = dt("nhgrn_p", [2, NH, 128, 128], O)
    o_hgs = dt("nhgrn_s", [2, max(NS, 1), NH, 128, 128], O)
    o_convp = dt("nconv_p", [4, 2, DFF], O)
    o_convs = dt("nconv_s", [4, max(NS, 1), 2, DFF], O)

    S = Sched()
    es = ExitStack()
    with es:
        def sb(name, shape, dtype=F32):
            return es.enter_context(nc.sbuf_tensor(name, list(shape), dtype))

        x = sb("x", [128, NDC, TT])
        E = sb("E", [128, NDC, 16 + TT], BF16)
        hnT = E
        rstd = sb("rstd", [128, TT])
        hid = sb("hid", [128, NFC, TT], BF16)
        sq = hid
        V = hid[:, 0:8, :].rearrange("p (a b) t -> p a (b t)", b=2)
        ofin = hid[:, 8:16, :]
        gateb = sb("gateb", [128, 4, TT], BF16)
        qtb = sb("qtb", [128, 4, TT], BF16)
        ktb = sb("ktb", [128, 2, TT], BF16)
        ebb = sb("ebb", [128, 2, TT])
        temps = sb("temps", [128, NTMP * TSLOT])
        wup = sb("wup", [128, 3, NDC, 2, 128], BF16)
        wdn = sb("wdn", [128, 3, NFC, 128], BF16)
        wqfg = sb("wqfg", [128, 2, NDC, 3, 128], BF16)
        wi = sb("wi", [128, NDC, D], BF16)
        wout = sb("wout", [128, 2, NDC, 128], BF16)
        poolw = sb("poolw", [128, 4, 2, 256], BF16)
        S32 = [sb(f"S32_{k}", [128, NH, 128]) for k in range(3)]
        Scar = [sb(f"Scar_{k}", [128, NH, 128], BF16) for k in range(3)]
        Scar2 = sb("Scar2", [128, 3, 3, 128], BF16)
        Sv = sb("Sv", [128, 3, 8, 128], BF16)
        khl = sb("khl", [128, 2, 2, 4, 128], BF16)
        ATs = sb("ATs", [128, 3, 4, 128], BF16)
        xin = sb("xin", [128, 2, D])
        yout = xin
        identb = sb("identb", [128, 128], BF16)
        identf = sb("identf", [128, 128])
        onesD = sb("onesD", [128, 128], BF16)
        onesH = sb("onesH", [128, 128], BF16)
        maskbd = sb("maskbd", [128, 4, 128], BF16)
        rmask = sb("rmask", [128, TT])
        corr = sb("corr", [128, 4, 16])
        c_gmix = sb("c_gmix", [128, 4, NDC])
        c_gffn = sb("c_gffn", [128, 4, NDC])
        c_gout = sb("c_gout", [128, 1, NDC])
        c_psc = sb("c_psc", [128, 2, NDC])
        c_gn = sb("c_gn", [128, 2, NDC])
        c_lbl = sb("c_lbl", [128, 4, NDC])
        c_lb = sb("c_lb", [128, 4, NDC])
        c_oml = sb("c_oml", [128, 2, NDC])
        c_noml = sb("c_noml", [128, 2, NDC])
        c_cw = sb("c_cw", [128, 4, 3, NFC])
        c_cb = sb("c_cb", [128, 4, NFC])
        c_eps = sb("c_eps", [128, 1])
        c_one = sb("c_one", [128, 1])
        ghP = sb("ghP", [128, 4, NFC, 2])
        ghS = sb("ghS", [128, 4, NFC, 2, 2])
        EhP = sb("EhP", [128, 2, NDC, 16], BF16)
        phist = sb("phist", [128, 2, NDC, 16])
        pout = sb("pout", [128, 3, 2, NDC, 16])
        psum = es.enter_context(nc.psum_tensor("psum", [128, 8, 512], F32))

        tmp_ctr = [0]
        ps_ctr = [0]

        def tmp(n=TT, dtype=F32):
            s = tmp_ctr[0] % NTMP
            tmp_ctr[0] += 1
            key = S.fresh("t", s)
            ap = temps[:, s * TSLOT:(s + 1) * TSLOT]
            if dtype == BF16:
                ap = ap.bitcast(BF16)
            return ap[:, 0:n], key

        def tmp_at(s, n=TT, dtype=F32):
            key = S.fresh("t", s)
            ap = temps[:, s * TSLOT:(s + 1) * TSLOT]
            if dtype == BF16:
                ap = ap.bitcast(BF16)
            return ap[:, 0:n], key

        def psb_at(b, dtype=F32):
            key = S.fresh("ps", b)
            ap = psum[:, b, :]
            if dtype == BF16:
                ap = ap.bitcast(BF16)
            return ap, key

        def psb(dtype=F32):
            b = ps_ctr[0] % 8
            ps_ctr[0] += 1
            key = S.fresh("ps", b)
            ap = psum[:, b, :]
            if dtype == BF16:
                ap = ap.bitcast(BF16)
            return ap, key

        def A(out, in_, func, r, w, bias=None, scale=None):
            kw = {}
            if bias is not None:
                kw["bias"] = bias
            if scale is not None:
                kw["scale"] = scale
            S.add("act", lambda e: e.activation(out=out, in_=in_, func=func, **kw), r, w)

        def TS(out, in0, s1, s2, op0, op1, r, w, eng="dve"):
            if s2 is None:
                S.add(eng, lambda e: e.tensor_scalar(out=out, in0=in0, scalar1=s1, scalar2=None, op0=op0), r, w)
            else:
                S.add(eng, lambda e: e.tensor_scalar(out=out, in0=in0, scalar1=s1, scalar2=s2, op0=op0, op1=op1), r, w)

        def STT(out, in0, sc, in1, op0, op1, r, w):
            S.add("dve", lambda e: e.scalar_tensor_tensor(out=out, in0=in0, scalar=sc, in1=in1, op0=op0, op1=op1), r, w)

        def TTo(out, in0, in1, op, r, w, eng="dve"):
            S.add(eng, lambda e: e.tensor_tensor(out=out, in0=in0, in1=in1, op=op), r, w)

        def CP(out, in_, r, w, eng="dve"):
            if eng == "act":
                S.add(eng, lambda e: e.activation(out=out, in_=in_, func=AF.Copy), r, w)
            else:
                S.add(eng, lambda e: e.tensor_copy(out=out, in_=in_), r, w)

        def MS(ap, val, w, eng="pool"):
            S.add(eng, lambda e: e.memset(ap, val), (), w)

        def RECIP(out, in_, r, w):
            S.add("dve", lambda e: e.reciprocal(out=out, in_=in_), r, w)

        def MM(out, pairs, r, w):
            def fn(e):
                n = len(pairs)
                ins = None
                for i, (l, rr) in enumerate(pairs):
                    ins = e.matmul(out, lhsT=l, rhs=rr, start=(i == 0), stop=(i == n - 1))
                return ins
            S.add("pe", fn, r, w).ninst = len(pairs)

        def MMseq(items, r, w):
            def fn(e):
                ins = None
                for it_ in items:
                    out, pairs, st = it_[0], it_[1], it_[2]
                    sp = it_[3] if len(it_) > 3 else True
                    n = len(pairs)
                    for i, (l, rr) in enumerate(pairs):
                        ins = e.matmul(out, lhsT=l, rhs=rr, start=(st and i == 0), stop=(sp and i == n - 1))
                return ins
            S.add("pe", fn, r, w).ninst = sum(len(it_[1]) for it_ in items)

        def TR(items, r, w):
            def fn(e):
                ins = None
                for out, in_, ident in items:
                    ins = e.transpose(out=out, in_=in_, identity=ident)
                return ins
            S.add("pe", fn, r, w).ninst = len(items)

        def DMA(eng, key, pairs, r, w, slow=False):
            def fn(e, sem):
                for out, in_ in pairs:
                    if slow:
                        e.dma_start(out=out, in_=in_, allow_slow_non_contiguous=True).then_inc(sem, 16)
                    else:
                        e.dma_start(out=out, in_=in_).then_inc(sem, 16)
            S.add(eng, fn, r, w, dma_key=key, ndma=len(pairs))

        C = lambda n: ("c", n)

        MS(identb[:], 0.0, [C("identb")])
        S.add("pool", lambda e: e.affine_select(out=identb[:], in_=identb[:], pattern=[[-1, 128]], compare_op=ALU.not_equal,
                                                fill=1.0, base=0, channel_multiplier=1), [C("identb")], [C("identb")])
        MS(identf[:], 0.0, [C("identf")])
        S.add("pool", lambda e: e.affine_select(out=identf[:], in_=identf[:], pattern=[[-1, 128]], compare_op=ALU.not_equal,
                                                fill=1.0, base=0, channel_multiplier=1), [C("identf")], [C("identf")])
        MS(onesD[:], 1.0 / D, [C("onesD")])
        MS(onesH[:], 1.0 / 128, [C("onesH")])
        MS(c_eps[:], EPS, [C("eps")])
        MS(c_one[:], 1.0, [C("one")])
        MS(maskbd[:], 1.0, [C("maskbd")])
        S.add("pool", lambda e: e.affine_select(out=maskbd[:], in_=maskbd[:], pattern=[[0, 4], [1, 128]], compare_op=ALU.is_ge,
                                                fill=0.0, base=0, channel_multiplier=-1), [C("maskbd")], [C("maskbd")])
        MS(maskbd[0:64, :, 64:128], 0.0, [C("maskbd")])
        MS(rmask[:], 1.0, [C("rmask")])
        MS(rmask[:].rearrange("p (c l) -> p c l", l=64)[:, :, 0:1], 0.0, [C("rmask")])
        MS(corr[:], 1.0, [C("corr")])
        for g in range(4):
            win = 2 ** (g + 1)
            for t in range(win - 1):
                MS(corr[:, g, t:t + 1], float(win) / float(t + 1), [C("corr")])
        MS(khl[:], 0.0, [C("khl")])
        def vload(dst, src, name):
            DMA("sp", "setup", [(dst, src)], (), [C(name)], slow=True)
        def vload2(dst, src, nl, name):
            DMA("sp", "setup_" + name, [(dst[:, l, :], src[l].rearrange("(c p) -> p c", p=128)) for l in range(nl)], (), [C(name)], slow=True)
        vload2(c_gmix, d_gmix, 4, "gmix")
        vload2(c_gffn, d_gffn, 4, "gffn")
        vload2(c_gout, d_gout, 1, "gout")
        vload2(c_psc, d_pscale, 2, "psc")
        vload2(c_gn, d_gn, 2, "gn")
        vload2(c_lbl, d_lbl, 4, "lbl")
        vload2(c_cb, d_cb, 4, "cb")
        for l in range(4):
            vload2(c_cw[:, l], d_cw[l], 3, "cw")
        A(c_lb[:], c_lbl[:], AF.Exp, [C("lbl")], [C("lb")])
        ssum, k_ssum = tmp(NDC)
        TTo(ssum, c_lb[:, 0, :], c_lb[:, 1, :], ALU.add, [C("lb")], [k_ssum])
        TTo(ssum, ssum, c_lb[:, 2, :], ALU.add, [C("lb"), k_ssum], [k_ssum])
        TTo(ssum, ssum, c_lb[:, 3, :], ALU.add, [C("lb"), k_ssum], [k_ssum])
        RECIP(ssum, ssum, [k_ssum], [k_ssum])
        num, k_num = tmp(NDC)
        TTo(c_noml[:, 0, :], c_lb[:, 1, :], ssum, ALU.mult, [C("lb"), k_ssum], [C("oml")])
        TTo(num, c_lb[:, 1, :], c_lb[:, 2, :], ALU.add, [C("lb")], [k_num])
        TTo(num, num, c_lb[:, 3, :], ALU.add, [C("lb"), k_num], [k_num])
        TTo(c_noml[:, 1, :], num, ssum, ALU.mult, [k_num, k_ssum, C("oml")], [C("oml")])
        TS(c_oml[:], c_noml[:], -1.0, 1.0, ALU.mult, ALU.add, [C("oml")], [C("oml")])
        TS(c_noml[:], c_noml[:], -1.0, None, ALU.add, None, [C("oml")], [C("oml")])

        wslot = {"wup": 0, "wdn": 0, "wqfg": 0, "wout": 0}
        first_pass = [True]

        def scr(name, shape):
            return nc.dram_tensor(name, list(shape), BF16, kind="Internal").ap()

        scr_wup = scr("scr_wup", [4, NFC, 128, NDC * 2 * 128])
        scr_wdn = scr("scr_wdn", [4, NDC, 128, NFC * 128])
        scr_wqfg = scr("scr_wqfg", [2, NH, 128, NDC * 3 * 128])
        scr_wi = scr("scr_wi", [2, 128, NDC * D])
        scr_wout = scr("scr_wout", [2, NDC, 128, NDC * 128])
        scr_poolw = scr("scr_poolw", [2, 128, 4 * 2 * 256])

        def wload(slot_key, slot2d, scr2d, scr_key, cast_pairs):
            if first_pass[0]:
                DMA("pool", slot_key, cast_pairs, (), [slot_key])
                sk = ("st",) + (slot_key if isinstance(slot_key, tuple) else (slot_key,))
                DMA("sp", sk, [(scr2d, slot2d)], [slot_key], [scr_key])
            else:
                DMA(os.environ.get("K_WQ", "pool"), slot_key, [(slot2d, scr2d)], [scr_key], [slot_key])

        def load_wup(i, fc):
            s = wslot["wup"] % 3
            wslot["wup"] += 1
            src = d_wup[i].rearrange("(c p) n -> p c n", p=128)
            c0 = fc * 128
            wload(("wup", s), wup[:, s].rearrange("p c k n -> p (c k n)"), scr_wup[i, fc], ("scr", "wup", i, fc),
                  [(wup[:, s, :, 0, :], src[:, :, c0:c0 + 128]), (wup[:, s, :, 1, :], src[:, :, DFF + c0:DFF + c0 + 128])])
            return s

        def load_wdn(i, ec):
            s = wslot["wdn"] % 3
            wslot["wdn"] += 1
            src = d_wdn[i].rearrange("(c p) n -> p c n", p=128)
            wload(("wdn", s), wdn[:, s].rearrange("p c n -> p (c n)"), scr_wdn[i, ec], ("scr", "wdn", i, ec),
                  [(wdn[:, s, :, :], src[:, :, ec * 128:(ec + 1) * 128])])
            return s

        def load_wqfg(j, h):
            s = wslot["wqfg"] % 2
            wslot["wqfg"] += 1
            src = d_win[j].rearrange("(c p) n -> p c n", p=128)
            prs = []
            for k, base in enumerate((0, D, 3 * D)):
                prs.append((wqfg[:, s, :, k, :], src[:, :, base + h * 128: base + (h + 1) * 128]))
            wload(("wqfg", s), wqfg[:, s].rearrange("p c k n -> p (c k n)"), scr_wqfg[j, h], ("scr", "wqfg", j, h), prs)
            return s

        def load_wi(j):
            src = d_win[j].rearrange("(c p) n -> p c n", p=128)
            wload("wi", wi[:, :, :].rearrange("p c n -> p (c n)"), scr_wi[j], ("scr", "wi", j),
                  [(wi[:, :, :], src[:, :, 2 * D:3 * D])])

        def load_wout(j, ec):
            s = wslot["wout"] % 2
            wslot["wout"] += 1
            src = d_wout[j].rearrange("(c p) n -> p c n", p=128)
            wload(("wout", s), wout[:, s].rearrange("p c n -> p (c n)"), scr_wout[j, ec], ("scr", "wout", j, ec),
                  [(wout[:, s, :, :], src[:, :, ec * 128:(ec + 1) * 128])])
            return s

        def load_poolw(j):
            src = d_poolw[j].rearrange("g (k p) n -> p g k n", p=128)
            wload("poolw", poolw[:, :, :, :].rearrange("p g k n -> p (g k n)"), scr_poolw[j], ("scr", "poolw", j),
                  [(poolw[:, :, :, :], src)])

        class Tile:
            pass

        tiles = []
        for k in range(NPT):
            t = Tile()
            t.kind, t.k, t.nseg, t.L, t.T = "p", k, 1, TT, TT
            t.first, t.last = (k == 0), (k == NPT - 1)
            tiles.append(t)
        if NS > 0:
            t = Tile()
            t.kind, t.k, t.nseg, t.L, t.T = "s", 0, NS, 64, NS * 64
            t.first, t.last = True, True
            tiles.append(t)

        def v3(ap2d, t):
            return ap2d.rearrange("p (s l) -> p s l", s=t.nseg)

        def norm(t, gvec, dest, gkey):
            T = t.T
            ps, kps = psb()
            for dc in range(NDC):
                A(sq[:, dc, 0:T], x[:, dc, 0:T], AF.Square, [("x", dc)], [("hid", dc)])
            MM(ps[:, 0:T], [(onesD[:], sq[:, dc, 0:T]) for dc in range(NDC)],
               [("hid", dc) for dc in range(NDC)] + [C("onesD")], [kps])
            A(rstd[:, 0:T], ps[:, 0:T], AF.Ln, [kps, C("eps")], ["rstd"], bias=c_eps[:])
            A(rstd[:, 0:T], rstd[:, 0:T], AF.Exp, ["rstd"], ["rstd"], scale=-0.5)
            for dc in range(NDC):
                dst, wk = dest(dc)
                STT(dst, v3(x[:, dc, 0:T], t), gvec[:, dc:dc + 1], v3(rstd[:, 0:T], t), ALU.mult, ALU.mult,
                    [("x", dc), "rstd", C(gkey)], [wk])

        def hn_dest(t):
            return lambda dc: (v3(hnT[:, dc, 0:t.T], t), ("hn", dc))

        def pool_layer(t, i):
            j = i // 2
            T, L, ns = t.T, t.L, t.nseg
            W = 16 + L
            load_poolw(j)

            def Ev(dc):
                return E[:, dc, 0:ns * W].rearrange("p (s w) -> p s w", s=ns)

            if t.kind == "p":
                if t.first:
                    MS(E[:, :, 0:16], 0.0, [("hn", dc) for dc in range(NDC)], eng="dve")
                else:
                    CP(E[:, :, 1:16], EhP[:, j, :, 1:16], [("EhP", j)], [("hn", dc) for dc in range(NDC)], eng="dve")
            else:
                for s in range(ns):
                    DMA("sp", ("phist", s), [(phist[:, s, dc, 1:16], d_spool[j, s].rearrange("t (c p) -> p c t", p=128)[:, dc, :]) for dc in range(NDC)],
                        (), [("phist", s)], slow=True)
                    CP(E[:, :, s * W + 1: s * W + 16], phist[:, s, :, 1:16], [("phist", s)],
                       [("hn", dc) for dc in range(NDC)], eng="dve")
            norm(t, c_gmix[:, i, :], lambda dc: (Ev(dc)[:, :, 16:16 + L], ("hn", dc)), "gmix")
            if t.last:
                for s in range(ns):
                    st = 0 if t.kind == "p" else 1 + s
                    for dc in range(NDC):
                        a = s * L + L - 16
                        STT(pout[:, st, j, dc, :], x[:, dc, a:a + 16], c_gmix[:, i, dc:dc + 1], rstd[:, a:a + 16],
                            ALU.mult, ALU.mult, [("x", dc), "rstd"], [("pout", st, j)])
                    dstd = (o_poolp[j] if t.kind == "p" else o_pools[j, s]).rearrange("t (c p) -> p c t", p=128)
                    DMA("sp", "outs", [(dstd[:, dc, :], pout[:, st, j, dc, 1:16]) for dc in range(NDC)], [("pout", st, j)], [], slow=True)
            for dc in range(NDC):
                g = dc // 2
                win = 2 ** (g + 1)
                cur = Ev(dc)
                ck = ("hn", dc)
                sh = 1
                for lev in range(g + 1):
                    lo = 2 * sh
                    nt, kn = tmp(ns * W)
                    nv = nt.rearrange("p (s w) -> p s w", s=ns)
                    TTo(nv[:, :, lo:W], cur[:, :, lo:W], cur[:, :, lo - sh:W - sh], ALU.add, [ck], [kn])
                    cur, ck = nv, kn
                    sh *= 2
                if t.kind == "p" and t.first:
                    TTo(cur[:, 0, 16:32], cur[:, 0, 16:32], corr[:, g, :], ALU.mult, [ck, C("corr")], [ck])
                STT(v3(sq[:, dc, 0:T], t), cur[:, :, 16:W], 1.0 / win, Ev(dc)[:, :, 16:W], ALU.mult, ALU.subtract,
                    [ck, ("hn", dc)], [("hid", dc)])
            if t.kind == "p" and not t.last:
                CP(EhP[:, j, :, 1:16], E[:, :, L + 1:L + 16], [("hn", dc) for dc in range(NDC)], [("EhP", j)], eng="act")
            for ec in range(NDC):
                g, eo = ec // 2, ec % 2
                ps, kps = psb()
                MM(ps[:, 0:T], [(poolw[:, g, kc, eo * 128:(eo + 1) * 128], sq[:, 2 * g + kc, 0:T]) for kc in range(2)],
                   [("hid", 2 * g), ("hid", 2 * g + 1), "poolw"], [kps])
                STT(x[:, ec, 0:T], ps[:, 0:T], c_psc[:, j, ec:ec + 1], x[:, ec, 0:T], ALU.mult, ALU.add,
                    [kps, ("x", ec), C("psc")], [("x", ec)])

        def ffn_layer(t, i):
            T, L, ns = t.T, t.L, t.nseg
            norm(t, c_gffn[:, i, :], hn_dest(t), "gffn")
            if t.kind == "p":
                gh = lambda fc: ghP[:, i, fc, :].rearrange("p (s j) -> p s j", s=1)
                ghk = lambda fc: ("ghP", i, fc)
                if t.first:
                    MS(ghP[:, i, :, :], 0.0, [ghk(fc) for fc in range(NFC)], eng="dve")
            else:
                gh = lambda fc: ghS[:, i, fc, 0:ns, :]
                ghk = lambda fc: ("ghS", i, fc)
                for s in range(ns):
                    DMA("sp", ("ghS", i), [(ghS[:, i, :, s, jj], d_sconv[i, s, jj].rearrange("(c p) -> p c", p=128)) for jj in range(2)],
                        (), [ghk(fc) for fc in range(NFC)], slow=True)
            W = 2 + L
            pend = []

            def ffn_stage2():
                fc_, acc_, kacc_, pv_, kv_ = pend.pop(0)
                sl, ksl = tmp(T)
                A(sl, acc_, AF.Silu, [kacc_], [ksl])
                TTo(hid[:, fc_, 0:T], sl, pv_[:, 0:T], ALU.mult, [ksl, kv_], [("hid", fc_)])

            for fc in range(NFC):
                ws = load_wup(i, fc)
                for fl in range(1):
                    pg, kg = psb()
                    pv, kv = psb()
                    rd = [("hn", dc) for dc in range(NDC)] + [("wup", ws)]
                    MM(pg[:, 0:T], [(wup[:, ws, dc, 0, :], hnT[:, dc, 0:T]) for dc in range(NDC)], rd, [kg])
                    MM(pv[:, 0:T], [(wup[:, ws, dc, 1, :], hnT[:, dc, 0:T]) for dc in range(NDC)], rd, [kv])
                    G, kG = tmp(ns * W)
                    Gv = G.rearrange("p (s w) -> p s w", s=ns)
                    acc, kacc = tmp(T)
                    accv = v3(acc, t)
                    A(Gv[:, :, 2:W], v3(pg[:, 0:T], t), AF.Copy, [kg], [kG])
                    CP(Gv[:, :, 0:2], gh(fc), [ghk(fc)], [kG], eng="dve")
                    A(acc, pg[:, 0:T], AF.Identity, [kg, C("cw"), C("cb")], [kacc],
                      bias=c_cb[:, i, fc:fc + 1], scale=c_cw[:, i, 2, fc:fc + 1])
                    STT(accv, Gv[:, :, 1:1 + L], c_cw[:, i, 1, fc:fc + 1], accv, ALU.mult, ALU.add, [kG, kacc], [kacc])
                    STT(accv, Gv[:, :, 0:L], c_cw[:, i, 0, fc:fc + 1], accv, ALU.mult, ALU.add, [kG, kacc], [kacc])
                    CP(gh(fc), Gv[:, :, L:L + 2], [kG], [ghk(fc)], eng="dve")
                    pend.append((fc, acc, kacc, pv, kv))
                    if len(pend) > 1:
                        ffn_stage2()
            while pend:
                ffn_stage2()
            if t.last:
                for s in range(ns):
                    dstd = (o_convp[i] if t.kind == "p" else o_convs[i, s])
                    src = ghP[:, i, :, :] if t.kind == "p" else ghS[:, i, :, s, :]
                    DMA("sp", "outs", [(dstd[jj].rearrange("(c p) -> p c", p=128), src[:, :, jj]) for jj in range(2)],
                        [ghk(fc) for fc in range(NFC)], [], slow=True)
            for ec in range(NDC):
                ws = load_wdn(i, ec)
                ps, kps = psb()
                MM(ps[:, 0:T], [(wdn[:, ws, fc, :], hid[:, fc, 0:T]) for fc in range(NFC)],
                   [("hid", fc) for fc in range(NFC)] + [("wdn", ws)], [kps])
                TTo(x[:, ec, 0:T], x[:, ec, 0:T], ps[:, 0:T], ALU.add, [kps, ("x", ec)], [("x", ec)])

        def hgrn_layer(t, i):
            j = i // 2
            T, L, ns = t.T, t.L, t.nseg
            nch = T // 64
            npair = T // 128
            norm(t, c_gmix[:, i, :], hn_dest(t), "gmix")
            load_wi(j)
            if t.kind == "p":
                sid = [j] * nch
            else:
                sid = [0 if s == 0 else 2 for s in range(ns)] if j == 0 else [1 if s == 0 else 2 for s in range(ns)]
            if t.kind == "p":
                if t.first:
                    MS(S32[j][:], 0.0, [("S", j, h) for h in range(NH)], eng="dve")
                    MS(Scar[j][:], 0.0, [("Sc", j, h) for h in range(NH)], eng="dve")
            else:
                for s in range(ns):
                    b = sid[s]
                    DMA("sp", ("Sld", b), [(S32[b][:], d_shg[j, s].rearrange("h k v -> k h v"))], (),
                        [("S", b, h) for h in range(NH)])
                    CP(Scar[b][:], S32[b][:], [("S", b, h) for h in range(NH)], [("Sc", b, h) for h in range(NH)], eng="act")
            for tb in range(npair):
                for vh in range(2):
                    ps, kps = psb()
                    MM(ps[:, :], [(hnT[:, dc, tb * 128:(tb + 1) * 128], wi[:, dc, vh * 512:(vh + 1) * 512]) for dc in range(NDC)],
                       [("hn", dc) for dc in range(NDC)] + ["wi"], [kps])
                    CP(V[:, tb, vh * 512:(vh + 1) * 512], ps[:, :], [kps], [("hid", 2 * tb + vh)], eng="act")

            st = {}

            def partA1_pe(h):
                ws = load_wqfg(j, h)
                pq, kq = psb_at(0)
                pf, kf = psb_at(1)
                pgt, kgt = psb_at(2)
                rd = [("hn", dc) for dc in range(NDC)] + [("wqfg", ws)]
                for k, (pp, kk) in enumerate(((pq, kq), (pf, kf), (pgt, kgt))):
                    MM(pp[:, 0:T], [(wqfg[:, ws, dc, k, :], hnT[:, dc, 0:T]) for dc in range(NDC)], rd, [kk])
                st[h] = dict(pq=pq, kq=kq, pf=pf, kf=kf, pgt=pgt, kgt=kgt)

            def partA1_ew(h):
                d = st[h]
                pq, kq, pf, kf, pgt, kgt = d["pq"], d["kq"], d["pf"], d["kf"], d["pgt"], d["kgt"]
                s2, s4 = h % 2, h % 4
                sn, ksn = tmp_at(0, T)
                A(sn, pf[:, 0:T], AF.Sigmoid, [kf], [ksn], scale=-1.0)
                yield
                sgq, ksgq = tmp_at(1, T)
                A(sgq, pq[:, 0:T], AF.Sigmoid, [kq], [ksgq])
                yield
                sgg, ksgg = tmp_at(2, T)
                A(sgg, pgt[:, 0:T], AF.Sigmoid, [kgt], [ksgg])
                yield
                q, kq2 = tmp_at(3, T)
                TTo(q, sgq, pq[:, 0:T], ALU.mult, [ksgq, kq], [kq2])
                yield
                gate, kgate = gateb[:, s4, 0:T], ("gate", s4)
                TTo(gate, sgg, pgt[:, 0:T], ALU.mult, [ksgg, kgt], [kgate])
                yield
                gl, kgl = tmp_at(4, T)
                A(gl, sn, AF.Ln, [ksn, C("oml"), C("one")], [kgl], bias=c_one[:], scale=c_noml[:, j, h:h + 1])
                yield
                b, kb = tmp_at(5, T)
                S.add("dve", lambda e: e.tensor_tensor_scan(out=b, data0=rmask[:, 0:T], data1=gl, initial=0.0,
                                                            op0=ALU.mult, op1=ALU.add), [kgl, C("rmask")], [kb])
                yield
                eb, keb = ebb[:, s2, 0:T], ("eb", s2)
                A(eb, b, AF.Exp, [kb], [keb])
                yield
                enb, kenb = tmp_at(6, T)
                A(enb, b, AF.Exp, [kb], [kenb], scale=-1.0)
                yield
                qt, kqt = qtb[:, s4, 0:T], ("qt", s4)
                TTo(qt, q, eb, ALU.mult, [kq2, keb], [kqt])
                yield
                kt, kkt = ktb[:, s2, 0:T], ("kt", s2)
                STT(kt, sn, c_oml[:, j, h:h + 1], enb, ALU.mult, ALU.mult, [ksn, kenb, C("oml")], [kkt])
                yield
                for c in range(nch):
                    p_, hf = c // 2, c % 2
                    TS(khl[:, s2, hf, p_, hf * 64:(hf + 1) * 64], kt[:, c * 64:(c + 1) * 64], eb[:, c * 64 + 63:c * 64 + 64], None,
                       ALU.mult, None, [kkt, keb, C("khl")], [("khl", s2)])
                    yield
                st[h] = dict(gate=gate, kgate=kgate, qt=qt, kqt=kqt, kt=kt, kkt=kkt, eb=eb, keb=keb)

            def partA2(h):
                d = st[h]
                kt, kkt, eb, keb, qt, kqt = d["kt"], d["kkt"], d["eb"], d["keb"], d["qt"], d["kqt"]
                s2, s3 = h % 2, h % 3
                pT, kpT = psb_at(3, BF16)
                TR([(pT[:, (2 * p_ + hf) * 128:(2 * p_ + hf + 1) * 128], khl[:, s2, hf, p_, :], identb[:])
                    for p_ in range(npair) for hf in range(2)], [("khl", s2), C("identb")], [kpT])
                yield
                kT, kkT = tmp_at(7, 1024, BF16)
                CP(kT[:, 0:nch * 128], pT[:, 0:nch * 128], [kpT], [kkT], eng="act")
                yield
                pA, kpA = psb_at(4)
                MMseq([(pA[:, p_ * 128:(p_ + 1) * 128], [(kt[:, p_ * 128:(p_ + 1) * 128], qt[:, p_ * 128:(p_ + 1) * 128])], True)
                       for p_ in range(npair)], [kkt, kqt], [kpA])
                yield
                TTo(ATs[:, s3, 0:npair, :], pA[:, 0:npair * 128].rearrange("p (a b) -> p a b", b=128), maskbd[:, 0:npair, :], ALU.mult,
                    [kpA, C("maskbd")], [("ATs", s3)])
                yield
                pU = []
                for u0 in range(0, nch, 4):
                    pu, kpu = psb_at(5 + u0 // 4)
                    n = min(4, nch - u0)
                    MMseq([(pu[:, (c - u0) * 128:(c - u0 + 1) * 128], [(kT[:, c * 128:(c + 1) * 128], V[:, c // 2, h * 128:(h + 1) * 128])], True)
                           for c in range(u0, u0 + n)], [kkT] + [("hid", pg_) for pg_ in range(2 * (u0 // 2), 2 * ((u0 + n - 1) // 2) + 2)], [kpu])
                    pU.append((pu, kpu))
                    yield
                for c in range(nch):
                    b_ = sid[c]
                    pu, kpu = pU[c // 4]
                    STT(S32[b_][:, h, :], S32[b_][:, h, :], eb[:, c * 64 + 63:c * 64 + 64], pu[:, (c % 4) * 128:(c % 4 + 1) * 128],
                        ALU.mult, ALU.add, [("S", b_, h), keb, kpu], [("S", b_, h)])
                    yield
                    lastc = (c == nch - 1) or (t.kind == "s")
                    if lastc:
                        CP(Scar2[:, s3, b_, :], S32[b_][:, h, :], [("S", b_, h)], [("Sc2", s3, b_)], eng="act")
                    else:
                        CP(Sv[:, s3, c, :], S32[b_][:, h, :], [("S", b_, h)], [("Sv", s3, c)], eng="act")
                    yield

            def partB1(h):
                d = st[h]
                qt, kqt = d["qt"], d["kqt"]
                s3 = h % 3
                po, kpo = psb_at(7)
                items = []
                rd = [("ATs", s3), kqt]
                for c in range(nch):
                    p_ = c // 2
                    if c % 2 == 0:
                        items.append((po[:, p_ * 128:(p_ + 1) * 128], [(V[:, p_, h * 128:(h + 1) * 128], ATs[:, s3, p_, :])], True, False))
                        rd += [("hid", 2 * p_), ("hid", 2 * p_ + 1)]
                    b_ = sid[c]
                    if c == 0 or t.kind == "s":
                        lhs = Scar[b_][:, h, :]
                        rd.append(("Sc", b_, h))
                    else:
                        lhs = Sv[:, s3, c - 1, :]
                        rd.append(("Sv", s3, c - 1))
                    items.append((po[:, c * 64:(c + 1) * 64], [(lhs, qt[:, c * 64:(c + 1) * 64])], False, (c % 2 == 1)))
                MMseq(items, rd, [kpo])
                osq, kosq = tmp_at(8, T, BF16)
                A(osq, po[:, 0:T], AF.Square, [kpo], [kosq])
                d["po"], d["kpo"], d["osq"], d["kosq"] = po, kpo, osq, kosq

            def partB2(h):
                d = st.pop(h)
                po, kpo, osq, kosq, gate, kgate = d["po"], d["kpo"], d["osq"], d["kosq"], d["gate"], d["kgate"]
                s3 = h % 3
                pss, kpss = psb_at(6)
                MM(pss[:, 0:T], [(onesH[:], osq)], [kosq, C("onesH")], [kpss])
                rs, krs = tmp_at(9, T)
                A(rs, pss[:, 0:T], AF.Ln, [kpss, C("eps")], [krs], bias=c_eps[:])
                A(rs, rs, AF.Exp, [krs], [krs], scale=-0.5)
                t1, kt1 = tmp_at(10, T)
                STT(t1, po[:, 0:T], c_gn[:, j, h:h + 1], rs, ALU.mult, ALU.mult, [kpo, krs, C("gn")], [kt1])
                TTo(ofin[:, h, 0:T], t1, gate, ALU.mult, [kt1, kgate], [("hid", 8 + h)])
                for b_ in sorted(set(sid)):
                    CP(Scar[b_][:, h, :], Scar2[:, s3, b_, :], [("Sc2", s3, b_)], [("Sc", b_, h)], eng="act")

            def merge(gens):
                gens = list(gens)
                while gens:
                    for g_ in list(gens):
                        try:
                            next(g_)
                        except StopIteration:
                            gens.remove(g_)

            for k in range(NH + 3):
                if 0 <= k - 3 < NH:
                    partB1(k - 3)
                if k < NH:
                    partA1_pe(k)
                if 0 <= k - 3 < NH:
                    partB2(k - 3)
                gl_ = []
                if 0 <= k - 1 < NH:
                    gl_.append(partA2(k - 1))
                if k < NH:
                    gl_.append(partA1_ew(k))
                merge(gl_)
            if t.last:
                for s in range(ns):
                    b_ = sid[0] if t.kind == "p" else sid[s]
                    dstd = (o_hgp[j] if t.kind == "p" else o_hgs[j, s]).rearrange("h k v -> k h v")
                    DMA("sp", "outs", [(dstd, S32[b_][:])], [("S", b_, h) for h in range(NH)], [])
            for ec in range(NDC):
                ws = load_wout(j, ec)
                ps, kps = psb()
                MM(ps[:, 0:T], [(wout[:, ws, dc, :], ofin[:, dc, 0:T]) for dc in range(NDC)],
                   [("hid", 8 + dc) for dc in range(NDC)] + [("wout", ws)], [kps])
                TTo(x[:, ec, 0:T], x[:, ec, 0:T], ps[:, 0:T], ALU.add, [kps, ("x", ec)], [("x", ec)])


        xin_ctr = [0]

        def load_x(t):
            ntb = t.T // 128
            src = d_xp if t.kind == "p" else d_xs
            r0 = t.k * TT if t.kind == "p" else 0
            for tb in range(ntb):
                s = xin_ctr[0] % 2
                xin_ctr[0] += 1
                DMA("sp", ("io", s), [(xin[:, s, :], src[r0 + tb * 128: r0 + (tb + 1) * 128, :])], (), [("io", s)])
                for hb in range(2):
                    ps, kps = psb()
                    TR([(ps[:, q_ * 128:(q_ + 1) * 128], xin[:, s, (hb * 4 + q_) * 128:(hb * 4 + q_ + 1) * 128], identf[:]) for q_ in range(4)],
                       [("io", s), C("identf")], [kps])
                    CP(x[:, hb * 4:hb * 4 + 4, tb * 128:(tb + 1) * 128], ps[:, :].rearrange("p (a b) -> p a b", b=128), [kps],
                       [("x", hb * 4 + q_) for q_ in range(4)], eng="act")

        yo_ctr = [0]

        def store_y(t):
            T = t.T
            ntb = T // 128
            norm(t, c_gout[:, 0, :], lambda dc: (v3(x[:, dc, 0:T], t), ("x", dc)), "gout")
            dst = o_yp if t.kind == "p" else o_ys
            r0 = t.k * TT if t.kind == "p" else 0
            for tb in range(ntb):
                s = xin_ctr[0] % 2
                xin_ctr[0] += 1
                for hb in range(2):
                    ps, kps = psb()
                    TR([(ps[:, q_ * 128:(q_ + 1) * 128], x[:, hb * 4 + q_, tb * 128:(tb + 1) * 128], identf[:]) for q_ in range(4)],
                       [("x", hb * 4 + q_) for q_ in range(4)] + [C("identf")], [kps])
                    CP(yout[:, s, hb * 512:(hb + 1) * 512], ps[:, :], [kps], [("io", s)], eng="act")
                DMA("sp", ("io", s), [(dst[r0 + tb * 128: r0 + (tb + 1) * 128, :], yout[:, s, :])], [("io", s)], [])

        for ti, t in enumerate(tiles):
            first_pass[0] = (ti == 0)
            load_x(t)
            for i in LY:
                if i % 2 == 0:
                    pool_layer(t, i)
                else:
                    hgrn_layer(t, i)
                ffn_layer(t, i)
            store_y(t)

        S.finalize()
        esems = {e: es.enter_context(nc.semaphore("se_" + e)) for e in Sched.ENGS}
        dsems = {}
        for n, k in enumerate(S.dma_cnt):
            dsems[k] = es.enter_context(nc.semaphore(f"sd_{n}"))
        out_keys = ["outs"] + [("io", s) for s in range(2)]
        with nc.Block() as block:
            @block.tensor
            def _(e):
                S.emit("pe", e, esems, dsems)

            @block.scalar
            def _(e):
                S.emit("act", e, esems, dsems)

            @block.vector
            def _(e):
                S.emit("dve", e, esems, dsems)

            @block.gpsimd
            def _(e):
                S.emit("pool", e, esems, dsems)

            @block.sync
            def _(e):
                S.emit("sp", e, esems, dsems)
                for k in out_keys:
                    if k in S.dma_cnt:
                        e.wait_ge(dsems[k], 16 * S.dma_cnt[k])
    nc._sched = S
    return nc


_NC_CACHE = {}


def run_cores(cfg, in_maps):
    key = (cfg.layers, cfg.npt, cfg.ns)
    if key not in _NC_CACHE:
        _NC_CACHE[key] = build(cfg)
    nc = _NC_CACHE[key]
    return run_bass_kernel_spmd(nc, in_maps, core_ids=list(range(len(in_maps))))


def kernel(x_prompt, x_sample, state_pool, state_hgrn, state_ffn_conv, norm_mix_g, pool_w, pool_scale,
           hgrn_w_in, hgrn_lb_logits, hgrn_norm_g, hgrn_w_out, norm_ffn_g, ffn_w_up, ffn_conv_w,
           ffn_conv_b, ffn_w_down, norm_out_g):
    f = lambda a: np.ascontiguousarray(np.asarray(a, dtype=np.float32))
    x_prompt, x_sample = f(x_prompt), f(x_sample)
    state_pool, state_hgrn, state_ffn_conv = f(state_pool), f(state_hgrn), f(state_ffn_conv)
    BP, SEQ, _ = x_prompt.shape
    NSB = x_sample.shape[0]
    ncores = 8
    nsc = NSB // ncores
    cfg = Cfg(layers=(0, 1, 2, 3), npt=SEQ // TT, ns=nsc)
    shared = {
        "norm_mix_g": f(norm_mix_g), "pool_w": f(pool_w), "pool_scale": f(pool_scale), "hgrn_w_in": f(hgrn_w_in),
        "hgrn_lb_logits": f(hgrn_lb_logits), "hgrn_norm_g": f(hgrn_norm_g), "hgrn_w_out": f(hgrn_w_out),
        "norm_ffn_g": f(norm_ffn_g), "ffn_w_up": f(ffn_w_up), "ffn_conv_w": f(ffn_conv_w), "ffn_conv_b": f(ffn_conv_b),
        "ffn_w_down": f(ffn_w_down), "norm_out_g": f(norm_out_g).reshape(1, D),
    }
    in_maps = []
    for c in range(ncores):
        sl = slice(c * nsc, (c + 1) * nsc)
        m = dict(shared)
        m["xp"] = x_prompt[c % BP]
        m["xs"] = x_sample[sl].reshape(nsc * 64, D)
        m["s_pool"] = np.ascontiguousarray(state_pool[:, sl])
        m["s_hgrn"] = np.ascontiguousarray(state_hgrn[:, sl])
        m["s_conv"] = np.ascontiguousarray(state_ffn_conv[:, sl])
        in_maps.append(m)
    res = run_cores(cfg, in_maps).results
    y_prompt = np.stack([res[b]["yp"] for b in range(BP)])
    y_sample = np.concatenate([res[c]["ys"].reshape(nsc, 64, D) for c in range(ncores)])
    pool_p = np.stack([res[b]["npool_p"] for b in range(BP)], axis=1)
    pool_s = np.concatenate([res[c]["npool_s"] for c in range(ncores)], axis=1)
    hg_p = np.stack([res[b]["nhgrn_p"] for b in range(BP)], axis=1)
    hg_s = np.concatenate([res[c]["nhgrn_s"] for c in range(ncores)], axis=1)
    cv_p = np.stack([res[b]["nconv_p"] for b in range(BP)], axis=1)
    cv_s = np.concatenate([res[c]["nconv_s"] for c in range(ncores)], axis=1)
    return tuple(np.ascontiguousarray(a, dtype=np.float32) for a in (y_prompt, y_sample, pool_p, pool_s, hg_p, hg_s, cv_p, cv_s))
```
